# Optimizing a Trainium2 kernel written in Bass

```python
import math
import jax, jax.numpy as jnp
from jax import lax
import numpy as np

D_MODEL = 1024
BATCH = 16
SEQ = 2048
DEPTH = 4

N_EVEN = (DEPTH + 1) // 2
N_ODD = DEPTH // 2
ROPE_THETA = 10000.0
BLOCK = 128
NORM_EPS = 1e-5

RET_HEADS = 8
RET_DK = D_MODEL // RET_HEADS
RET_DV = D_MODEL // RET_HEADS
RET_QK_WIDTH = RET_HEADS * RET_DK
RET_WIDTH = RET_HEADS * RET_DV

LRU_WIDTH = D_MODEL
LRU_BLOCKS = 8
LRU_BLOCK_DIM = LRU_WIDTH // LRU_BLOCKS
CONV_WIDTH = 4
LRU_C = 8.0

DIFF_HEADS = 8
DIFF_DH = D_MODEL // DIFF_HEADS
DIFF_DV = 2 * DIFF_DH
DIFF_QK_WIDTH = 2 * DIFF_HEADS * DIFF_DH
DIFF_WIDTH = DIFF_HEADS * DIFF_DV

EVEN_IN = 3 * RET_QK_WIDTH + RET_WIDTH + 2 * LRU_WIDTH
EVEN_MIX = RET_WIDTH + LRU_WIDTH
ODD_IN = 2 * DIFF_QK_WIDTH + 2 * DIFF_WIDTH
ODD_MIX = DIFF_WIDTH

DEEPNORM_ALPHA = (2.0 * DEPTH) ** 0.25
DEEPNORM_BETA = (8.0 * DEPTH) ** -0.25

kernel_name = "hybrid_retention_rglru_diffattn_deepnorm"


def rope(x, pos):
    half = x.shape[-1] // 2
    inv_freq = ROPE_THETA ** (-jnp.arange(half, dtype=jnp.float32) / half)
    ang = pos.astype(jnp.float32)[:, None] * inv_freq[None, :]
    cos = jnp.cos(ang)[:, None, :]
    sin = jnp.sin(ang)[:, None, :]
    x1 = x[..., :half].astype(jnp.float32)
    x2 = x[..., half:].astype(jnp.float32)
    return jnp.concatenate([x1 * cos - x2 * sin, x2 * cos + x1 * sin], axis=-1)


def layer_norm(x, g, b):
    x = x.astype(jnp.float32)
    mu = jnp.mean(x, axis=-1, keepdims=True)
    var = jnp.mean(jnp.square(x - mu), axis=-1, keepdims=True)
    return (x - mu) * lax.rsqrt(var + NORM_EPS) * g + b


def retention(q, k, v):
    B, S, H, dk = q.shape
    dv = v.shape[-1]
    N = S // BLOCK
    log_g = jnp.log(1.0 - 2.0 ** (-5.0 - jnp.arange(H, dtype=jnp.float32)))
    idx = jnp.arange(BLOCK, dtype=jnp.float32)
    rel = idx[:, None] - idx[None, :]
    decay = jnp.where(rel[None] >= 0, jnp.exp(log_g[:, None, None] * jnp.maximum(rel, 0.0)[None]), 0.0)
    qc = q.reshape(B, N, BLOCK, H, dk)
    kc = k.reshape(B, N, BLOCK, H, dk)
    vc = v.astype(jnp.float32).reshape(B, N, BLOCK, H, dv)
    scores = jnp.einsum('bnihd,bnjhd->bnhij', qc, kc) * decay
    intra = jnp.einsum('bnhij,bnjhe->bnihe', scores, vc)
    k_end = kc * jnp.exp(log_g[None, :] * (BLOCK - 1 - idx)[:, None])[:, :, None]
    kv = jnp.einsum('bnjhd,bnjhe->nbhde', k_end, vc)
    chunk_decay = jnp.exp(log_g * BLOCK)[None, :, None, None]

    def step(R, kv_n):
        return R * chunk_decay + kv_n, R

    _, R_prev = lax.scan(step, jnp.zeros((B, H, dk, dv), jnp.float32), kv)
    q_dec = qc * jnp.exp(log_g[None, :] * (idx + 1.0)[:, None])[:, :, None]
    cross = jnp.einsum('bnihd,nbhde->bnihe', q_dec, R_prev)
    return (intra + cross).reshape(B, S, H, dv)


def rg_lru_branch(xb, conv_w, conv_b, wa, ba, wx, bx, lam):
    B, S, W = xb.shape
    xb = xb.astype(jnp.float32)
    xc = lax.conv_general_dilated(
        xb, conv_w.astype(jnp.float32)[:, None, :], window_strides=(1,),
        padding=[(CONV_WIDTH - 1, 0)], dimension_numbers=('NWC', 'WIO', 'NWC'),
        feature_group_count=W) + conv_b
    xg = xc.reshape(B, S, LRU_BLOCKS, LRU_BLOCK_DIM)
    r = jax.nn.sigmoid(jnp.einsum('bsgi,gij->bsgj', xg, wa).reshape(B, S, W) + ba)
    i = jax.nn.sigmoid(jnp.einsum('bsgi,gij->bsgj', xg, wx).reshape(B, S, W) + bx)
    log_a = -LRU_C * r * jax.nn.softplus(-lam.astype(jnp.float32))
    a = jnp.exp(log_a)
    u = xc * i * jnp.sqrt(-jnp.expm1(2.0 * log_a))

    def combine(e1, e2):
        a1, b1 = e1
        a2, b2 = e2
        return a1 * a2, a2 * b1 + b2

    _, h = lax.associative_scan(combine, (a, u), axis=1)
    return h


def diff_attention(q, k, v, lam):
    B, S, H2, d = q.shape
    H = H2 // 2
    nb = S // BLOCK
    v = v.astype(jnp.float32)
    qb = q.reshape(B, nb, BLOCK, H2, d).transpose(1, 0, 2, 3, 4)
    kpos = jnp.arange(S)

    def one_block(args):
        qblk, start = args
        s = jnp.einsum('bqhd,bkhd->bhqk', qblk, k)
        qpos = start + jnp.arange(BLOCK)
        s = jnp.where((kpos[None, :] <= qpos[:, None])[None, None], s, -jnp.inf)
        p = jax.nn.softmax(s, axis=-1).reshape(B, H, 2, BLOCK, S)
        w = p[:, :, 0] - lam * p[:, :, 1]
        return jnp.einsum('bhqk,bkhe->bqhe', w, v)

    out = lax.map(one_block, (qb, jnp.arange(nb) * BLOCK))
    return out.transpose(1, 0, 2, 3, 4).reshape(B, S, H, 2 * d)


def even_layer(x, w_in, conv_w, conv_b, wa, ba, wx, bx, lam, gn_g, gn_b, w_out, pos):
    B, S, _ = x.shape
    proj = jnp.einsum('bsd,de->bse', x, w_in)
    q, k, v, g_ret, xb, g_lru = jnp.split(
        proj, [RET_QK_WIDTH, 2 * RET_QK_WIDTH, 2 * RET_QK_WIDTH + RET_WIDTH,
               2 * RET_QK_WIDTH + 2 * RET_WIDTH, 2 * RET_QK_WIDTH + 2 * RET_WIDTH + LRU_WIDTH], axis=-1)
    q = rope(q.reshape(B, S, RET_HEADS, RET_DK), pos)
    k = rope(k.reshape(B, S, RET_HEADS, RET_DK), pos) * (RET_DK ** -0.5)
    ret = retention(q, k, v.reshape(B, S, RET_HEADS, RET_DV))
    mu = jnp.mean(ret, axis=-1, keepdims=True)
    var = jnp.mean(jnp.square(ret - mu), axis=-1, keepdims=True)
    ret = ((ret - mu) * lax.rsqrt(var + NORM_EPS)).reshape(B, S, RET_WIDTH) * gn_g + gn_b
    ret = ret * jax.nn.silu(g_ret.astype(jnp.float32))
    lru = rg_lru_branch(xb, conv_w, conv_b, wa, ba, wx, bx, lam) * jax.nn.silu(g_lru.astype(jnp.float32))
    return jnp.einsum('bse,ed->bsd', jnp.concatenate([ret, lru], axis=-1), w_out)


def odd_layer(x, w_in, lq1, lk1, lq2, lk2, subln_g, w_out, pos, lambda_init):
    B, S, _ = x.shape
    proj = jnp.einsum('bsd,de->bse', x, w_in)
    q, k, v, g = jnp.split(proj, [DIFF_QK_WIDTH, 2 * DIFF_QK_WIDTH, 2 * DIFF_QK_WIDTH + DIFF_WIDTH], axis=-1)
    q = rope(q.reshape(B, S, 2 * DIFF_HEADS, DIFF_DH), pos) * (DIFF_DH ** -0.5)
    k = rope(k.reshape(B, S, 2 * DIFF_HEADS, DIFF_DH), pos)
    lam = (jnp.exp(jnp.sum(lq1.astype(jnp.float32) * lk1)) - jnp.exp(jnp.sum(lq2.astype(jnp.float32) * lk2))
           + lambda_init)
    out = diff_attention(q, k, v.reshape(B, S, DIFF_HEADS, DIFF_DV), lam)
    out = out * lax.rsqrt(jnp.mean(jnp.square(out), axis=-1, keepdims=True) + NORM_EPS) * subln_g
    out = (out * (1.0 - lambda_init)).reshape(B, S, DIFF_WIDTH) * jax.nn.silu(g.astype(jnp.float32))
    return jnp.einsum('bse,ed->bsd', out, w_out)


def setup_inputs(seed: int = 0) -> dict:
    key = jax.random.key(seed)
    ks = jax.random.split(key, 32)
    f32 = jnp.float32
    nrm = lambda k, shape, s: jax.random.normal(k, shape, f32) * s
    ev_scale = jnp.concatenate([jnp.ones((2 * RET_QK_WIDTH,), f32), jnp.full((RET_WIDTH,), DEEPNORM_BETA, f32),
                                jnp.ones((RET_WIDTH,), f32), jnp.full((LRU_WIDTH,), DEEPNORM_BETA, f32),
                                jnp.ones((LRU_WIDTH,), f32)])
    od_scale = jnp.concatenate([jnp.ones((2 * DIFF_QK_WIDTH,), f32), jnp.full((DIFF_WIDTH,), DEEPNORM_BETA, f32),
                                jnp.ones((DIFF_WIDTH,), f32)])
    u = jax.random.uniform(ks[8], (N_EVEN, LRU_WIDTH), f32, 0.9, 0.999)
    a0 = u ** (1.0 / LRU_C)
    return {
        "x": nrm(ks[0], (BATCH, SEQ, D_MODEL), 1.0),
        "ev_w_in": nrm(ks[1], (N_EVEN, D_MODEL, EVEN_IN), D_MODEL ** -0.5) * ev_scale,
        "ev_conv_w": nrm(ks[2], (N_EVEN, CONV_WIDTH, LRU_WIDTH), CONV_WIDTH ** -0.5),
        "ev_conv_b": nrm(ks[3], (N_EVEN, LRU_WIDTH), 0.01),
        "ev_gate_a_w": nrm(ks[4], (N_EVEN, LRU_BLOCKS, LRU_BLOCK_DIM, LRU_BLOCK_DIM), LRU_BLOCK_DIM ** -0.5),
        "ev_gate_a_b": nrm(ks[5], (N_EVEN, LRU_WIDTH), 0.01),
        "ev_gate_x_w": nrm(ks[6], (N_EVEN, LRU_BLOCKS, LRU_BLOCK_DIM, LRU_BLOCK_DIM), LRU_BLOCK_DIM ** -0.5),
        "ev_gate_x_b": nrm(ks[7], (N_EVEN, LRU_WIDTH), 0.01),
        "ev_lru_lambda": jnp.log(a0) - jnp.log1p(-a0),
        "ev_ret_gn_g": 1.0 + nrm(ks[9], (N_EVEN, RET_WIDTH), 0.02),
        "ev_ret_gn_b": nrm(ks[10], (N_EVEN, RET_WIDTH), 0.02),
        "ev_w_out": nrm(ks[11], (N_EVEN, EVEN_MIX, D_MODEL), EVEN_MIX ** -0.5 * DEEPNORM_BETA),
        "ev_ln_g": 1.0 + nrm(ks[12], (N_EVEN, D_MODEL), 0.02),
        "ev_ln_b": nrm(ks[13], (N_EVEN, D_MODEL), 0.02),
        "od_w_in": nrm(ks[14], (N_ODD, D_MODEL, ODD_IN), D_MODEL ** -0.5) * od_scale,
        "od_lambda_q1": nrm(ks[15], (N_ODD, DIFF_DH), 0.1),
        "od_lambda_k1": nrm(ks[16], (N_ODD, DIFF_DH), 0.1),
        "od_lambda_q2": nrm(ks[17], (N_ODD, DIFF_DH), 0.1),
        "od_lambda_k2": nrm(ks[18], (N_ODD, DIFF_DH), 0.1),
        "od_subln_g": 1.0 + nrm(ks[19], (N_ODD, DIFF_DV), 0.02),
        "od_w_out": nrm(ks[20], (N_ODD, ODD_MIX, D_MODEL), ODD_MIX ** -0.5 * DEEPNORM_BETA),
        "od_ln_g": 1.0 + nrm(ks[21], (N_ODD, D_MODEL), 0.02),
        "od_ln_b": nrm(ks[22], (N_ODD, D_MODEL), 0.02),
    }


def reference(x, ev_w_in, ev_conv_w, ev_conv_b, ev_gate_a_w, ev_gate_a_b, ev_gate_x_w, ev_gate_x_b,
              ev_lru_lambda, ev_ret_gn_g, ev_ret_gn_b, ev_w_out, ev_ln_g, ev_ln_b,
              od_w_in, od_lambda_q1, od_lambda_k1, od_lambda_q2, od_lambda_k2, od_subln_g, od_w_out,
              od_ln_g, od_ln_b):
    in_dtype = x.dtype
    pos = jnp.arange(x.shape[1], dtype=jnp.int32)
    h = x
    for l in range(DEPTH):
        j = l // 2
        if l % 2 == 0:
            y = even_layer(h, ev_w_in[j], ev_conv_w[j], ev_conv_b[j], ev_gate_a_w[j], ev_gate_a_b[j],
                           ev_gate_x_w[j], ev_gate_x_b[j], ev_lru_lambda[j], ev_ret_gn_g[j], ev_ret_gn_b[j],
                           ev_w_out[j], pos)
            h = layer_norm(DEEPNORM_ALPHA * h.astype(jnp.float32) + y, ev_ln_g[j], ev_ln_b[j])
        else:
            lambda_init = 0.8 - 0.6 * math.exp(-0.3 * l)
            y = odd_layer(h, od_w_in[j], od_lambda_q1[j], od_lambda_k1[j], od_lambda_q2[j], od_lambda_k2[j],
                          od_subln_g[j], od_w_out[j], pos, lambda_init)
            h = layer_norm(DEEPNORM_ALPHA * h.astype(jnp.float32) + y, od_ln_g[j], od_ln_b[j])
    return h.astype(in_dtype)
```

```python
import math
import numpy as np
from contextlib import ExitStack
import concourse.bass as bass
import concourse.mybir as mybir
from concourse.bass_utils import run_bass_kernel_spmd

F32 = mybir.dt.float32
BF16 = mybir.dt.bfloat16
ALU = mybir.AluOpType
AF = mybir.ActivationFunctionType
AX = mybir.AxisListType

NCORES = 8
S = 2048
D = 1024
NT = 16
DEPTH = 4
EPS = 1e-5
ALPHA = (2.0 * DEPTH) ** 0.25
SCALE = 128.0 ** -0.5

ENG = ("pe", "act", "dve", "pool", "sp")
EPOCH = 30000


class Res:
    __slots__ = ("name", "w", "r", "sem", "excl")

    def __init__(self, name, excl=False):
        self.excl = excl
        self.name = name
        self.w = None
        self.r = {}
        self.sem = None


class Prog:
    def __init__(self, nc):
        self.nc = nc
        self.ops = {e: [] for e in ENG}
        self.waited = {e: {} for e in ENG}
        self.dma_cnt = []
        self.stack = ExitStack()
        self.n_sb = 0

    def sb(self, shape, dt, name=None):
        self.n_sb += 1
        return self.stack.enter_context(self.nc.sbuf_tensor("S_" + (name or f"sb{self.n_sb}"), list(shape), dt))

    def ps(self, shape, dt, name=None):
        self.n_sb += 1
        return self.stack.enter_context(self.nc.psum_tensor("P_" + (name or f"ps{self.n_sb}"), list(shape), dt))

    def _need(self, eng, dep, waits):
        if dep is None:
            return
        key = (dep[0], dep[1])
        val = dep[2]
        if self.waited[eng].get(key, -1) >= val:
            return
        if waits.get(key, -1) >= val:
            return
        waits[key] = val

    def op(self, eng, fn, reads=(), writes=()):
        ex = [r for r in reads if r.excl]
        if ex:
            reads = [r for r in reads if not r.excl]
            writes = list(writes) + [r for r in ex if r not in writes]
        idx = len(self.ops[eng])
        waits = {}
        for r in reads:
            d = r.w
            if d is None:
                continue
            if d[0] == "op" and d[1] == eng and eng in ("pe", "sp"):
                continue
            self._need(eng, d, waits)
        for w in writes:
            for d in ([w.w] if w.w is not None else []) + list(w.r.values()):
                if d[0] == "op" and d[1] == eng:
                    continue
                self._need(eng, d, waits)
        for k, v in waits.items():
            self.waited[eng][k] = v
        me = ("op", eng, idx)
        for r in reads:
            r.r[("op", eng)] = me
        for w in writes:
            w.w = me
            w.r = {}
        self.ops[eng].append(dict(fn=fn, waits=waits, dma=None, inc=False))
        return me

    def dma(self, eng, out, in_, reads=(), writes=(), sem_res=None):
        if sem_res is None:
            sem_res = writes[0] if writes else reads[0]
        if sem_res.sem is None:
            sem_res.sem = len(self.dma_cnt)
            self.dma_cnt.append(0)
        sid = sem_res.sem
        waits = {}
        for r in reads:
            self._need(eng, r.w, waits)
        for w in writes:
            for d in ([w.w] if w.w is not None else []) + list(w.r.values()):
                self._need(eng, d, waits)
        for k, v in waits.items():
            self.waited[eng][k] = v
        self.dma_cnt[sid] += 16
        me = ("dma", sid, self.dma_cnt[sid])
        for r in reads:
            r.r[("dma", sid)] = me
        for w in writes:
            w.w = me
            w.r = {}
        self.ops[eng].append(dict(fn=lambda e, o=out, i=in_: e.dma_start(out=o, in_=i), waits=waits, dma=sid, inc=False))
        return me

    def wait_all_dma(self, eng, resources):
        waits = {}
        for r in resources:
            for d in ([r.w] if r.w is not None else []) + list(r.r.values()):
                if d[0] == "dma":
                    self._need(eng, d, waits)
        for k, v in waits.items():
            self.waited[eng][k] = v
        self.ops[eng].append(dict(fn=None, waits=waits, dma=None, inc=False))

    def emit(self):
        nc = self.nc
        for e in ENG:
            for o in self.ops[e]:
                for (kind, src), val in o["waits"].items():
                    if kind == "op":
                        self.ops[src][val]["inc"] = True
        semval = {e: {} for e in ENG}
        nsem = {}
        for e in ENG:
            c = 0
            for i, o in enumerate(self.ops[e]):
                if o["inc"]:
                    semval[e][i] = (c // EPOCH, c % EPOCH + 1)
                    c += 1
            nsem[e] = (c + EPOCH - 1) // EPOCH
        st = self.stack
        esems = {e: [st.enter_context(nc.semaphore(f"s_{e}{k}")) for k in range(nsem[e])] for e in ENG}
        dsems = [st.enter_context(nc.semaphore(f"d{k}")) for k in range(len(self.dma_cnt))]
        self.stats = {e: (len(self.ops[e]), sum(len(o["waits"]) for o in self.ops[e])) for e in ENG}
        block = st.enter_context(nc.Block())

        def runner(e):
            def run(eng):
                for i, o in enumerate(self.ops[e]):
                    for (kind, src), val in o["waits"].items():
                        if kind == "op":
                            ep, v = semval[src][val]
                            eng.wait_ge(esems[src][ep], v)
                        else:
                            eng.wait_ge(dsems[src], val)
                    if o["fn"] is None:
                        continue
                    ins = o["fn"](eng)
                    if o["dma"] is not None:
                        ins.then_inc(dsems[o["dma"]], 16)
                    elif o["inc"]:
                        ep, v = semval[e][i]
                        ins.then_inc(esems[e][ep], 1)
            return run

        block.tensor(runner("pe"))
        block.scalar(runner("act"))
        block.vector(runner("dve"))
        block.gpsimd(runner("pool"))
        block.sync(runner("sp"))
        st.close()


def lambda_init(l):
    return 0.8 - 0.6 * math.exp(-0.3 * l)


def build_program(layers, nseq, first_is_input=True):
    nc = bass.Bass("TRN2", target_bir_lowering=False)
    p = Prog(nc)

    def din(name, shape):
        return nc.dram_tensor(name, list(shape), F32, kind="ExternalInput").ap()

    x_d = din("x", [nseq, S, D])
    out_d = nc.dram_tensor("out", [nseq, S, D], F32, kind="ExternalOutput").ap()
    cos_d = din("cosT", [128, S])
    sin_d = din("sinS", [128, S])
    identf_d = din("identf", [128, 128])
    cmask_d = din("cmask", [128, 128])
    dmask_d = din("dmask", [128, 8, 128])
    gpow_d = din("gpow", [128, 8])
    kend_d = din("kend", [128, 8])
    wd = {}
    for l in layers:
        if l % 2 == 0:
            wd[l] = dict(w_in=din(f"w_in{l}", [8, D, 768]), w_out=din(f"w_out{l}", [8, 256, D]),
                         gw=din(f"gw{l}", [8, 128, 256]), lrup=din(f"lrup{l}", [128, 8, 8]),
                         gn=din(f"gn{l}", [128, 2, D]), ln=din(f"ln{l}", [128, 2, D]))
        else:
            wd[l] = dict(w_in=din(f"w_in{l}", [8, D, 1024]), w_out=din(f"w_out{l}", [8, 256, D]),
                         lam=din(f"lam{l}", [128, 4, 128]), sub=din(f"sub{l}", [128, 256]),
                         ln=din(f"ln{l}", [128, 2, D]))

    h32 = p.sb([128, NT, D], F32, "h32")
    rh = [Res(f"h32_{t}") for t in range(NT)]
    hT = p.sb([128, 8, S], BF16, "hT")
    rhT = [Res(f"hT_{t}") for t in range(NT)]
    slabA = p.sb([128, 8, 512], BF16, "slabA"); rslabA = Res("slabA")
    slabB = p.sb([128, 8, 512], BF16, "slabB"); rslabB = Res("slabB")
    wout = [p.sb([128, 2, D], BF16, f"wout{i}") for i in range(1)]
    rwout = [Res(f"wout{i}") for i in range(1)]
    qk = p.sb([128, 4, S], BF16, "qk")
    rqk = [Res(f"qk{i}") for i in range(4)]
    cosT = p.sb([128, S], F32, "cosT"); sinS = p.sb([128, S], F32, "sinS")
    rconst = Res("const")
    identf = p.sb([128, 128], F32, "identf")
    identb = p.sb([128, 128], BF16, "identb")
    cmask = p.sb([128, 128], BF16, "cmask")
    gpow = p.sb([128, 8], F32, "gpow")
    kend = p.sb([128, 8], F32, "kend")
    epst = p.sb([128, 1], F32, "eps")
    onet = p.sb([128, 1], F32, "one")
    rxs = [p.sb([128, 512], F32, f"rxs{i}") for i in range(2)]; r_rxs = [Res(f"rxs{i}") for i in range(2)]
    rtt = [p.sb([128, 512], F32, f"rtt{i}") for i in range(2)]; r_rtt = [Res(f"rtt{i}") for i in range(2)]
    ARENA = 45 * 1024
    arena = p.sb([128, ARENA // 4], F32, "arena")
    st12 = p.sb([128, NT, 12], F32, "st12"); r_st12 = Res("st12")
    mvall = p.sb([128, NT, 2], F32, "mvall"); r_mv = Res("mvall")
    rstd = p.sb([128, NT], F32, "rstd"); r_rstd = Res("rstd")
    nmr = p.sb([128, NT], F32, "nmr"); r_nmr = Res("nmr")
    prm = {}
    for l in layers:
        if l % 2 == 0:
            prm[l] = dict(lrup=p.sb([128, 8, 8], F32, f"lrup{l}"), c=p.sb([128, 8], F32, f"c{l}"),
                          c2=p.sb([128, 8], F32, f"c2{l}"), cn=p.sb([128, 8], F32, f"cn{l}"), r=Res(f"prm{l}"))
        else:
            prm[l] = dict(lamt=p.sb([128, 4, 128], F32, f"lamt{l}"), nlam=p.sb([128, 1], F32, f"nlam{l}"),
                          sub2=p.sb([128, 256], F32, f"sub2{l}"), tmp=p.sb([128, 2, 128], F32, f"ltmp{l}"),
                          s12=p.sb([128, 2], F32, f"s12{l}"), r=Res(f"prm{l}"))

    banks = [p.ps([128, 512], F32, f"bank{i}") for i in range(8)]
    rbank = [Res(f"bank{i}", excl=True) for i in range(8)]
    grp = {"proj": [0, 1], "st": [2, 3], "o": [4, 5, 6, 7]}
    gcnt = {k: 0 for k in grp}

    def nxt(g):
        i = grp[g][gcnt[g] % len(grp[g])]
        gcnt[g] += 1
        return banks[i], rbank[i]

    def mm(out, lhsT, rhs, start, stop, reads, writes):
        p.op("pe", lambda e: e.matmul(out, lhsT=lhsT, rhs=rhs, start=start, stop=stop), reads, writes)

    def tr(out, in_, ident, reads, writes):
        p.op("pe", lambda e: e.transpose(out=out, in_=in_, identity=ident), reads, writes)

    def act(out, in_, func, reads, writes, bias=None, scale=1.0, accum=None):
        kw = {}
        if bias is not None:
            kw["bias"] = bias
        if accum is not None:
            kw["accum_out"] = accum
        p.op("act", lambda e: e.activation(out=out, in_=in_, func=func, scale=scale, **kw), reads, writes)

    def tt(eng, out, in0, in1, op, reads, writes):
        p.op(eng, lambda e: e.tensor_tensor(out=out, in0=in0, in1=in1, op=op), reads, writes)

    def ts(eng, out, in0, s1, s2, op0, op1, reads, writes):
        if s2 is None:
            p.op(eng, lambda e: e.tensor_single_scalar(out=out, in_=in0, scalar=s1, op=op0), reads, writes)
        else:
            p.op(eng, lambda e: e.tensor_scalar(out=out, in0=in0, scalar1=s1, scalar2=s2, op0=op0, op1=op1), reads, writes)

    def stt(eng, out, in0, scalar, in1, op0, op1, reads, writes):
        p.op(eng, lambda e: e.scalar_tensor_tensor(out=out, in0=in0, scalar=scalar, in1=in1, op0=op0, op1=op1), reads, writes)

    def cp(eng, out, in_, reads, writes):
        if eng == "act":
            p.op("act", lambda e: e.copy(out=out, in_=in_), reads, writes)
        else:
            p.op(eng, lambda e: e.tensor_copy(out=out, in_=in_), reads, writes)

    p.dma("sp", cosT[:], cos_d, writes=[rconst])
    p.dma("sp", sinS[:], sin_d, writes=[rconst])
    p.dma("sp", identf[:], identf_d, writes=[rconst])
    p.dma("pool", identb[:], identf_d, writes=[rconst])
    p.dma("pool", cmask[:], cmask_d, writes=[rconst])
    p.dma("sp", gpow[:], gpow_d, writes=[rconst])
    p.dma("sp", kend[:], kend_d, writes=[rconst])
    r_eps = Res("eps")
    p.op("dve", lambda e: e.memset(epst[:], EPS), writes=[r_eps])
    p.op("dve", lambda e: e.memset(onet[:], 1.0), writes=[r_eps])
    for l in layers:
        pr = prm[l]
        if STOP <= -2:
            continue
        if l % 2 == 0:
            p.dma("sp", pr["lrup"][:], wd[l]["lrup"], writes=[pr["r"]])
            act(pr["c"][:], pr["lrup"][:, :, 7], AF.Exp, [pr["r"]], [pr["r"]], scale=-1.0)
            act(pr["c"][:], pr["c"][:], AF.Ln, [pr["r"], r_eps], [pr["r"]], bias=onet[:])
            ts("dve", pr["cn"][:], pr["c"][:], 8.0, None, ALU.mult, None, [pr["r"]], [pr["r"]])
            ts("dve", pr["c"][:], pr["cn"][:], -1.0, None, ALU.mult, None, [pr["r"]], [pr["r"]])
            ts("dve", pr["c2"][:], pr["c"][:], 2.0, None, ALU.mult, None, [pr["r"]], [pr["r"]])
        else:
            li = lambda_init(l)
            p.dma("sp", pr["lamt"][:], wd[l]["lam"], writes=[pr["r"]])
            p.dma("sp", pr["sub2"][:], wd[l]["sub"], writes=[pr["r"]])
            tt("dve", pr["tmp"][:, 0, :], pr["lamt"][:, 0, :], pr["lamt"][:, 1, :], ALU.mult, [pr["r"]], [pr["r"]])
            tt("dve", pr["tmp"][:, 1, :], pr["lamt"][:, 2, :], pr["lamt"][:, 3, :], ALU.mult, [pr["r"]], [pr["r"]])
            p.op("dve", lambda e, pr=pr: e.reduce_sum(out=pr["s12"][:], in_=pr["tmp"][:], axis=AX.X), [pr["r"]], [pr["r"]])
            act(pr["s12"][:], pr["s12"][:], AF.Exp, [pr["r"]], [pr["r"]])
            tt("dve", pr["nlam"][:], pr["s12"][:, 1:2], pr["s12"][:, 0:1], ALU.subtract, [pr["r"]], [pr["r"]])
            ts("dve", pr["nlam"][:], pr["nlam"][:], -li, None, ALU.add, None, [pr["r"]], [pr["r"]])
            ts("dve", pr["sub2"][:], pr["sub2"][:], 1.0 - li, None, ALU.mult, None, [pr["r"]], [pr["r"]])

    sched = []
    for s_ in range(nseq):
        for l_ in layers:
            for u_ in range(8):
                sched.append((l_, u_))
    pos = {"A": 0, "B": 0}

    def load_next(which):
        i = pos[which]
        pos[which] += 1
        if i >= len(sched):
            return
        l, u = sched[i]
        src = wd[l]["w_in"][u].rearrange("(kc p) c -> p kc c", p=128)
        if which == "A":
            dst, rdst, c0, w = slabA, rslabA, 0, 512
        else:
            dst, rdst, c0, w = slabB, rslabB, 512, (512 if l % 2 == 1 else 256)
        for kc0 in range(0, 8, 4):
            p.dma("pool", dst[:, kc0:kc0 + 4, 0:w], src[:, kc0:kc0 + 4, c0:c0 + w], writes=[rdst])

    def load_wout(l, u):
        src = wd[l]["w_out"][u].rearrange("(c p) n -> p c n", p=128)
        p.dma("pool", wout[0][:], src, writes=[rwout[0]])
        return wout[0], rwout[0]

    rope_ctr = [0]

    def rope(bank, rb, dst, rdst, G):
        i = rope_ctr[0] % 2
        rope_ctr[0] += 1
        xs, r1 = rxs[i], r_rxs[i]
        t1, r2 = rtt[i], r_rtt[i]
        cs = cosT[:, G * 512:(G + 1) * 512]
        sn = sinS[:, G * 512:(G + 1) * 512]
        p.op("act", lambda e: e.copy(out=xs[0:64, :], in_=bank[64:128, :]), [rb], [r1])
        p.op("act", lambda e: e.copy(out=xs[64:128, :], in_=bank[0:64, :]), [rb], [r1])
        tt("dve", t1[:], bank[:], cs, ALU.mult, [rb, rconst], [r2])
        tt("pool", xs[:], xs[:], sn, ALU.mult, [r1, rconst], [r1])
        tt("dve", dst, t1[:], xs[:], ALU.add, [r1, r2], [rdst])

    def proj_feat(sl, rsl, c0, G):
        bank, rb = nxt("proj")
        for kc in range(8):
            mm(bank[:], sl[:, kc, c0:c0 + 128], hT[:, kc, G * 512:(G + 1) * 512], kc == 0, kc == 7,
               [rsl] + rhT[4 * G:4 * G + 4], [rb])
        return bank, rb

    def proj_tok(sl, rsl, c0, width, t):
        bank, rb = nxt("proj")
        for kc in range(8):
            mm(bank[:, 0:width], hT[:, kc, t * 128:(t + 1) * 128], sl[:, kc, c0:c0 + width], kc == 0, kc == 7,
               [rsl, rhT[t]], [rb])
        return bank, rb

    def out_proj(mixT_views, rmix, wo, rwo, first):
        for t in range(NT):
            for cg in range(2):
                bank, rb = nxt("proj")
                for c in range(2):
                    mm(bank[:], mixT_views[c][:, t * 128:(t + 1) * 128], wo[:, c, cg * 512:(cg + 1) * 512],
                       c == 0, c == 1, [rwo] + rmix, [rb])
                hv = h32[:, t, cg * 512:(cg + 1) * 512]
                if first:
                    stt("dve", hv, hv, ALPHA, bank[:], ALU.mult, ALU.add, [rb, rh[t]], [rh[t]])
                else:
                    tt("dve", hv, hv, bank[:], ALU.add, [rb, rh[t]], [rh[t]])

    def make_hT(t):
        for half in range(2):
            bank, rb = nxt("proj")
            for k in range(4):
                kc = half * 4 + k
                tr(bank[:, k * 128:(k + 1) * 128], h32[:, t, kc * 128:(kc + 1) * 128], identf[:], [rh[t], rconst], [rb])
            dst = hT[:, half * 4:half * 4 + 4, t * 128:(t + 1) * 128]
            src = bank[:].rearrange("p (k n) -> p k n", k=4)
            cp("act", dst, src, [rb], [rhT[t]])

    def layer_norm(l, s, last):
        lnt = qk[:, 0:2, :].bitcast(F32)
        p.dma("sp", lnt, wd[l]["ln"], writes=[rqk[0], rqk[1]])
        rl = [rqk[0], rqk[1]]
        for t in range(NT):
            for hf in range(2):
                p.op("dve", lambda e, t=t, hf=hf: e.bn_stats(out=st12[:, t, hf * 6:(hf + 1) * 6], in_=h32[:, t, hf * 512:(hf + 1) * 512]),
                     [rh[t]], [r_st12])
            p.op("dve", lambda e, t=t: e.bn_aggr(out=mvall[:, t, :], in_=st12[:, t, :]), [r_st12], [r_mv])
        act(rstd[:], mvall[:, :, 1], AF.Sqrt, [r_mv, r_eps], [r_rstd], bias=epst[:])
        p.op("dve", lambda e: e.reciprocal(out=rstd[:], in_=rstd[:]), [r_rstd], [r_rstd])
        stt("dve", nmr[:], mvall[:, :, 0], -1.0, rstd[:], ALU.mult, ALU.mult, [r_mv, r_rstd], [r_nmr])
        for t in range(NT):
            hv = h32[:, t, :]
            ts("dve", hv, hv, rstd[:, t:t + 1], nmr[:, t:t + 1], ALU.mult, ALU.add, [rh[t], r_rstd, r_nmr], [rh[t]])
            tt("pool", hv, hv, lnt[:, 0, :], ALU.mult, [rh[t]] + rl, [rh[t]])
            tt("dve", hv, hv, lnt[:, 1, :], ALU.add, [rh[t]] + rl, [rh[t]])
            if last:
                p.dma("sp", out_d[s].rearrange("(t p) d -> p t d", p=128)[:, t, :], hv, reads=[rh[t]])
            else:
                make_hT(t)

    def odd_layer(l):
        pr = prm[l]
        off = [0]

        def carve(nbytes, dt, shape):
            a = arena[:, off[0] // 4:(off[0] + nbytes) // 4]
            off[0] += nbytes
            if dt == BF16:
                a = a.bitcast(BF16)
            if len(shape) == 3:
                a = a.rearrange("p (a b) -> p a b", a=shape[1])
            return a

        vaug = carve(NT * 264 * 2, BF16, [128, NT, 264]); r_v = Res("vaug")
        sg = carve(NT * 256 * 2, BF16, [128, NT, 256]); r_sg = Res("sg")
        dd = carve(4 * 256 * 4, F32, [128, 4, 256]); r_dd = [Res(f"dd{i}") for i in range(4)]
        pTs = [carve(512 * 2, BF16, [128, 512]) for _ in range(3)]; r_pT = [Res(f"pT{i}") for i in range(3)]
        mixb = [carve(256 * 2, BF16, [128, 256]) for _ in range(2)]; r_mixb = [Res(f"mixb{i}") for i in range(2)]
        mtmp = [carve(256 * 4, F32, [128, 256]) for _ in range(2)]; r_mtmp = [Res(f"mtmp{i}") for i in range(2)]
        mixT = carve(2 * S * 2, BF16, [128, 2, S]); r_mixT = Res("mixT")
        sq = carve(256 * 4, F32, [128, 256]); r_sq = Res("sqj")
        ss = carve(16 * 4, F32, [128, 16]); r_ss = Res("ss")
        rec = carve(16 * 4, F32, [128, 16]); r_rec = Res("rec")
        assert off[0] <= ARENA, off[0]
        p.op("pool", lambda e: e.memset(vaug[:, :, 256:257], 1.0), [], [r_v])
        pT_ctr = [0]
        mix_ctr = [0]
        for h in range(8):
            for c in range(4):
                for G in range(4):
                    bank, rb = proj_feat(slabA, rslabA, c * 128, G)
                    rope(bank, rb, qk[:, c, G * 512:(G + 1) * 512], rqk[c], G)
            load_next("A")
            for t in range(NT):
                bank, rb = proj_tok(slabB, rslabB, 0, 512, t)
                cp("dve", vaug[:, t, 0:256], bank[:, 0:256], [rb], [r_v])
                act(sg[:, t, :], bank[:, 256:512], AF.Silu, [rb], [r_sg])
            load_next("B")
            wo, rwo = load_wout(l, h)
            for G in range(4):
                nk = 4 * G + 4
                for m in range(2):
                    qc, kc_ = m, 2 + m
                    obs = [nxt("o") for _ in range(4)]
                    stb = {}

                    def emit_s(kt):
                        q0 = max(4 * G, kt)
                        o_ = (q0 - 4 * G) * 128
                        bank, rb = nxt("st")
                        mm(bank[:, o_:512], qk[:, kc_, kt * 128:(kt + 1) * 128], qk[:, qc, G * 512 + o_:(G + 1) * 512],
                           True, True, [rqk[kc_], rqk[qc]], [rb])
                        stb[kt] = (bank, rb, q0, o_)

                    emit_s(0)
                    for kt in range(nk):
                        if kt + 1 < nk:
                            emit_s(kt + 1)
                        bank, rb, q0, o_ = stb.pop(kt)
                        i = pT_ctr[0] % 3
                        pT_ctr[0] += 1
                        pT, rp = pTs[i], r_pT[i]
                        act(pT[:, o_:512], bank[:, o_:512], AF.Exp, [rb], [rp], scale=SCALE)
                        if kt >= 4 * G:
                            tt("pool", pT[:, o_:o_ + 128], pT[:, o_:o_ + 128], cmask[:], ALU.mult, [rp, rconst], [rp])
                        for qt in range(q0, 4 * G + 4):
                            qi = qt - 4 * G
                            ob, rob = obs[qi]
                            mm(ob[:, 0:257], pT[:, qi * 128:(qi + 1) * 128], vaug[:, kt, 0:257], kt == 0, kt == qt,
                               [rp, r_v], [rob])
                    for qi in range(4):
                        ob, rob = obs[qi]
                        t = 4 * G + qi
                        col = m * 8 + qi
                        p.op("dve", lambda e, ob=ob, col=col: e.reciprocal(out=rec[:, col:col + 1], in_=ob[:, 256:257]), [rob], [r_rec])
                        if m == 0:
                            ts("dve", dd[:, qi, :], ob[:, 0:256], rec[:, col:col + 1], None, ALU.mult, None, [rob, r_rec], [r_dd[qi]])
                        else:
                            tt("dve", rec[:, col + 4:col + 5], rec[:, col:col + 1], pr["nlam"][:], ALU.mult, [r_rec, pr["r"]], [r_rec])
                            stt("dve", dd[:, qi, :], ob[:, 0:256], rec[:, col + 4:col + 5], dd[:, qi, :], ALU.mult, ALU.add,
                                [rob, r_rec, r_dd[qi]], [r_dd[qi]])
                p.op("dve", lambda e, G=G: e.memset(ss[:, G * 4:G * 4 + 4], 0.0), [], [r_ss])
                for qi in range(4):
                    act(sq[:], dd[:, qi, :], AF.Square, [r_dd[qi]], [r_sq, r_ss], accum=ss[:, G * 4 + qi:G * 4 + qi + 1])
                sv = ss[:, G * 4:G * 4 + 4]
                act(sv, sv, AF.Sqrt, [r_ss, r_eps], [r_ss], bias=epst[:], scale=1.0 / 256.0)
                p.op("dve", lambda e, sv=sv: e.reciprocal(out=sv, in_=sv), [r_ss], [r_ss])
                for qi in range(4):
                    t = 4 * G + qi
                    i = mix_ctr[0] % 2
                    mix_ctr[0] += 1
                    stt("dve", mtmp[i][:], dd[:, qi, :], ss[:, t:t + 1], pr["sub2"][:], ALU.mult, ALU.mult,
                        [r_dd[qi], r_ss, pr["r"]], [r_mtmp[i]])
                    tt("dve", mixb[i][:], mtmp[i][:], sg[:, t, :], ALU.mult, [r_mtmp[i], r_sg], [r_mixb[i]])
                    bank, rb = nxt("proj")
                    bv = bank[:, 0:128].bitcast(BF16)
                    for c in range(2):
                        tr(bv[:, c * 128:(c + 1) * 128], mixb[i][:, c * 128:(c + 1) * 128], identb[:], [r_mixb[i], rconst], [rb])
                    cp("act", mixT[:, :, t * 128:(t + 1) * 128], bv.rearrange("p (c n) -> p c n", c=2), [rb], [r_mixT])
            out_proj([mixT[:, 0, :], mixT[:, 1, :]], [r_mixT], wo, rwo, first=(h == 0))

    def even_layer(l):
        pr = prm[l]
        off = [0]

        def carve(nbytes, dt, shape):
            a = arena[:, off[0] // 4:(off[0] + nbytes) // 4]
            off[0] += nbytes
            if dt == BF16:
                a = a.bitcast(BF16)
            if len(shape) == 3:
                a = a.rearrange("p (a b) -> p a b", a=shape[1])
            return a

        kE = carve(NT * 128 * 2, BF16, [128, NT, 128]); r_kE = Res("kE")
        vv = carve(NT * 128 * 2, BF16, [128, NT, 128]); r_vv = Res("vv")
        sgr = carve(NT * 128 * 2, BF16, [128, NT, 128]); r_sgr = [Res(f"sgr{t}") for t in range(NT)]
        Rbs = [carve(128 * 2, BF16, [128, 128]) for _ in range(3)]; r_Rbs = [Res(f"Rb{i}") for i in range(3)]
        R32 = carve(128 * 4, F32, [128, 128]); r_R32 = Res("R32")
        ro = qk[:, 2:4, :].bitcast(F32).rearrange("p a (b c) -> p (a b) c", c=128)
        r_ro = [Res(f"ro{t}") for t in range(NT)]
        dmk = carve(128 * 4, F32, [128, 128]); r_dmk = Res("dmk")
        scs = [carve(128 * 2, BF16, [128, 128]) for _ in range(3)]; r_sc = [Res(f"sc{i}") for i in range(3)]
        mixT = carve(2 * S * 2, BF16, [128, 2, S]); r_mixR = Res("mixTr"); r_mixL = Res("mixTl")
        gnt = carve(2 * 128 * 4, F32, [128, 2, 128]); r_gnt = Res("gnt")
        gwt = carve(256 * 2, BF16, [128, 256]); r_gwt = Res("gwt")
        st6 = carve(NT * 6 * 4, F32, [128, NT, 6]); r_st6 = Res("st6")
        mv2 = carve(NT * 2 * 4, F32, [128, NT, 2]); r_mv2 = Res("mv2")
        rs2 = carve(NT * 4, F32, [128, NT]); r_rs2 = Res("rs2")
        nm2 = carve(NT * 4, F32, [128, NT]); r_nm2 = Res("nm2")
        xb32 = carve(516 * 4, F32, [128, 516]); r_xb = Res("xb32")
        names = ["xc", "rr", "ig", "aa", "a2", "hh0", "hh1"]
        lt = {n: carve(512 * 4, F32, [128, 512]) for n in names}
        rlt = {n: Res(n) for n in names}
        xcb = carve(512 * 2, BF16, [128, 512]); r_xcb = Res("xcb")
        sgl = carve(512 * 4, F32, [128, 512]); r_sgl = Res("sgl")
        assert off[0] <= ARENA, off[0]
        sc_ctr = [0]
        for u in range(8):
            if STOP <= 0:
                continue
            gC = float((1.0 - 2.0 ** (-5.0 - u)) ** 128)
            p.dma("sp", gnt[:], wd[l]["gn"][:, :, u * 128:(u + 1) * 128], writes=[r_gnt])
            p.dma("pool", gwt[:], wd[l]["gw"][u], writes=[r_gwt])
            p.dma("sp", dmk[:], dmask_d[:, u, :], writes=[r_dmk])
            for c in range(2):
                for G in range(4):
                    bank, rb = proj_feat(slabA, rslabA, c * 128, G)
                    rope(bank, rb, qk[:, c, G * 512:(G + 1) * 512], rqk[c], G)
            if STOP <= 1:
                continue
            for t in range(NT):
                bank, rb = proj_tok(slabA, rslabA, 256, 256, t)
                cp("dve", vv[:, t, :], bank[:, 0:128], [rb], [r_vv])
                act(sgr[:, t, :], bank[:, 128:256], AF.Silu, [rb], [r_sgr[t]])
            load_next("A")
            if STOP <= 2:
                continue
            for t4 in range(4):
                bank, rb = nxt("proj")
                bv = bank[:, 0:256].bitcast(BF16)
                for k in range(4):
                    t = t4 * 4 + k
                    tr(bv[:, k * 128:(k + 1) * 128], qk[:, 1, t * 128:(t + 1) * 128], identb[:], [rqk[1], rconst], [rb])
                ts("dve", kE[:, t4 * 4:t4 * 4 + 4, :], bv.rearrange("p (k n) -> p k n", k=4), kend[:, u:u + 1], None, ALU.mult, None,
                   [rb, rconst], [r_kE])
            if STOP <= 3:
                continue
            stbs = {}

            def emit_a(t):
                bank, rb = nxt("st")
                mm(bank[:, 0:128], qk[:, 1, t * 128:(t + 1) * 128], qk[:, 0, t * 128:(t + 1) * 128], True, True, [rqk[0], rqk[1]], [rb])
                mm(bank[:, 128:256], kE[:, t, :], vv[:, t, :], True, True, [r_kE, r_vv], [rb])
                stbs[t] = (bank, rb)

            emit_a(0)
            for t in range(NT):
                if t + 1 < NT:
                    emit_a(t + 1)
                bank, rb = stbs.pop(t)
                i = sc_ctr[0] % 3
                sc_ctr[0] += 1
                sc, rsc = scs[i], r_sc[i]
                tt("dve", sc[:], bank[:, 0:128], dmk[:], ALU.mult, [rb, r_dmk], [rsc])
                if t + 1 < NT:
                    if t == 0:
                        cp("act", R32[:], bank[:, 128:256], [rb], [r_R32])
                    else:
                        stt("dve", R32[:], R32[:], gC, bank[:, 128:256], ALU.mult, ALU.add, [rb, r_R32], [r_R32])
                    cp("act", Rbs[(t + 1) % 3][:], R32[:], [r_R32], [r_Rbs[(t + 1) % 3]])
                ob, rob = nxt("o")
                mm(ob[:, 0:128], sc[:], vv[:, t, :], True, True, [rsc, r_vv], [rob])
                if t > 0:
                    mm(ob[:, 128:256], qk[:, 0, t * 128:(t + 1) * 128], Rbs[t % 3][:], True, True, [rqk[0], r_Rbs[t % 3]], [rob])
                cp("act", ro[:, t, :], ob[:, 0:128], [rob], [r_ro[t]])
                if t > 0:
                    stt("dve", ro[:, t, :], ob[:, 128:256], gpow[:, u:u + 1], ro[:, t, :], ALU.mult, ALU.add,
                        [rob, r_ro[t], rconst], [r_ro[t]])
                p.op("dve", lambda e, t=t: e.bn_stats(out=st6[:, t, :], in_=ro[:, t, :]), [r_ro[t]], [r_st6])
                p.op("dve", lambda e, t=t: e.bn_aggr(out=mv2[:, t, :], in_=st6[:, t, :]), [r_st6], [r_mv2])
            if STOP <= 4:
                continue
            act(rs2[:], mv2[:, :, 1], AF.Sqrt, [r_mv2, r_eps], [r_rs2], bias=epst[:])
            p.op("dve", lambda e: e.reciprocal(out=rs2[:], in_=rs2[:]), [r_rs2], [r_rs2])
            stt("dve", nm2[:], mv2[:, :, 0], -1.0, rs2[:], ALU.mult, ALU.mult, [r_mv2, r_rs2], [r_nm2])
            for t in range(NT):
                rv = ro[:, t, :]
                ts("dve", rv, rv, rs2[:, t:t + 1], nm2[:, t:t + 1], ALU.mult, ALU.add, [r_ro[t], r_rs2, r_nm2], [r_ro[t]])
                tt("pool", rv, rv, gnt[:, 0, :], ALU.mult, [r_ro[t], r_gnt], [r_ro[t]])
                tt("pool", rv, rv, gnt[:, 1, :], ALU.add, [r_ro[t], r_gnt], [r_ro[t]])
                tt("dve", sgr[:, t, :], rv, sgr[:, t, :], ALU.mult, [r_ro[t], r_sgr[t]], [r_sgr[t]])
            for t4 in range(4):
                bank, rb = nxt("proj")
                bv = bank[:, 0:256].bitcast(BF16)
                for k in range(4):
                    t = t4 * 4 + k
                    tr(bv[:, k * 128:(k + 1) * 128], sgr[:, t, :], identb[:], [r_sgr[t], rconst], [rb])
                cp("act", mixT[:, 0, t4 * 512:(t4 + 1) * 512], bv, [rb], [r_mixR])
            if STOP <= 5:
                continue
            L = pr["lrup"]
            p.op("dve", lambda e: e.memset(xb32[:, 0:3], 0.0), [], [r_xb])
            for G in range(4):
                hh, r_hh = lt[f"hh{G % 2}"], rlt[f"hh{G % 2}"]
                hp, r_hp = lt[f"hh{(G + 1) % 2}"], rlt[f"hh{(G + 1) % 2}"]
                if G > 0:
                    cp("dve", xb32[:, 0:3], xb32[:, 512:515], [r_xb], [r_xb])
                bank, rb = proj_feat(slabB, rslabB, 0, G)
                cp("act", xb32[:, 3:515], bank[:], [rb], [r_xb])
                bank, rb = proj_feat(slabB, rslabB, 128, G)
                act(sgl[:], bank[:], AF.Silu, [rb], [r_sgl])
                xc = lt["xc"]
                ts("dve", xc[:], xb32[:, 3:515], L[:, u, 3:4], L[:, u, 4:5], ALU.mult, ALU.add, [r_xb, pr["r"]], [rlt["xc"]])
                for w in (2, 1, 0):
                    stt("dve", xc[:], xb32[:, w:w + 512], L[:, u, w:w + 1], xc[:], ALU.mult, ALU.add, [r_xb, pr["r"], rlt["xc"]], [rlt["xc"]])
                cp("pool", xcb[:], xc[:], [rlt["xc"]], [r_xcb])
                bank, rb = nxt("proj")
                mm(bank[:], gwt[:, 0:128], xcb[:], True, True, [r_gwt, r_xcb], [rb])
                act(lt["rr"][:], bank[:], AF.Sigmoid, [rb, pr["r"]], [rlt["rr"]], bias=L[:, u, 5:6])
                bank, rb = nxt("proj")
                mm(bank[:], gwt[:, 128:256], xcb[:], True, True, [r_gwt, r_xcb], [rb])
                act(lt["ig"][:], bank[:], AF.Sigmoid, [rb, pr["r"]], [rlt["ig"]], bias=L[:, u, 6:7])
                act(lt["aa"][:], lt["rr"][:], AF.Exp, [rlt["rr"], pr["r"]], [rlt["aa"]], scale=pr["c"][:, u:u + 1])
                act(lt["a2"][:], lt["rr"][:], AF.Exp, [rlt["rr"], pr["r"]], [rlt["a2"]], scale=pr["c2"][:, u:u + 1])
                act(lt["rr"][:], lt["rr"][:], AF.Tanh, [rlt["rr"], pr["r"]], [rlt["rr"]], scale=pr["cn"][:, u:u + 1])
                stt("dve", lt["a2"][:], lt["a2"][:], 1.0, lt["rr"][:], ALU.add, ALU.mult, [rlt["a2"], rlt["rr"]], [rlt["a2"]])
                act(lt["a2"][:], lt["a2"][:], AF.Sqrt, [rlt["a2"]], [rlt["a2"]])
                tt("pool", lt["ig"][:], lt["ig"][:], xc[:], ALU.mult, [rlt["ig"], rlt["xc"]], [rlt["ig"]])
                tt("dve", lt["ig"][:], lt["ig"][:], lt["a2"][:], ALU.mult, [rlt["ig"], rlt["a2"]], [rlt["ig"]])
                init = 0.0 if G == 0 else hp[:, 511:512]
                p.op("dve", lambda e, hh=hh, init=init: e.tensor_tensor_scan(out=hh[:], data0=lt["aa"][:], data1=lt["ig"][:], initial=init,
                                                                              op0=ALU.mult, op1=ALU.add),
                     [rlt["aa"], rlt["ig"], r_hp], [r_hh])
                tt("pool", mixT[:, 1, G * 512:(G + 1) * 512], hh[:], sgl[:], ALU.mult, [r_hh, r_sgl], [r_mixL])
            load_next("B")
            if STOP <= 6:
                continue
            wo, rwo = load_wout(l, u)
            out_proj([mixT[:, 0, :], mixT[:, 1, :]], [r_mixR, r_mixL], wo, rwo, first=(u == 0))

    load_next("A")
    load_next("B")
    for s in range(nseq):
        xv = x_d[s].rearrange("(t p) d -> p t d", p=128)
        for t in range(NT):
            p.dma("sp", h32[:, t, :], xv[:, t, :], writes=[rh[t]])
        for t in range(NT):
            make_hT(t)
        if not layers:
            for t in range(NT):
                p.dma("sp", out_d[s].rearrange("(t p) d -> p t d", p=128)[:, t, :], h32[:, t, :], reads=[rh[t]])
        for li_, l in enumerate(layers):
            if l % 2 == 0:
                even_layer(l)
            else:
                odd_layer(l)
            if STOP <= -1:
                for t in range(NT):
                    p.dma("sp", out_d[s].rearrange("(t p) d -> p t d", p=128)[:, t, :], h32[:, t, :], reads=[rh[t]])
                continue
            layer_norm(l, s, last=(li_ == len(layers) - 1))
    p.wait_all_dma("sp", rh)
    p.emit()
    return nc, p.stats


def host_constants():
    half = 64
    inv_freq = (10000.0 ** (-np.arange(half, dtype=np.float32) / half)).astype(np.float32)
    pos = np.arange(S, dtype=np.float32)
    ang = pos[None, :] * inv_freq[:, None]
    cos = np.cos(ang).astype(np.float32)
    sin = np.sin(ang).astype(np.float32)
    cosT = np.concatenate([cos, cos], axis=0)
    sinS = np.concatenate([-sin, sin], axis=0)
    identf = np.eye(128, dtype=np.float32)
    j = np.arange(128)[:, None]
    i = np.arange(128)[None, :]
    cmask = (i >= j).astype(np.float32)
    g = (1.0 - 2.0 ** (-5.0 - np.arange(8, dtype=np.float64)))
    dmask = np.zeros((128, 8, 128), np.float64)
    for h in range(8):
        dmask[:, h, :] = np.where(i >= j, g[h] ** np.maximum(i - j, 0), 0.0) * SCALE
    gpow = g[None, :] ** (np.arange(128)[:, None] + 1.0)
    kend = (g[None, :] ** (127.0 - np.arange(128)[:, None])) * SCALE
    return dict(cosT=np.ascontiguousarray(cosT), sinS=np.ascontiguousarray(sinS), identf=identf, cmask=cmask,
                dmask=dmask.astype(np.float32), gpow=gpow.astype(np.float32), kend=kend.astype(np.float32))


def rep(a):
    return np.ascontiguousarray(np.broadcast_to(a, (128,) + a.shape))


def host_weights(inp, layers):
    m = {}
    for l in layers:
        j = l // 2
        if l % 2 == 0:
            w = inp["ev_w_in"][j]
            cols = []
            for u in range(8):
                cols.append(np.concatenate([w[:, k * 1024 + u * 128:k * 1024 + (u + 1) * 128] for k in range(6)], axis=1))
            m[f"w_in{l}"] = np.ascontiguousarray(np.stack(cols, 0))
            wo = inp["ev_w_out"][j]
            m[f"w_out{l}"] = np.ascontiguousarray(np.stack(
                [np.concatenate([wo[u * 128:(u + 1) * 128], wo[1024 + u * 128:1024 + (u + 1) * 128]], 0) for u in range(8)], 0))
            m[f"gw{l}"] = np.ascontiguousarray(np.concatenate([inp["ev_gate_a_w"][j], inp["ev_gate_x_w"][j]], axis=2))
            lr = np.stack([inp["ev_conv_w"][j][0], inp["ev_conv_w"][j][1], inp["ev_conv_w"][j][2], inp["ev_conv_w"][j][3],
                           inp["ev_conv_b"][j], inp["ev_gate_a_b"][j], inp["ev_gate_x_b"][j], inp["ev_lru_lambda"][j]], axis=1)
            m[f"lrup{l}"] = np.ascontiguousarray(lr.reshape(8, 128, 8).transpose(1, 0, 2))
            m[f"gn{l}"] = rep(np.stack([inp["ev_ret_gn_g"][j], inp["ev_ret_gn_b"][j]], 0))
            m[f"ln{l}"] = rep(np.stack([inp["ev_ln_g"][j], inp["ev_ln_b"][j]], 0))
        else:
            w = inp["od_w_in"][j]
            cols = []
            for h in range(8):
                q1 = w[:, (2 * h) * 128:(2 * h + 1) * 128]
                q2 = w[:, (2 * h + 1) * 128:(2 * h + 2) * 128]
                k1 = w[:, 2048 + (2 * h) * 128:2048 + (2 * h + 1) * 128]
                k2 = w[:, 2048 + (2 * h + 1) * 128:2048 + (2 * h + 2) * 128]
                v = w[:, 4096 + h * 256:4096 + (h + 1) * 256]
                gg = w[:, 6144 + h * 256:6144 + (h + 1) * 256]
                cols.append(np.concatenate([q1, q2, k1, k2, v, gg], axis=1))
            m[f"w_in{l}"] = np.ascontiguousarray(np.stack(cols, 0))
            m[f"w_out{l}"] = np.ascontiguousarray(inp["od_w_out"][j].reshape(8, 256, 1024))
            m[f"lam{l}"] = rep(np.stack([inp["od_lambda_q1"][j], inp["od_lambda_k1"][j], inp["od_lambda_q2"][j], inp["od_lambda_k2"][j]], 0))
            m[f"sub{l}"] = rep(inp["od_subln_g"][j])
            m[f"ln{l}"] = rep(np.stack([inp["od_ln_g"][j], inp["od_ln_b"][j]], 0))
    return m


_PROGS = {}


def run_layers(x, inp, layers, ncores=NCORES, trace=False):
    B = x.shape[0]
    nseq = B // ncores
    key = (tuple(layers), nseq)
    if key not in _PROGS:
        _PROGS[key] = build_program(layers, nseq)
    nc, stats = _PROGS[key]
    consts = host_constants()
    wts = host_weights(inp, layers)
    in_maps = []
    for c in range(ncores):
        mdict = dict(consts)
        mdict.update(wts)
        mdict["x"] = np.ascontiguousarray(x[c * nseq:(c + 1) * nseq])
        in_maps.append(mdict)
    res = run_bass_kernel_spmd(nc, in_maps, core_ids=list(range(ncores)))
    return np.concatenate([r["out"] for r in res.results], axis=0)


FUSED = False
import os
STOP = int(os.environ.get("KSTOP", "99"))


def kernel(**inputs):
    inp = {k: np.asarray(v) for k, v in inputs.items()}
    x = np.ascontiguousarray(inp["x"], dtype=np.float32)
    if FUSED:
        return run_layers(x, inp, [0, 1, 2, 3]).astype(np.float32)
    h = x
    for l in range(DEPTH):
        h = run_layers(h, inp, [l])
    return h.astype(np.float32)
```

```python
import math
import numpy as np
from contextlib import ExitStack
import concourse.bass as bass
import concourse.mybir as mybir
from concourse.bass_utils import run_bass_kernel_spmd

F32 = mybir.dt.float32
BF16 = mybir.dt.bfloat16
ALU = mybir.AluOpType
AF = mybir.ActivationFunctionType
AX = mybir.AxisListType

NCORES = 8
S = 2048
D = 1024
NT = 16
DEPTH = 4
EPS = 1e-5
ALPHA = (2.0 * DEPTH) ** 0.25
SCALE = 128.0 ** -0.5

ENG = ("pe", "act", "dve", "pool", "sp")
EPOCH = 30000


class Res:
    __slots__ = ("name", "w", "r", "sem", "excl")

    def __init__(self, name, excl=False):
        self.excl = excl
        self.name = name
        self.w = None
        self.r = {}
        self.sem = None


class Prog:
    def __init__(self, nc):
        self.nc = nc
        self.ops = {e: [] for e in ENG}
        self.waited = {e: {} for e in ENG}
        self.dma_cnt = []
        self.stack = ExitStack()
        self.n_sb = 0

    def sb(self, shape, dt, name=None):
        self.n_sb += 1
        return self.stack.enter_context(self.nc.sbuf_tensor("S_" + (name or f"sb{self.n_sb}"), list(shape), dt))

    def ps(self, shape, dt, name=None):
        self.n_sb += 1
        return self.stack.enter_context(self.nc.psum_tensor("P_" + (name or f"ps{self.n_sb}"), list(shape), dt))

    def _need(self, eng, dep, waits):
        if dep is None:
            return
        key = (dep[0], dep[1])
        val = dep[2]
        if self.waited[eng].get(key, -1) >= val:
            return
        if waits.get(key, -1) >= val:
            return
        waits[key] = val

    def op(self, eng, fn, reads=(), writes=()):
        ex = [r for r in reads if r.excl]
        if ex:
            reads = [r for r in reads if not r.excl]
            writes = list(writes) + [r for r in ex if r not in writes]
        idx = len(self.ops[eng])
        waits = {}
        for r in reads:
            d = r.w
            if d is None:
                continue
            if d[0] == "op" and d[1] == eng and eng in ("pe", "sp"):
                continue
            self._need(eng, d, waits)
        for w in writes:
            for d in ([w.w] if w.w is not None else []) + list(w.r.values()):
                if d[0] == "op" and d[1] == eng:
                    continue
                self._need(eng, d, waits)
        for k, v in waits.items():
            self.waited[eng][k] = v
        me = ("op", eng, idx)
        for r in reads:
            r.r[("op", eng)] = me
        for w in writes:
            w.w = me
            w.r = {}
        self.ops[eng].append(dict(fn=fn, waits=waits, dma=None, inc=False))
        return me

    def dma(self, eng, out, in_, reads=(), writes=(), sem_res=None):
        if sem_res is None:
            sem_res = writes[0] if writes else reads[0]
        if sem_res.sem is None:
            sem_res.sem = len(self.dma_cnt)
            self.dma_cnt.append(0)
        sid = sem_res.sem
        waits = {}
        for r in reads:
            self._need(eng, r.w, waits)
        for w in writes:
            for d in ([w.w] if w.w is not None else []) + list(w.r.values()):
                self._need(eng, d, waits)
        for k, v in waits.items():
            self.waited[eng][k] = v
        self.dma_cnt[sid] += 16
        me = ("dma", sid, self.dma_cnt[sid])
        for r in reads:
            r.r[("dma", sid)] = me
        for w in writes:
            w.w = me
            w.r = {}
        self.ops[eng].append(dict(fn=lambda e, o=out, i=in_: e.dma_start(out=o, in_=i), waits=waits, dma=sid, inc=False))
        return me

    def barrier(self, engines=("pe", "act", "dve", "pool")):
        last = {}
        for e in engines:
            i = len(self.ops[e]) - 1
            while i >= 0 and (self.ops[e][i]["fn"] is None or self.ops[e][i]["dma"] is not None):
                i -= 1
            if i >= 0:
                last[e] = i
        for e in tuple(engines) + ("sp",):
            waits = {}
            for e2, i2 in last.items():
                if e2 != e:
                    self._need(e, ("op", e2, i2), waits)
            for k, v in waits.items():
                self.waited[e][k] = v
            self.ops[e].append(dict(fn=None, waits=waits, dma=None, inc=False))

    def wait_all_dma(self, eng, resources):
        waits = {}
        for r in resources:
            for d in ([r.w] if r.w is not None else []) + list(r.r.values()):
                if d[0] == "dma":
                    self._need(eng, d, waits)
        for k, v in waits.items():
            self.waited[eng][k] = v
        self.ops[eng].append(dict(fn=None, waits=waits, dma=None, inc=False))

    def emit(self):
        nc = self.nc
        for e in ENG:
            for o in self.ops[e]:
                for (kind, src), val in o["waits"].items():
                    if kind == "op":
                        self.ops[src][val]["inc"] = True
        semval = {e: {} for e in ENG}
        nsem = {}
        for e in ENG:
            c = 0
            for i, o in enumerate(self.ops[e]):
                if o["inc"]:
                    semval[e][i] = (c // EPOCH, c % EPOCH + 1)
                    c += 1
            nsem[e] = (c + EPOCH - 1) // EPOCH
        st = self.stack
        esems = {e: [st.enter_context(nc.semaphore(f"s_{e}{k}")) for k in range(nsem[e])] for e in ENG}
        dsems = [st.enter_context(nc.semaphore(f"d{k}")) for k in range(len(self.dma_cnt))]
        self.stats = {e: (len(self.ops[e]), sum(len(o["waits"]) for o in self.ops[e])) for e in ENG}
        block = st.enter_context(nc.Block())

        def runner(e):
            def run(eng):
                for i, o in enumerate(self.ops[e]):
                    for (kind, src), val in o["waits"].items():
                        if kind == "op":
                            ep, v = semval[src][val]
                            eng.wait_ge(esems[src][ep], v)
                        else:
                            eng.wait_ge(dsems[src], val)
                    if o["fn"] is None:
                        continue
                    ins = o["fn"](eng)
                    if o["dma"] is not None:
                        ins.then_inc(dsems[o["dma"]], 16)
                    elif o["inc"]:
                        ep, v = semval[e][i]
                        ins.then_inc(esems[e][ep], 1)
            return run

        block.tensor(runner("pe"))
        block.scalar(runner("act"))
        block.vector(runner("dve"))
        block.gpsimd(runner("pool"))
        block.sync(runner("sp"))
        st.close()


def lambda_init(l):
    return 0.8 - 0.6 * math.exp(-0.3 * l)


def build_program(layers, nseq, first_is_input=True):
    nc = bass.Bass("TRN2", target_bir_lowering=False)
    p = Prog(nc)

    def din(name, shape):
        return nc.dram_tensor(name, list(shape), F32, kind="ExternalInput").ap()

    x_d = din("x", [nseq, S, D])
    out_d = nc.dram_tensor("out", [nseq, S, D], F32, kind="ExternalOutput").ap()
    cos_d = din("cosT", [128, S])
    sin_d = din("sinS", [128, S])
    identf_d = din("identf", [128, 128])
    cmask_d = din("cmask", [128, 128])
    dmask_d = din("dmask", [128, 8, 128])
    gpow_d = din("gpow", [128, 8])
    kend_d = din("kend", [128, 8])
    wd = {}
    for l in layers:
        if l % 2 == 0:
            wd[l] = dict(w_in=din(f"w_in{l}", [8, D, 768]), w_out=din(f"w_out{l}", [8, 256, D]),
                         gw=din(f"gw{l}", [8, 128, 256]), lrup=din(f"lrup{l}", [128, 8, 8]),
                         gn=din(f"gn{l}", [128, 2, D]), ln=din(f"ln{l}", [128, 2, D]))
        else:
            wd[l] = dict(w_in=din(f"w_in{l}", [8, D, 1024]), w_out=din(f"w_out{l}", [8, 256, D]),
                         lam=din(f"lam{l}", [128, 4, 128]), sub=din(f"sub{l}", [128, 256]),
                         ln=din(f"ln{l}", [128, 2, D]))

    h32 = p.sb([128, NT, D], F32, "h32")
    rh = [Res(f"h32_{t}") for t in range(NT)]
    hT = p.sb([128, 8, S], BF16, "hT")
    rhT = [Res(f"hT_{t}") for t in range(NT)]
    slabA = p.sb([128, 8, 512], BF16, "slabA"); rslabA = Res("slabA")
    slabB = p.sb([128, 8, 512], BF16, "slabB"); rslabB = Res("slabB")
    wout = [p.sb([128, 2, D], BF16, f"wout{i}") for i in range(1)]
    rwout = [Res(f"wout{i}") for i in range(1)]
    qk = p.sb([128, 4, S], BF16, "qk")
    rqk = [Res(f"qk{i}") for i in range(4)]
    cosT = p.sb([128, S], F32, "cosT"); sinS = p.sb([128, S], F32, "sinS")
    rconst = Res("const")
    identf = p.sb([128, 128], F32, "identf")
    identb = p.sb([128, 128], BF16, "identb")
    cmask = p.sb([128, 128], BF16, "cmask")
    gpow = p.sb([128, 8], F32, "gpow")
    kend = p.sb([128, 8], F32, "kend")
    epst = p.sb([128, 1], F32, "eps")
    onet = p.sb([128, 1], F32, "one")
    rxs = [p.sb([128, 512], F32, f"rxs{i}") for i in range(2)]; r_rxs = [Res(f"rxs{i}") for i in range(2)]
    rtt = [p.sb([128, 512], F32, f"rtt{i}") for i in range(2)]; r_rtt = [Res(f"rtt{i}") for i in range(2)]
    ARENA = 44800
    arena = p.sb([128, ARENA // 4], F32, "arena")
    st12 = p.sb([128, NT, 12], F32, "st12"); r_st12 = Res("st12")
    mvall = p.sb([128, NT, 2], F32, "mvall"); r_mv = Res("mvall")
    rstd = p.sb([128, NT], F32, "rstd"); r_rstd = Res("rstd")
    nmr = p.sb([128, NT], F32, "nmr"); r_nmr = Res("nmr")
    prm = {}
    for l in layers:
        if l % 2 == 0:
            prm[l] = dict(lrup=p.sb([128, 8, 8], F32, f"lrup{l}"), c=p.sb([128, 8], F32, f"c{l}"),
                          c2=p.sb([128, 8], F32, f"c2{l}"), cn=p.sb([128, 8], F32, f"cn{l}"), r=Res(f"prm{l}"))
        else:
            prm[l] = dict(lamt=rxs[0][:].rearrange("p (a b) -> p a b", a=4), nlam=p.sb([128, 1], F32, f"nlam{l}"),
                          sub2=p.sb([128, 256], F32, f"sub2{l}"), tmp=rtt[0][:, 0:256].rearrange("p (a b) -> p a b", a=2),
                          s12=p.sb([128, 2], F32, f"s12{l}"), r=Res(f"prm{l}"))

    banks = [p.ps([128, 512], F32, f"bank{i}") for i in range(8)]
    rbank = [Res(f"bank{i}", excl=True) for i in range(8)]
    grp = {"proj": [0, 1], "st": [2, 3], "o": [4, 5, 6, 7]}
    gcnt = {k: 0 for k in grp}

    def nxt(g):
        i = grp[g][gcnt[g] % len(grp[g])]
        gcnt[g] += 1
        return banks[i], rbank[i]

    def mm(out, lhsT, rhs, start, stop, reads, writes):
        p.op("pe", lambda e: e.matmul(out, lhsT=lhsT, rhs=rhs, start=start, stop=stop), reads, writes)

    def tr(out, in_, ident, reads, writes):
        p.op("pe", lambda e: e.transpose(out=out, in_=in_, identity=ident), reads, writes)

    def act(out, in_, func, reads, writes, bias=None, scale=1.0, accum=None):
        kw = {}
        if bias is not None:
            kw["bias"] = bias
        if accum is not None:
            kw["accum_out"] = accum
        p.op("act", lambda e: e.activation(out=out, in_=in_, func=func, scale=scale, **kw), reads, writes)

    def tt(eng, out, in0, in1, op, reads, writes):
        p.op(eng, lambda e: e.tensor_tensor(out=out, in0=in0, in1=in1, op=op), reads, writes)

    def ts(eng, out, in0, s1, s2, op0, op1, reads, writes):
        if s2 is None:
            p.op(eng, lambda e: e.tensor_single_scalar(out=out, in_=in0, scalar=s1, op=op0), reads, writes)
        else:
            p.op(eng, lambda e: e.tensor_scalar(out=out, in0=in0, scalar1=s1, scalar2=s2, op0=op0, op1=op1), reads, writes)

    def stt(eng, out, in0, scalar, in1, op0, op1, reads, writes):
        p.op(eng, lambda e: e.scalar_tensor_tensor(out=out, in0=in0, scalar=scalar, in1=in1, op0=op0, op1=op1), reads, writes)

    def cp(eng, out, in_, reads, writes):
        if eng == "act":
            p.op("act", lambda e: e.copy(out=out, in_=in_), reads, writes)
        else:
            p.op(eng, lambda e: e.tensor_copy(out=out, in_=in_), reads, writes)

    p.dma("sp", cosT[:], cos_d, writes=[rconst])
    p.dma("sp", sinS[:], sin_d, writes=[rconst])
    p.dma("sp", identf[:], identf_d, writes=[rconst])
    p.dma("pool", identb[:], identf_d, writes=[rconst])
    p.dma("pool", cmask[:], cmask_d, writes=[rconst])
    p.dma("sp", gpow[:], gpow_d, writes=[rconst])
    p.dma("sp", kend[:], kend_d, writes=[rconst])
    r_eps = Res("eps")
    p.op("dve", lambda e: e.memset(epst[:], EPS), writes=[r_eps])
    p.op("dve", lambda e: e.memset(onet[:], 1.0), writes=[r_eps])
    for l in layers:
        pr = prm[l]
        if STOP <= -2:
            continue
        if l % 2 == 0:
            p.dma("sp", pr["lrup"][:], wd[l]["lrup"], writes=[pr["r"]])
            act(pr["c"][:], pr["lrup"][:, :, 7], AF.Exp, [pr["r"]], [pr["r"]], scale=-1.0)
            act(pr["c"][:], pr["c"][:], AF.Ln, [pr["r"], r_eps], [pr["r"]], bias=onet[:])
            ts("dve", pr["cn"][:], pr["c"][:], 8.0, None, ALU.mult, None, [pr["r"]], [pr["r"]])
            ts("dve", pr["c"][:], pr["cn"][:], -1.0, None, ALU.mult, None, [pr["r"]], [pr["r"]])
            ts("dve", pr["c2"][:], pr["c"][:], 2.0, None, ALU.mult, None, [pr["r"]], [pr["r"]])
        else:
            li = lambda_init(l)
            p.dma("sp", pr["lamt"], wd[l]["lam"], writes=[r_rxs[0]])
            p.dma("sp", pr["sub2"][:], wd[l]["sub"], writes=[pr["r"]])
            tt("dve", pr["tmp"][:, 0, :], pr["lamt"][:, 0, :], pr["lamt"][:, 1, :], ALU.mult, [r_rxs[0]], [r_rtt[0]])
            tt("dve", pr["tmp"][:, 1, :], pr["lamt"][:, 2, :], pr["lamt"][:, 3, :], ALU.mult, [r_rxs[0]], [r_rtt[0]])
            p.op("dve", lambda e, pr=pr: e.reduce_sum(out=pr["s12"][:], in_=pr["tmp"], axis=AX.X), [r_rtt[0]], [pr["r"]])
            act(pr["s12"][:], pr["s12"][:], AF.Exp, [pr["r"]], [pr["r"]])
            tt("dve", pr["nlam"][:], pr["s12"][:, 1:2], pr["s12"][:, 0:1], ALU.subtract, [pr["r"]], [pr["r"]])
            ts("dve", pr["nlam"][:], pr["nlam"][:], -li, None, ALU.add, None, [pr["r"]], [pr["r"]])
            ts("dve", pr["sub2"][:], pr["sub2"][:], 1.0 - li, None, ALU.mult, None, [pr["r"]], [pr["r"]])

    sched = []
    for s_ in range(nseq):
        for l_ in layers:
            for u_ in range(8):
                sched.append((l_, u_))
    pos = {"A": 0, "B": 0}

    def load_next(which):
        i = pos[which]
        pos[which] += 1
        if i >= len(sched):
            return
        l, u = sched[i]
        src = wd[l]["w_in"][u].rearrange("(kc p) c -> p kc c", p=128)
        if which == "A":
            dst, rdst, c0, w = slabA, rslabA, 0, 512
        else:
            dst, rdst, c0, w = slabB, rslabB, 512, (512 if l % 2 == 1 else 256)
        for kc0 in range(0, 8, 4):
            p.dma("pool", dst[:, kc0:kc0 + 4, 0:w], src[:, kc0:kc0 + 4, c0:c0 + w], writes=[rdst])

    def load_wout(l, u):
        src = wd[l]["w_out"][u].rearrange("(c p) n -> p c n", p=128)
        p.dma("pool", wout[0][:], src, writes=[rwout[0]])
        return wout[0], rwout[0]

    rope_ctr = [0]

    def rope(bank, rb, dst, rdst, G):
        i = rope_ctr[0] % 2
        rope_ctr[0] += 1
        xs, r1 = rxs[i], r_rxs[i]
        t1, r2 = rtt[i], r_rtt[i]
        cs = cosT[:, G * 512:(G + 1) * 512]
        sn = sinS[:, G * 512:(G + 1) * 512]
        p.op("act", lambda e: e.copy(out=xs[0:64, :], in_=bank[64:128, :]), [rb], [r1])
        p.op("act", lambda e: e.copy(out=xs[64:128, :], in_=bank[0:64, :]), [rb], [r1])
        tt("dve", t1[:], bank[:], cs, ALU.mult, [rb, rconst], [r2])
        tt("pool", xs[:], xs[:], sn, ALU.mult, [r1, rconst], [r1])
        tt("dve", dst, t1[:], xs[:], ALU.add, [r1, r2], [rdst])

    def proj_feat(sl, rsl, c0, G):
        bank, rb = nxt("proj")
        for kc in range(8):
            mm(bank[:], sl[:, kc, c0:c0 + 128], hT[:, kc, G * 512:(G + 1) * 512], kc == 0, kc == 7,
               [rsl] + rhT[4 * G:4 * G + 4], [rb])
        return bank, rb

    def proj_tok(sl, rsl, c0, width, t):
        bank, rb = nxt("proj")
        for kc in range(8):
            mm(bank[:, 0:width], hT[:, kc, t * 128:(t + 1) * 128], sl[:, kc, c0:c0 + width], kc == 0, kc == 7,
               [rsl, rhT[t]], [rb])
        return bank, rb

    def out_proj(mixT_views, rmix, wo, rwo, first):
        for t in range(NT):
            for cg in range(2):
                bank, rb = nxt("proj")
                for c in range(2):
                    mm(bank[:], mixT_views[c][:, t * 128:(t + 1) * 128], wo[:, c, cg * 512:(cg + 1) * 512],
                       c == 0, c == 1, [rwo] + rmix, [rb])
                hv = h32[:, t, cg * 512:(cg + 1) * 512]
                if first:
                    stt("dve", hv, hv, ALPHA, bank[:], ALU.mult, ALU.add, [rb, rh[t]], [rh[t]])
                else:
                    tt("dve", hv, hv, bank[:], ALU.add, [rb, rh[t]], [rh[t]])

    def make_hT(t):
        for half in range(2):
            bank, rb = nxt("proj")
            for k in range(4):
                kc = half * 4 + k
                tr(bank[:, k * 128:(k + 1) * 128], h32[:, t, kc * 128:(kc + 1) * 128], identf[:], [rh[t], rconst], [rb])
            dst = hT[:, half * 4:half * 4 + 4, t * 128:(t + 1) * 128]
            src = bank[:].rearrange("p (k n) -> p k n", k=4)
            cp("act", dst, src, [rb], [rhT[t]])

    def layer_norm(l, s, last):
        lnt = qk[:, 0:2, :].bitcast(F32)
        p.dma("sp", lnt, wd[l]["ln"], writes=[rqk[0], rqk[1]])
        rl = [rqk[0], rqk[1]]
        for t in range(NT):
            for hf in range(2):
                p.op("dve", lambda e, t=t, hf=hf: e.bn_stats(out=st12[:, t, hf * 6:(hf + 1) * 6], in_=h32[:, t, hf * 512:(hf + 1) * 512]),
                     [rh[t]], [r_st12])
            p.op("dve", lambda e, t=t: e.bn_aggr(out=mvall[:, t, :], in_=st12[:, t, :]), [r_st12], [r_mv])
        act(rstd[:], mvall[:, :, 1], AF.Sqrt, [r_mv, r_eps], [r_rstd], bias=epst[:])
        p.op("dve", lambda e: e.reciprocal(out=rstd[:], in_=rstd[:]), [r_rstd], [r_rstd])
        stt("dve", nmr[:], mvall[:, :, 0], -1.0, rstd[:], ALU.mult, ALU.mult, [r_mv, r_rstd], [r_nmr])
        for t in range(NT):
            hv = h32[:, t, :]
            ts("dve", hv, hv, rstd[:, t:t + 1], nmr[:, t:t + 1], ALU.mult, ALU.add, [rh[t], r_rstd, r_nmr], [rh[t]])
            tt("pool", hv, hv, lnt[:, 0, :], ALU.mult, [rh[t]] + rl, [rh[t]])
            tt("dve", hv, hv, lnt[:, 1, :], ALU.add, [rh[t]] + rl, [rh[t]])
            if last:
                p.dma("sp", out_d[s].rearrange("(t p) d -> p t d", p=128)[:, t, :], hv, reads=[rh[t]])
            else:
                make_hT(t)

    def odd_layer(l):
        pr = prm[l]
        off = [0]

        def carve(nbytes, dt, shape):
            a = arena[:, off[0] // 4:(off[0] + nbytes) // 4]
            off[0] += nbytes
            if dt == BF16:
                a = a.bitcast(BF16)
            if len(shape) == 3:
                a = a.rearrange("p (a b) -> p a b", a=shape[1])
            return a

        vaug = carve(NT * 264 * 2, BF16, [128, NT, 264]); r_v = Res("vaug")
        sg = carve(NT * 256 * 2, BF16, [128, NT, 256]); r_sg = Res("sg")
        dd = carve(4 * 256 * 4, F32, [128, 4, 256]); r_dd = [Res(f"dd{i}") for i in range(4)]
        pTs = [carve(512 * 2, BF16, [128, 512]) for _ in range(3)]; r_pT = [Res(f"pT{i}") for i in range(3)]
        mixb = [carve(256 * 2, BF16, [128, 256]) for _ in range(2)]; r_mixb = [Res(f"mixb{i}") for i in range(2)]
        mtmp = [carve(256 * 4, F32, [128, 256]) for _ in range(2)]; r_mtmp = [Res(f"mtmp{i}") for i in range(2)]
        mixT = carve(2 * S * 2, BF16, [128, 2, S]); r_mixT = Res("mixT")
        sq = carve(256 * 4, F32, [128, 256]); r_sq = Res("sqj")
        ss = carve(16 * 4, F32, [128, 16]); r_ss = Res("ss")
        rec = carve(16 * 4, F32, [128, 16]); r_rec = Res("rec")
        assert off[0] <= ARENA, off[0]
        p.op("pool", lambda e: e.memset(vaug[:, :, 256:257], 1.0), [], [r_v])
        pT_ctr = [0]
        mix_ctr = [0]
        for h in range(8):
            for c in range(4):
                for G in range(4):
                    bank, rb = proj_feat(slabA, rslabA, c * 128, G)
                    rope(bank, rb, qk[:, c, G * 512:(G + 1) * 512], rqk[c], G)
            load_next("A")
            for t in range(NT):
                bank, rb = proj_tok(slabB, rslabB, 0, 512, t)
                cp("dve", vaug[:, t, 0:256], bank[:, 0:256], [rb], [r_v])
                act(sg[:, t, :], bank[:, 256:512], AF.Silu, [rb], [r_sg])
            load_next("B")
            wo, rwo = load_wout(l, h)
            for G in range(4):
                nk = 4 * G + 4
                for m in range(2):
                    qc, kc_ = m, 2 + m
                    obs = [nxt("o") for _ in range(4)]
                    stb = {}

                    def emit_s(kt):
                        q0 = max(4 * G, kt)
                        o_ = (q0 - 4 * G) * 128
                        bank, rb = nxt("st")
                        mm(bank[:, o_:512], qk[:, kc_, kt * 128:(kt + 1) * 128], qk[:, qc, G * 512 + o_:(G + 1) * 512],
                           True, True, [rqk[kc_], rqk[qc]], [rb])
                        stb[kt] = (bank, rb, q0, o_)

                    emit_s(0)
                    for kt in range(nk):
                        if kt + 1 < nk:
                            emit_s(kt + 1)
                        bank, rb, q0, o_ = stb.pop(kt)
                        i = pT_ctr[0] % 3
                        pT_ctr[0] += 1
                        pT, rp = pTs[i], r_pT[i]
                        act(pT[:, o_:512], bank[:, o_:512], AF.Exp, [rb], [rp], scale=SCALE)
                        if kt >= 4 * G:
                            tt("pool", pT[:, o_:o_ + 128], pT[:, o_:o_ + 128], cmask[:], ALU.mult, [rp, rconst], [rp])
                        for qt in range(q0, 4 * G + 4):
                            qi = qt - 4 * G
                            ob, rob = obs[qi]
                            mm(ob[:, 0:257], pT[:, qi * 128:(qi + 1) * 128], vaug[:, kt, 0:257], kt == 0, kt == qt,
                               [rp, r_v], [rob])
                    for qi in range(4):
                        ob, rob = obs[qi]
                        t = 4 * G + qi
                        col = m * 8 + qi
                        p.op("dve", lambda e, ob=ob, col=col: e.reciprocal(out=rec[:, col:col + 1], in_=ob[:, 256:257]), [rob], [r_rec])
                        if m == 0:
                            ts("dve", dd[:, qi, :], ob[:, 0:256], rec[:, col:col + 1], None, ALU.mult, None, [rob, r_rec], [r_dd[qi]])
                        else:
                            tt("dve", rec[:, col + 4:col + 5], rec[:, col:col + 1], pr["nlam"][:], ALU.mult, [r_rec, pr["r"]], [r_rec])
                            stt("dve", dd[:, qi, :], ob[:, 0:256], rec[:, col + 4:col + 5], dd[:, qi, :], ALU.mult, ALU.add,
                                [rob, r_rec, r_dd[qi]], [r_dd[qi]])
                p.op("dve", lambda e, G=G: e.memset(ss[:, G * 4:G * 4 + 4], 0.0), [], [r_ss])
                for qi in range(4):
                    act(sq[:], dd[:, qi, :], AF.Square, [r_dd[qi]], [r_sq, r_ss], accum=ss[:, G * 4 + qi:G * 4 + qi + 1])
                sv = ss[:, G * 4:G * 4 + 4]
                act(sv, sv, AF.Sqrt, [r_ss, r_eps], [r_ss], bias=epst[:], scale=1.0 / 256.0)
                p.op("dve", lambda e, sv=sv: e.reciprocal(out=sv, in_=sv), [r_ss], [r_ss])
                for qi in range(4):
                    t = 4 * G + qi
                    i = mix_ctr[0] % 2
                    mix_ctr[0] += 1
                    stt("dve", mtmp[i][:], dd[:, qi, :], ss[:, t:t + 1], pr["sub2"][:], ALU.mult, ALU.mult,
                        [r_dd[qi], r_ss, pr["r"]], [r_mtmp[i]])
                    tt("dve", mixb[i][:], mtmp[i][:], sg[:, t, :], ALU.mult, [r_mtmp[i], r_sg], [r_mixb[i]])
                    bank, rb = nxt("proj")
                    bv = bank[:, 0:128].bitcast(BF16)
                    for c in range(2):
                        tr(bv[:, c * 128:(c + 1) * 128], mixb[i][:, c * 128:(c + 1) * 128], identb[:], [r_mixb[i], rconst], [rb])
                    cp("act", mixT[:, :, t * 128:(t + 1) * 128], bv.rearrange("p (c n) -> p c n", c=2), [rb], [r_mixT])
            out_proj([mixT[:, 0, :], mixT[:, 1, :]], [r_mixT], wo, rwo, first=(h == 0))

    def even_layer(l):
        pr = prm[l]
        off = [0]

        def carve(nbytes, dt, shape):
            a = arena[:, off[0] // 4:(off[0] + nbytes) // 4]
            off[0] += nbytes
            if dt == BF16:
                a = a.bitcast(BF16)
            if len(shape) == 3:
                a = a.rearrange("p (a b) -> p a b", a=shape[1])
            return a

        kE = carve(NT * 128 * 2, BF16, [128, NT, 128]); r_kE = Res("kE")
        vv = carve(NT * 128 * 2, BF16, [128, NT, 128]); r_vv = Res("vv")
        sgr = carve(NT * 128 * 2, BF16, [128, NT, 128]); r_sgr = [Res(f"sgr{t}") for t in range(NT)]
        Rbs = [carve(128 * 2, BF16, [128, 128]) for _ in range(3)]; r_Rbs = [Res(f"Rb{i}") for i in range(3)]
        R32 = carve(128 * 4, F32, [128, 128]); r_R32 = Res("R32")
        ro = qk[:, 2:4, :].bitcast(F32).rearrange("p a (b c) -> p (a b) c", c=128)
        r_ro = [Res(f"ro{t}") for t in range(NT)]
        dmk = carve(128 * 4, F32, [128, 128]); r_dmk = Res("dmk")
        scs = [carve(128 * 2, BF16, [128, 128]) for _ in range(3)]; r_sc = [Res(f"sc{i}") for i in range(3)]
        mixT = carve(2 * S * 2, BF16, [128, 2, S]); r_mixR = Res("mixTr"); r_mixL = Res("mixTl")
        gnt = carve(2 * 128 * 4, F32, [128, 2, 128]); r_gnt = Res("gnt")
        gwt = carve(256 * 2, BF16, [128, 256]); r_gwt = Res("gwt")
        st6 = carve(NT * 6 * 4, F32, [128, NT, 6]); r_st6 = Res("st6")
        mv2 = carve(NT * 2 * 4, F32, [128, NT, 2]); r_mv2 = Res("mv2")
        rs2 = carve(NT * 4, F32, [128, NT]); r_rs2 = Res("rs2")
        nm2 = carve(NT * 4, F32, [128, NT]); r_nm2 = Res("nm2")
        xb32 = carve(516 * 4, F32, [128, 516]); r_xb = Res("xb32")
        names = ["xc", "rr", "ig", "aa", "a2", "hh0", "hh1"]
        lt = {n: carve(512 * 4, F32, [128, 512]) for n in names}
        rlt = {n: Res(n) for n in names}
        xcb = carve(512 * 2, BF16, [128, 512]); r_xcb = Res("xcb")
        sgl = carve(512 * 4, F32, [128, 512]); r_sgl = Res("sgl")
        assert off[0] <= ARENA, off[0]
        sc_ctr = [0]
        for u in range(8):
            if STOP <= 0:
                continue
            gC = float((1.0 - 2.0 ** (-5.0 - u)) ** 128)
            p.dma("sp", gnt[:], wd[l]["gn"][:, :, u * 128:(u + 1) * 128], writes=[r_gnt])
            p.dma("pool", gwt[:], wd[l]["gw"][u], writes=[r_gwt])
            p.dma("sp", dmk[:], dmask_d[:, u, :], writes=[r_dmk])
            for c in range(2):
                for G in range(4):
                    bank, rb = proj_feat(slabA, rslabA, c * 128, G)
                    rope(bank, rb, qk[:, c, G * 512:(G + 1) * 512], rqk[c], G)
            if STOP <= 1:
                continue
            for t in range(NT):
                bank, rb = proj_tok(slabA, rslabA, 256, 256, t)
                cp("dve", vv[:, t, :], bank[:, 0:128], [rb], [r_vv])
                act(sgr[:, t, :], bank[:, 128:256], AF.Silu, [rb], [r_sgr[t]])
            load_next("A")
            if STOP <= 2:
                continue
            for t4 in range(4):
                bank, rb = nxt("proj")
                bv = bank[:, 0:256].bitcast(BF16)
                for k in range(4):
                    t = t4 * 4 + k
                    tr(bv[:, k * 128:(k + 1) * 128], qk[:, 1, t * 128:(t + 1) * 128], identb[:], [rqk[1], rconst], [rb])
                ts("dve", kE[:, t4 * 4:t4 * 4 + 4, :], bv.rearrange("p (k n) -> p k n", k=4), kend[:, u:u + 1], None, ALU.mult, None,
                   [rb, rconst], [r_kE])
            if STOP <= 3:
                continue
            stbs = {}

            def emit_a(t):
                bank, rb = nxt("st")
                mm(bank[:, 0:128], qk[:, 1, t * 128:(t + 1) * 128], qk[:, 0, t * 128:(t + 1) * 128], True, True, [rqk[0], rqk[1]], [rb])
                mm(bank[:, 128:256], kE[:, t, :], vv[:, t, :], True, True, [r_kE, r_vv], [rb])
                stbs[t] = (bank, rb)

            emit_a(0)
            for t in range(NT):
                if t + 1 < NT:
                    emit_a(t + 1)
                bank, rb = stbs.pop(t)
                i = sc_ctr[0] % 3
                sc_ctr[0] += 1
                sc, rsc = scs[i], r_sc[i]
                tt("dve", sc[:], bank[:, 0:128], dmk[:], ALU.mult, [rb, r_dmk], [rsc])
                if t + 1 < NT:
                    if t == 0:
                        cp("act", R32[:], bank[:, 128:256], [rb], [r_R32])
                    else:
                        stt("dve", R32[:], R32[:], gC, bank[:, 128:256], ALU.mult, ALU.add, [rb, r_R32], [r_R32])
                    cp("act", Rbs[(t + 1) % 3][:], R32[:], [r_R32], [r_Rbs[(t + 1) % 3]])
                ob, rob = nxt("o")
                mm(ob[:, 0:128], sc[:], vv[:, t, :], True, True, [rsc, r_vv], [rob])
                if t > 0:
                    mm(ob[:, 128:256], qk[:, 0, t * 128:(t + 1) * 128], Rbs[t % 3][:], True, True, [rqk[0], r_Rbs[t % 3]], [rob])
                cp("act", ro[:, t, :], ob[:, 0:128], [rob], [r_ro[t]])
                if t > 0:
                    stt("dve", ro[:, t, :], ob[:, 128:256], gpow[:, u:u + 1], ro[:, t, :], ALU.mult, ALU.add,
                        [rob, r_ro[t], rconst], [r_ro[t]])
                p.op("dve", lambda e, t=t: e.bn_stats(out=st6[:, t, :], in_=ro[:, t, :]), [r_ro[t]], [r_st6])
                p.op("dve", lambda e, t=t: e.bn_aggr(out=mv2[:, t, :], in_=st6[:, t, :]), [r_st6], [r_mv2])
            if STOP <= 4:
                continue
            act(rs2[:], mv2[:, :, 1], AF.Sqrt, [r_mv2, r_eps], [r_rs2], bias=epst[:])
            p.op("dve", lambda e: e.reciprocal(out=rs2[:], in_=rs2[:]), [r_rs2], [r_rs2])
            stt("dve", nm2[:], mv2[:, :, 0], -1.0, rs2[:], ALU.mult, ALU.mult, [r_mv2, r_rs2], [r_nm2])
            for t in range(NT):
                rv = ro[:, t, :]
                ts("dve", rv, rv, rs2[:, t:t + 1], nm2[:, t:t + 1], ALU.mult, ALU.add, [r_ro[t], r_rs2, r_nm2], [r_ro[t]])
                tt("pool", rv, rv, gnt[:, 0, :], ALU.mult, [r_ro[t], r_gnt], [r_ro[t]])
                tt("pool", rv, rv, gnt[:, 1, :], ALU.add, [r_ro[t], r_gnt], [r_ro[t]])
                tt("dve", sgr[:, t, :], rv, sgr[:, t, :], ALU.mult, [r_ro[t], r_sgr[t]], [r_sgr[t]])
            for t4 in range(4):
                bank, rb = nxt("proj")
                bv = bank[:, 0:256].bitcast(BF16)
                for k in range(4):
                    t = t4 * 4 + k
                    tr(bv[:, k * 128:(k + 1) * 128], sgr[:, t, :], identb[:], [r_sgr[t], rconst], [rb])
                cp("act", mixT[:, 0, t4 * 512:(t4 + 1) * 512], bv, [rb], [r_mixR])
            if STOP <= 5:
                continue
            L = pr["lrup"]
            p.op("dve", lambda e: e.memset(xb32[:, 0:3], 0.0), [], [r_xb])
            for G in range(4):
                hh, r_hh = lt[f"hh{G % 2}"], rlt[f"hh{G % 2}"]
                hp, r_hp = lt[f"hh{(G + 1) % 2}"], rlt[f"hh{(G + 1) % 2}"]
                if G > 0:
                    cp("dve", xb32[:, 0:3], xb32[:, 512:515], [r_xb], [r_xb])
                bank, rb = proj_feat(slabB, rslabB, 0, G)
                cp("act", xb32[:, 3:515], bank[:], [rb], [r_xb])
                bank, rb = proj_feat(slabB, rslabB, 128, G)
                act(sgl[:], bank[:], AF.Silu, [rb], [r_sgl])
                xc = lt["xc"]
                ts("dve", xc[:], xb32[:, 3:515], L[:, u, 3:4], L[:, u, 4:5], ALU.mult, ALU.add, [r_xb, pr["r"]], [rlt["xc"]])
                for w in (2, 1, 0):
                    stt("dve", xc[:], xb32[:, w:w + 512], L[:, u, w:w + 1], xc[:], ALU.mult, ALU.add, [r_xb, pr["r"], rlt["xc"]], [rlt["xc"]])
                cp("pool", xcb[:], xc[:], [rlt["xc"]], [r_xcb])
                bank, rb = nxt("proj")
                mm(bank[:], gwt[:, 0:128], xcb[:], True, True, [r_gwt, r_xcb], [rb])
                act(lt["rr"][:], bank[:], AF.Sigmoid, [rb, pr["r"]], [rlt["rr"]], bias=L[:, u, 5:6])
                bank, rb = nxt("proj")
                mm(bank[:], gwt[:, 128:256], xcb[:], True, True, [r_gwt, r_xcb], [rb])
                act(lt["ig"][:], bank[:], AF.Sigmoid, [rb, pr["r"]], [rlt["ig"]], bias=L[:, u, 6:7])
                act(lt["aa"][:], lt["rr"][:], AF.Exp, [rlt["rr"], pr["r"]], [rlt["aa"]], scale=pr["c"][:, u:u + 1])
                act(lt["a2"][:], lt["rr"][:], AF.Exp, [rlt["rr"], pr["r"]], [rlt["a2"]], scale=pr["c2"][:, u:u + 1])
                act(lt["rr"][:], lt["rr"][:], AF.Tanh, [rlt["rr"], pr["r"]], [rlt["rr"]], scale=pr["cn"][:, u:u + 1])
                stt("dve", lt["a2"][:], lt["a2"][:], 1.0, lt["rr"][:], ALU.add, ALU.mult, [rlt["a2"], rlt["rr"]], [rlt["a2"]])
                act(lt["a2"][:], lt["a2"][:], AF.Sqrt, [rlt["a2"]], [rlt["a2"]])
                tt("pool", lt["ig"][:], lt["ig"][:], xc[:], ALU.mult, [rlt["ig"], rlt["xc"]], [rlt["ig"]])
                tt("dve", lt["ig"][:], lt["ig"][:], lt["a2"][:], ALU.mult, [rlt["ig"], rlt["a2"]], [rlt["ig"]])
                init = 0.0 if G == 0 else hp[:, 511:512]
                p.op("dve", lambda e, hh=hh, init=init: e.tensor_tensor_scan(out=hh[:], data0=lt["aa"][:], data1=lt["ig"][:], initial=init,
                                                                              op0=ALU.mult, op1=ALU.add),
                     [rlt["aa"], rlt["ig"], r_hp], [r_hh])
                tt("pool", mixT[:, 1, G * 512:(G + 1) * 512], hh[:], sgl[:], ALU.mult, [r_hh, r_sgl], [r_mixL])
            load_next("B")
            if STOP <= 6:
                continue
            wo, rwo = load_wout(l, u)
            out_proj([mixT[:, 0, :], mixT[:, 1, :]], [r_mixR, r_mixL], wo, rwo, first=(u == 0))

    load_next("A")
    load_next("B")
    for s in range(nseq):
        xv = x_d[s].rearrange("(t p) d -> p t d", p=128)
        for t in range(NT):
            p.dma("sp", h32[:, t, :], xv[:, t, :], writes=[rh[t]])
        for t in range(NT):
            make_hT(t)
        if not layers:
            for t in range(NT):
                p.dma("sp", out_d[s].rearrange("(t p) d -> p t d", p=128)[:, t, :], h32[:, t, :], reads=[rh[t]])
        for li_, l in enumerate(layers):
            p.barrier()
            if l % 2 == 0:
                even_layer(l)
            else:
                odd_layer(l)
            if STOP <= -1:
                for t in range(NT):
                    p.dma("sp", out_d[s].rearrange("(t p) d -> p t d", p=128)[:, t, :], h32[:, t, :], reads=[rh[t]])
                continue
            layer_norm(l, s, last=(li_ == len(layers) - 1))
    p.wait_all_dma("sp", rh)
    p.emit()
    return nc, p.stats


def host_constants():
    half = 64
    inv_freq = (10000.0 ** (-np.arange(half, dtype=np.float32) / half)).astype(np.float32)
    pos = np.arange(S, dtype=np.float32)
    ang = pos[None, :] * inv_freq[:, None]
    cos = np.cos(ang).astype(np.float32)
    sin = np.sin(ang).astype(np.float32)
    cosT = np.concatenate([cos, cos], axis=0)
    sinS = np.concatenate([-sin, sin], axis=0)
    identf = np.eye(128, dtype=np.float32)
    j = np.arange(128)[:, None]
    i = np.arange(128)[None, :]
    cmask = (i >= j).astype(np.float32)
    g = (1.0 - 2.0 ** (-5.0 - np.arange(8, dtype=np.float64)))
    dmask = np.zeros((128, 8, 128), np.float64)
    for h in range(8):
        dmask[:, h, :] = np.where(i >= j, g[h] ** np.maximum(i - j, 0), 0.0) * SCALE
    gpow = g[None, :] ** (np.arange(128)[:, None] + 1.0)
    kend = (g[None, :] ** (127.0 - np.arange(128)[:, None])) * SCALE
    return dict(cosT=np.ascontiguousarray(cosT), sinS=np.ascontiguousarray(sinS), identf=identf, cmask=cmask,
                dmask=dmask.astype(np.float32), gpow=gpow.astype(np.float32), kend=kend.astype(np.float32))


def rep(a):
    return np.ascontiguousarray(np.broadcast_to(a, (128,) + a.shape))


def host_weights(inp, layers):
    m = {}
    for l in layers:
        j = l // 2
        if l % 2 == 0:
            w = inp["ev_w_in"][j]
            cols = []
            for u in range(8):
                cols.append(np.concatenate([w[:, k * 1024 + u * 128:k * 1024 + (u + 1) * 128] for k in range(6)], axis=1))
            m[f"w_in{l}"] = np.ascontiguousarray(np.stack(cols, 0))
            wo = inp["ev_w_out"][j]
            m[f"w_out{l}"] = np.ascontiguousarray(np.stack(
                [np.concatenate([wo[u * 128:(u + 1) * 128], wo[1024 + u * 128:1024 + (u + 1) * 128]], 0) for u in range(8)], 0))
            m[f"gw{l}"] = np.ascontiguousarray(np.concatenate([inp["ev_gate_a_w"][j], inp["ev_gate_x_w"][j]], axis=2))
            lr = np.stack([inp["ev_conv_w"][j][0], inp["ev_conv_w"][j][1], inp["ev_conv_w"][j][2], inp["ev_conv_w"][j][3],
                           inp["ev_conv_b"][j], inp["ev_gate_a_b"][j], inp["ev_gate_x_b"][j], inp["ev_lru_lambda"][j]], axis=1)
            m[f"lrup{l}"] = np.ascontiguousarray(lr.reshape(8, 128, 8).transpose(1, 0, 2))
            m[f"gn{l}"] = rep(np.stack([inp["ev_ret_gn_g"][j], inp["ev_ret_gn_b"][j]], 0))
            m[f"ln{l}"] = rep(np.stack([inp["ev_ln_g"][j], inp["ev_ln_b"][j]], 0))
        else:
            w = inp["od_w_in"][j]
            cols = []
            for h in range(8):
                q1 = w[:, (2 * h) * 128:(2 * h + 1) * 128]
                q2 = w[:, (2 * h + 1) * 128:(2 * h + 2) * 128]
                k1 = w[:, 2048 + (2 * h) * 128:2048 + (2 * h + 1) * 128]
                k2 = w[:, 2048 + (2 * h + 1) * 128:2048 + (2 * h + 2) * 128]
                v = w[:, 4096 + h * 256:4096 + (h + 1) * 256]
                gg = w[:, 6144 + h * 256:6144 + (h + 1) * 256]
                cols.append(np.concatenate([q1, q2, k1, k2, v, gg], axis=1))
            m[f"w_in{l}"] = np.ascontiguousarray(np.stack(cols, 0))
            m[f"w_out{l}"] = np.ascontiguousarray(inp["od_w_out"][j].reshape(8, 256, 1024))
            m[f"lam{l}"] = rep(np.stack([inp["od_lambda_q1"][j], inp["od_lambda_k1"][j], inp["od_lambda_q2"][j], inp["od_lambda_k2"][j]], 0))
            m[f"sub{l}"] = rep(inp["od_subln_g"][j])
            m[f"ln{l}"] = rep(np.stack([inp["od_ln_g"][j], inp["od_ln_b"][j]], 0))
    return m


_PROGS = {}


def run_layers(x, inp, layers, ncores=NCORES, trace=False):
    B = x.shape[0]
    nseq = B // ncores
    key = (tuple(layers), nseq)
    if key not in _PROGS:
        _PROGS[key] = build_program(layers, nseq)
    nc, stats = _PROGS[key]
    consts = host_constants()
    wts = host_weights(inp, layers)
    in_maps = []
    for c in range(ncores):
        mdict = dict(consts)
        mdict.update(wts)
        mdict["x"] = np.ascontiguousarray(x[c * nseq:(c + 1) * nseq])
        in_maps.append(mdict)
    res = run_bass_kernel_spmd(nc, in_maps, core_ids=list(range(ncores)))
    return np.concatenate([r["out"] for r in res.results], axis=0)


FUSED = True
import os
STOP = int(os.environ.get("KSTOP", "99"))


def kernel(**inputs):
    inp = {k: np.asarray(v) for k, v in inputs.items()}
    x = np.ascontiguousarray(inp["x"], dtype=np.float32)
    if FUSED:
        return run_layers(x, inp, [0, 1, 2, 3]).astype(np.float32)
    h = x
    for l in range(DEPTH):
        h = run_layers(h, inp, [l])
    return h.astype(np.float32)
```

```python
import math
import numpy as np
from contextlib import ExitStack
import concourse.bass as bass
import concourse.mybir as mybir
from concourse.bass_utils import run_bass_kernel_spmd

F32 = mybir.dt.float32
BF16 = mybir.dt.bfloat16
ALU = mybir.AluOpType
AF = mybir.ActivationFunctionType
AX = mybir.AxisListType

NCORES = 8
S = 2048
D = 1024
NT = 16
DEPTH = 4
EPS = 1e-5
ALPHA = (2.0 * DEPTH) ** 0.25
SCALE = 128.0 ** -0.5

ENG = ("pe", "act", "dve", "pool", "sp")
EPOCH = 30000


class Res:
    __slots__ = ("name", "w", "r", "sem", "excl")

    def __init__(self, name, excl=False):
        self.excl = excl
        self.name = name
        self.w = None
        self.r = {}
        self.sem = None


class Prog:
    def __init__(self, nc):
        self.nc = nc
        self.ops = {e: [] for e in ENG}
        self.waited = {e: {} for e in ENG}
        self.dma_cnt = []
        self.stack = ExitStack()
        self.n_sb = 0

    def sb(self, shape, dt, name=None):
        self.n_sb += 1
        return self.stack.enter_context(self.nc.sbuf_tensor("S_" + (name or f"sb{self.n_sb}"), list(shape), dt))

    def ps(self, shape, dt, name=None):
        self.n_sb += 1
        return self.stack.enter_context(self.nc.psum_tensor("P_" + (name or f"ps{self.n_sb}"), list(shape), dt))

    def _need(self, eng, dep, waits):
        if dep is None:
            return
        key = (dep[0], dep[1])
        val = dep[2]
        if self.waited[eng].get(key, -1) >= val:
            return
        if waits.get(key, -1) >= val:
            return
        waits[key] = val

    def op(self, eng, fn, reads=(), writes=()):
        ex = [r for r in reads if r.excl]
        if ex:
            reads = [r for r in reads if not r.excl]
            writes = list(writes) + [r for r in ex if r not in writes]
        idx = len(self.ops[eng])
        waits = {}
        for r in reads:
            d = r.w
            if d is None:
                continue
            if d[0] == "op" and d[1] == eng and eng in ("pe", "sp"):
                continue
            self._need(eng, d, waits)
        for w in writes:
            for d in ([w.w] if w.w is not None else []) + list(w.r.values()):
                if d[0] == "op" and d[1] == eng:
                    continue
                self._need(eng, d, waits)
        for k, v in waits.items():
            self.waited[eng][k] = v
        me = ("op", eng, idx)
        for r in reads:
            r.r[("op", eng)] = me
        for w in writes:
            w.w = me
            w.r = {}
        self.ops[eng].append(dict(fn=fn, waits=waits, dma=None, inc=False, ph=getattr(self, "phase", "")))
        return me

    def dma(self, eng, out, in_, reads=(), writes=(), sem_res=None):
        if sem_res is None:
            sem_res = writes[0] if writes else reads[0]
        if sem_res.sem is None:
            sem_res.sem = len(self.dma_cnt)
            self.dma_cnt.append(0)
        sid = sem_res.sem
        waits = {}
        for r in reads:
            self._need(eng, r.w, waits)
        for w in writes:
            for d in ([w.w] if w.w is not None else []) + list(w.r.values()):
                self._need(eng, d, waits)
        for k, v in waits.items():
            self.waited[eng][k] = v
        self.dma_cnt[sid] += 16
        me = ("dma", sid, self.dma_cnt[sid])
        for r in reads:
            r.r[("dma", sid)] = me
        for w in writes:
            w.w = me
            w.r = {}
        self.ops[eng].append(dict(fn=lambda e, o=out, i=in_: e.dma_start(out=o, in_=i), waits=waits, dma=sid, inc=False))
        return me

    def barrier(self, engines=("pe", "act", "dve", "pool")):
        last = {}
        for e in engines:
            i = len(self.ops[e]) - 1
            while i >= 0 and (self.ops[e][i]["fn"] is None or self.ops[e][i]["dma"] is not None):
                i -= 1
            if i >= 0:
                last[e] = i
        for e in tuple(engines) + ("sp",):
            waits = {}
            for e2, i2 in last.items():
                if e2 != e:
                    self._need(e, ("op", e2, i2), waits)
            for k, v in waits.items():
                self.waited[e][k] = v
            self.ops[e].append(dict(fn=None, waits=waits, dma=None, inc=False))

    def wait_all_dma(self, eng, resources):
        waits = {}
        for r in resources:
            for d in ([r.w] if r.w is not None else []) + list(r.r.values()):
                if d[0] == "dma":
                    self._need(eng, d, waits)
        for k, v in waits.items():
            self.waited[eng][k] = v
        self.ops[eng].append(dict(fn=None, waits=waits, dma=None, inc=False))

    def emit(self):
        nc = self.nc
        for e in ENG:
            for o in self.ops[e]:
                for (kind, src), val in o["waits"].items():
                    if kind == "op":
                        self.ops[src][val]["inc"] = True
        semval = {e: {} for e in ENG}
        nsem = {}
        for e in ENG:
            c = 0
            for i, o in enumerate(self.ops[e]):
                if o["inc"]:
                    semval[e][i] = (c // EPOCH, c % EPOCH + 1)
                    c += 1
            nsem[e] = (c + EPOCH - 1) // EPOCH
        st = self.stack
        esems = {e: [st.enter_context(nc.semaphore(f"s_{e}{k}")) for k in range(nsem[e])] for e in ENG}
        dsems = [st.enter_context(nc.semaphore(f"d{k}")) for k in range(len(self.dma_cnt))]
        self.stats = {e: (len(self.ops[e]), sum(len(o["waits"]) for o in self.ops[e])) for e in ENG}
        block = st.enter_context(nc.Block())

        def runner(e):
            def run(eng):
                for i, o in enumerate(self.ops[e]):
                    for (kind, src), val in o["waits"].items():
                        if kind == "op":
                            ep, v = semval[src][val]
                            eng.wait_ge(esems[src][ep], v)
                        else:
                            eng.wait_ge(dsems[src], val)
                    if o["fn"] is None:
                        continue
                    ins = o["fn"](eng)
                    if o["dma"] is not None:
                        ins.then_inc(dsems[o["dma"]], 16)
                    elif o["inc"]:
                        ep, v = semval[e][i]
                        ins.then_inc(esems[e][ep], 1)
            return run

        block.tensor(runner("pe"))
        block.scalar(runner("act"))
        block.vector(runner("dve"))
        block.gpsimd(runner("pool"))
        block.sync(runner("sp"))
        st.close()


def lambda_init(l):
    return 0.8 - 0.6 * math.exp(-0.3 * l)


def build_program(layers, nseq, first_is_input=True):
    nc = bass.Bass("TRN2", target_bir_lowering=False)
    p = Prog(nc)

    def din(name, shape):
        return nc.dram_tensor(name, list(shape), F32, kind="ExternalInput").ap()

    x_d = din("x", [nseq, S, D])
    out_d = nc.dram_tensor("out", [nseq, S, D], F32, kind="ExternalOutput").ap()
    cos_d = din("cosT", [128, S])
    sin_d = din("sinS", [128, S])
    identf_d = din("identf", [128, 128])
    cmask_d = din("cmask", [128, 128])
    dmask_d = din("dmask", [128, 8, 128])
    gpow_d = din("gpow", [128, 8])
    kend_d = din("kend", [128, 8])
    wd = {}
    for l in layers:
        if l % 2 == 0:
            wd[l] = dict(w_in=din(f"w_in{l}", [8, D, 768]), w_out=din(f"w_out{l}", [8, 256, D]),
                         gw=din(f"gw{l}", [8, 128, 256]), lrup=din(f"lrup{l}", [128, 8, 8]),
                         gn=din(f"gn{l}", [128, 2, D]), ln=din(f"ln{l}", [128, 2, D]))
        else:
            wd[l] = dict(w_in=din(f"w_in{l}", [8, D, 1024]), w_out=din(f"w_out{l}", [8, 256, D]),
                         lam=din(f"lam{l}", [128, 4, 128]), sub=din(f"sub{l}", [128, 256]),
                         ln=din(f"ln{l}", [128, 2, D]))

    h32 = p.sb([128, NT, D], F32, "h32")
    rh = [Res(f"h32_{t}") for t in range(NT)]
    hT = p.sb([128, 8, S], BF16, "hT")
    rhT = [Res(f"hT_{t}") for t in range(NT)]
    slabA = p.sb([128, 8, 512], BF16, "slabA"); rslabA = Res("slabA")
    slabB = p.sb([128, 8, 512], BF16, "slabB"); rslabB = Res("slabB")
    wout = [p.sb([128, 2, D], BF16, f"wout{i}") for i in range(1)]
    rwout = [Res(f"wout{i}") for i in range(1)]
    qk = p.sb([128, 4, S], BF16, "qk")
    rqk = [Res(f"qk{i}") for i in range(4)]
    cosT = p.sb([128, S], F32, "cosT"); sinS = p.sb([128, S], F32, "sinS")
    rconst = Res("const")
    identf = p.sb([128, 128], F32, "identf")
    identb = p.sb([128, 128], BF16, "identb")
    cmask = p.sb([128, 128], BF16, "cmask")
    gpow = p.sb([128, 8], F32, "gpow")
    kend = p.sb([128, 8], F32, "kend")
    epst = p.sb([128, 1], F32, "eps")
    onet = p.sb([128, 1], F32, "one")
    rxs = [p.sb([128, 512], F32, f"rxs{i}") for i in range(2)]; r_rxs = [Res(f"rxs{i}") for i in range(2)]
    rtt = [p.sb([128, 512], F32, f"rtt{i}") for i in range(2)]; r_rtt = [Res(f"rtt{i}") for i in range(2)]
    ARENA = 44800
    arena = p.sb([128, ARENA // 4], F32, "arena")
    st12 = p.sb([128, NT, 12], F32, "st12"); r_st12 = Res("st12")
    mvall = p.sb([128, NT, 2], F32, "mvall"); r_mv = Res("mvall")
    rstd = p.sb([128, NT], F32, "rstd"); r_rstd = Res("rstd")
    nmr = p.sb([128, NT], F32, "nmr"); r_nmr = Res("nmr")
    prm = {}
    for l in layers:
        if l % 2 == 0:
            prm[l] = dict(lrup=p.sb([128, 8, 8], F32, f"lrup{l}"), c=p.sb([128, 8], F32, f"c{l}"),
                          c2=p.sb([128, 8], F32, f"c2{l}"), cn=p.sb([128, 8], F32, f"cn{l}"), r=Res(f"prm{l}"))
        else:
            prm[l] = dict(lamt=rxs[0][:].rearrange("p (a b) -> p a b", a=4), nlam=p.sb([128, 1], F32, f"nlam{l}"),
                          sub2=p.sb([128, 256], F32, f"sub2{l}"), tmp=rtt[0][:, 0:256].rearrange("p (a b) -> p a b", a=2),
                          s12=p.sb([128, 2], F32, f"s12{l}"), r=Res(f"prm{l}"))

    banks = [p.ps([128, 512], F32, f"bank{i}") for i in range(8)]
    rbank = [Res(f"bank{i}", excl=True) for i in range(8)]
    grp = {"proj": [0, 1, 2, 3], "st": [0, 1, 2], "o": [4, 5, 6, 7]}
    gcnt = {k: 0 for k in grp}

    def nxt(g):
        i = grp[g][gcnt[g] % len(grp[g])]
        gcnt[g] += 1
        return banks[i], rbank[i]

    def mm(out, lhsT, rhs, start, stop, reads, writes):
        p.op("pe", lambda e: e.matmul(out, lhsT=lhsT, rhs=rhs, start=start, stop=stop), reads, writes)

    def tr(out, in_, ident, reads, writes):
        p.op("pe", lambda e: e.transpose(out=out, in_=in_, identity=ident), reads, writes)

    def act(out, in_, func, reads, writes, bias=None, scale=1.0, accum=None):
        kw = {}
        if bias is not None:
            kw["bias"] = bias
        if accum is not None:
            kw["accum_out"] = accum
        p.op("act", lambda e: e.activation(out=out, in_=in_, func=func, scale=scale, **kw), reads, writes)

    def tt(eng, out, in0, in1, op, reads, writes):
        p.op(eng, lambda e: e.tensor_tensor(out=out, in0=in0, in1=in1, op=op), reads, writes)

    def ts(eng, out, in0, s1, s2, op0, op1, reads, writes):
        if s2 is None:
            p.op(eng, lambda e: e.tensor_single_scalar(out=out, in_=in0, scalar=s1, op=op0), reads, writes)
        else:
            p.op(eng, lambda e: e.tensor_scalar(out=out, in0=in0, scalar1=s1, scalar2=s2, op0=op0, op1=op1), reads, writes)

    def stt(eng, out, in0, scalar, in1, op0, op1, reads, writes):
        p.op(eng, lambda e: e.scalar_tensor_tensor(out=out, in0=in0, scalar=scalar, in1=in1, op0=op0, op1=op1), reads, writes)

    def cp(eng, out, in_, reads, writes):
        if eng == "act":
            p.op("act", lambda e: e.copy(out=out, in_=in_), reads, writes)
        else:
            p.op(eng, lambda e: e.tensor_copy(out=out, in_=in_), reads, writes)

    p.dma("sp", cosT[:], cos_d, writes=[rconst])
    p.dma("sp", sinS[:], sin_d, writes=[rconst])
    p.dma("sp", identf[:], identf_d, writes=[rconst])
    p.dma("pool", identb[:], identf_d, writes=[rconst])
    p.dma("pool", cmask[:], cmask_d, writes=[rconst])
    p.dma("sp", gpow[:], gpow_d, writes=[rconst])
    p.dma("sp", kend[:], kend_d, writes=[rconst])
    r_eps = Res("eps")
    p.op("dve", lambda e: e.memset(epst[:], EPS), writes=[r_eps])
    p.op("dve", lambda e: e.memset(onet[:], 1.0), writes=[r_eps])
    for l in layers:
        pr = prm[l]
        if STOP <= -2:
            continue
        if l % 2 == 0:
            p.dma("sp", pr["lrup"][:], wd[l]["lrup"], writes=[pr["r"]])
            act(pr["c"][:], pr["lrup"][:, :, 7], AF.Exp, [pr["r"]], [pr["r"]], scale=-1.0)
            act(pr["c"][:], pr["c"][:], AF.Ln, [pr["r"], r_eps], [pr["r"]], bias=onet[:])
            ts("dve", pr["cn"][:], pr["c"][:], 8.0, None, ALU.mult, None, [pr["r"]], [pr["r"]])
            ts("dve", pr["c"][:], pr["cn"][:], -1.0, None, ALU.mult, None, [pr["r"]], [pr["r"]])
            ts("dve", pr["c2"][:], pr["c"][:], 2.0, None, ALU.mult, None, [pr["r"]], [pr["r"]])
        else:
            li = lambda_init(l)
            p.dma("sp", pr["lamt"], wd[l]["lam"], writes=[r_rxs[0]])
            p.dma("sp", pr["sub2"][:], wd[l]["sub"], writes=[pr["r"]])
            tt("dve", pr["tmp"][:, 0, :], pr["lamt"][:, 0, :], pr["lamt"][:, 1, :], ALU.mult, [r_rxs[0]], [r_rtt[0]])
            tt("dve", pr["tmp"][:, 1, :], pr["lamt"][:, 2, :], pr["lamt"][:, 3, :], ALU.mult, [r_rxs[0]], [r_rtt[0]])
            p.op("dve", lambda e, pr=pr: e.reduce_sum(out=pr["s12"][:], in_=pr["tmp"], axis=AX.X), [r_rtt[0]], [pr["r"]])
            act(pr["s12"][:], pr["s12"][:], AF.Exp, [pr["r"]], [pr["r"]])
            tt("dve", pr["nlam"][:], pr["s12"][:, 1:2], pr["s12"][:, 0:1], ALU.subtract, [pr["r"]], [pr["r"]])
            ts("dve", pr["nlam"][:], pr["nlam"][:], -li, None, ALU.add, None, [pr["r"]], [pr["r"]])
            ts("dve", pr["sub2"][:], pr["sub2"][:], 1.0 - li, None, ALU.mult, None, [pr["r"]], [pr["r"]])

    sched = []
    for s_ in range(nseq):
        for l_ in layers:
            for u_ in range(8):
                sched.append((l_, u_))
    pos = {"A": 0, "B": 0}

    def load_next(which):
        i = pos[which]
        pos[which] += 1
        if i >= len(sched):
            return
        l, u = sched[i]
        src = wd[l]["w_in"][u].rearrange("(kc p) c -> p kc c", p=128)
        if which == "A":
            dst, rdst, c0, w = slabA, rslabA, 0, 512
        else:
            dst, rdst, c0, w = slabB, rslabB, 512, (512 if l % 2 == 1 else 256)
        for kc0 in range(0, 8, 4):
            p.dma("pool", dst[:, kc0:kc0 + 4, 0:w], src[:, kc0:kc0 + 4, c0:c0 + w], writes=[rdst])

    def load_wout(l, u):
        src = wd[l]["w_out"][u].rearrange("(c p) n -> p c n", p=128)
        p.dma("pool", wout[0][:], src, writes=[rwout[0]])
        return wout[0], rwout[0]

    rope_ctr = [0]

    def rope(bank, rb, dst, rdst, G):
        i = rope_ctr[0] % 2
        rope_ctr[0] += 1
        xs, r1 = rxs[i], r_rxs[i]
        t1, r2 = rtt[i], r_rtt[i]
        cs = cosT[:, G * 512:(G + 1) * 512]
        sn = sinS[:, G * 512:(G + 1) * 512]
        p.op("act", lambda e: e.copy(out=xs[0:64, :], in_=bank[64:128, :]), [rb], [r1])
        p.op("act", lambda e: e.copy(out=xs[64:128, :], in_=bank[0:64, :]), [rb], [r1])
        tt("dve", t1[:], bank[:], cs, ALU.mult, [rb, rconst], [r2])
        tt("pool", xs[:], xs[:], sn, ALU.mult, [r1, rconst], [r1])
        tt("dve", dst, t1[:], xs[:], ALU.add, [r1, r2], [rdst])

    def proj_feat(sl, rsl, c0, G):
        bank, rb = nxt("proj")
        for kc in range(8):
            mm(bank[:], sl[:, kc, c0:c0 + 128], hT[:, kc, G * 512:(G + 1) * 512], kc == 0, kc == 7,
               [rsl] + rhT[4 * G:4 * G + 4], [rb])
        return bank, rb

    def proj_tok(sl, rsl, c0, width, t):
        bank, rb = nxt("proj")
        for kc in range(8):
            mm(bank[:, 0:width], hT[:, kc, t * 128:(t + 1) * 128], sl[:, kc, c0:c0 + width], kc == 0, kc == 7,
               [rsl, rhT[t]], [rb])
        return bank, rb

    def out_proj(mixT_views, rmix, wo, rwo, scale=None, rscale=None):
        k = 0
        for t in range(NT):
            for cg in range(2):
                bank, rb = nxt("proj")
                for c in range(2):
                    mm(bank[:], mixT_views[c][:, t * 128:(t + 1) * 128], wo[:, c, cg * 512:(cg + 1) * 512],
                       c == 0, c == 1, [rwo] + rmix, [rb])
                hv = h32[:, t, cg * 512:(cg + 1) * 512]
                extra = [rscale] if rscale is not None else []
                if k % 2 == 0:
                    if scale is None:
                        tt("dve", hv, hv, bank[:], ALU.add, [rb, rh[t]], [rh[t]])
                    else:
                        stt("dve", hv, bank[:], scale[:, t:t + 1], hv, ALU.mult, ALU.add, [rb, rh[t]] + extra, [rh[t]])
                else:
                    i = (k // 2) % 2
                    if scale is None:
                        cp("act", rxs[i][:], bank[:], [rb], [r_rxs[i]])
                    else:
                        act(rxs[i][:], bank[:], AF.Copy, [rb] + extra, [r_rxs[i]], scale=scale[:, t:t + 1])
                    tt("pool", hv, hv, rxs[i][:], ALU.add, [r_rxs[i], rh[t]], [rh[t]])
                k += 1

    def prescale():
        for t in range(NT):
            ts("dve", h32[:, t, :], h32[:, t, :], ALPHA, None, ALU.mult, None, [rh[t]], [rh[t]])

    def make_hT(t):
        for half in range(2):
            bank, rb = nxt("proj")
            for k in range(4):
                kc = half * 4 + k
                tr(bank[:, k * 128:(k + 1) * 128], h32[:, t, kc * 128:(kc + 1) * 128], identf[:], [rh[t], rconst], [rb])
            dst = hT[:, half * 4:half * 4 + 4, t * 128:(t + 1) * 128]
            src = bank[:].rearrange("p (k n) -> p k n", k=4)
            cp("act", dst, src, [rb], [rhT[t]])

    def layer_norm(l, s, last):
        p.phase = "ln"
        lnt = qk[:, 0:2, :].bitcast(F32)
        p.dma("sp", lnt, wd[l]["ln"], writes=[rqk[0], rqk[1]])
        rl = [rqk[0], rqk[1]]
        for t in range(NT):
            for hf in range(2):
                p.op("dve", lambda e, t=t, hf=hf: e.bn_stats(out=st12[:, t, hf * 6:(hf + 1) * 6], in_=h32[:, t, hf * 512:(hf + 1) * 512]),
                     [rh[t]], [r_st12])
            p.op("dve", lambda e, t=t: e.bn_aggr(out=mvall[:, t, :], in_=st12[:, t, :]), [r_st12], [r_mv])
        act(rstd[:], mvall[:, :, 1], AF.Sqrt, [r_mv, r_eps], [r_rstd], bias=epst[:])
        p.op("dve", lambda e: e.reciprocal(out=rstd[:], in_=rstd[:]), [r_rstd], [r_rstd])
        stt("dve", nmr[:], mvall[:, :, 0], -1.0, rstd[:], ALU.mult, ALU.mult, [r_mv, r_rstd], [r_nmr])
        for t in range(NT):
            hv = h32[:, t, :]
            ts("dve", hv, hv, rstd[:, t:t + 1], nmr[:, t:t + 1], ALU.mult, ALU.add, [rh[t], r_rstd, r_nmr], [rh[t]])
            tt("pool", hv, hv, lnt[:, 0, :], ALU.mult, [rh[t]] + rl, [rh[t]])
            tt("dve", hv, hv, lnt[:, 1, :], ALU.add, [rh[t]] + rl, [rh[t]])
            if last:
                p.dma("sp", out_d[s].rearrange("(t p) d -> p t d", p=128)[:, t, :], hv, reads=[rh[t]])
            else:
                make_hT(t)

    def odd_layer(l):
        pr = prm[l]
        off = [0]

        def carve(nbytes, dt, shape):
            a = arena[:, off[0] // 4:(off[0] + nbytes) // 4]
            off[0] += nbytes
            if dt == BF16:
                a = a.bitcast(BF16)
            if len(shape) == 3:
                a = a.rearrange("p (a b) -> p a b", a=shape[1])
            return a

        vaug = carve(NT * 264 * 2, BF16, [128, NT, 264]); r_v = Res("vaug")
        sg = carve(NT * 256 * 2, BF16, [128, NT, 256]); r_sg = Res("sg")
        dd = carve(4 * 256 * 4, F32, [128, 4, 256]); r_dd = [Res(f"dd{i}") for i in range(4)]
        pTs = [carve(512 * 2, BF16, [128, 512]) for _ in range(3)]; r_pT = [Res(f"pT{i}") for i in range(3)]
        mixb = [carve(256 * 2, BF16, [128, 256]) for _ in range(2)]; r_mixb = [Res(f"mixb{i}") for i in range(2)]
        mtmp = [carve(256 * 4, F32, [128, 256]) for _ in range(2)]; r_mtmp = [Res(f"mtmp{i}") for i in range(2)]
        mixT = carve(2 * S * 2, BF16, [128, 2, S]); r_mixT = Res("mixT")
        sq = carve(256 * 4, F32, [128, 256]); r_sq = Res("sqj")
        ss = carve(16 * 4, F32, [128, 16]); r_ss = Res("ss")
        rec = carve(16 * 4, F32, [128, 16]); r_rec = Res("rec")
        assert off[0] <= ARENA, off[0]
        p.op("pool", lambda e: e.memset(vaug[:, :, 256:257], 1.0), [], [r_v])
        pT_ctr = [0]
        mix_ctr = [0]
        for h in range(8):
            p.phase = "odd_qk"
            for c in range(4):
                for G in range(4):
                    bank, rb = proj_feat(slabA, rslabA, c * 128, G)
                    rope(bank, rb, qk[:, c, G * 512:(G + 1) * 512], rqk[c], G)
            load_next("A")
            p.phase = "odd_vg"
            for t in range(NT):
                bank, rb = proj_tok(slabB, rslabB, 0, 512, t)
                cp("dve", vaug[:, t, 0:256], bank[:, 0:256], [rb], [r_v])
                act(sg[:, t, :], bank[:, 256:512], AF.Silu, [rb], [r_sg])
            load_next("B")
            wo, rwo = load_wout(l, h)
            for G in range(4):
                p.phase = "odd_attn"
                nk = 4 * G + 4
                for m in range(2):
                    qc, kc_ = m, 2 + m
                    obs = [nxt("o") for _ in range(4)]
                    stb = {}

                    def emit_s(kt):
                        q0 = max(4 * G, kt)
                        o_ = (q0 - 4 * G) * 128
                        bank, rb = nxt("st")
                        mm(bank[:, o_:512], qk[:, kc_, kt * 128:(kt + 1) * 128], qk[:, qc, G * 512 + o_:(G + 1) * 512],
                           True, True, [rqk[kc_], rqk[qc]], [rb])
                        stb[kt] = (bank, rb, q0, o_)

                    emit_s(0)
                    emit_s(1)
                    for kt in range(nk):
                        if kt + 2 < nk:
                            emit_s(kt + 2)
                        bank, rb, q0, o_ = stb.pop(kt)
                        i = pT_ctr[0] % 3
                        pT_ctr[0] += 1
                        pT, rp = pTs[i], r_pT[i]
                        act(pT[:, o_:512], bank[:, o_:512], AF.Exp, [rb], [rp], scale=SCALE)
                        if kt >= 4 * G:
                            tt("dve", pT[:, o_:o_ + 128], pT[:, o_:o_ + 128], cmask[:], ALU.mult, [rp, rconst], [rp])
                        for qt in range(q0, 4 * G + 4):
                            qi = qt - 4 * G
                            ob, rob = obs[qi]
                            mm(ob[:, 0:257], pT[:, qi * 128:(qi + 1) * 128], vaug[:, kt, 0:257], kt == 0, kt == qt,
                               [rp, r_v], [rob])
                    for qi in range(4):
                        ob, rob = obs[qi]
                        t = 4 * G + qi
                        col = m * 8 + qi
                        p.op("dve", lambda e, ob=ob, col=col: e.reciprocal(out=rec[:, col:col + 1], in_=ob[:, 256:257]), [rob], [r_rec])
                        if m == 0:
                            ts("dve", dd[:, qi, :], ob[:, 0:256], rec[:, col:col + 1], None, ALU.mult, None, [rob, r_rec], [r_dd[qi]])
                        else:
                            tt("dve", rec[:, col + 4:col + 5], rec[:, col:col + 1], pr["nlam"][:], ALU.mult, [r_rec, pr["r"]], [r_rec])
                            stt("dve", dd[:, qi, :], ob[:, 0:256], rec[:, col + 4:col + 5], dd[:, qi, :], ALU.mult, ALU.add,
                                [rob, r_rec, r_dd[qi]], [r_dd[qi]])
                p.phase = "odd_fin"
                for qi in range(4):
                    t = 4 * G + qi
                    i = mix_ctr[0] % 2
                    mix_ctr[0] += 1
                    tt("dve", sq[:], dd[:, qi, :], dd[:, qi, :], ALU.mult, [r_dd[qi]], [r_sq])
                    p.op("dve", lambda e, t=t: e.reduce_sum(out=ss[:, t:t + 1], in_=sq[:], axis=AX.X), [r_sq], [r_ss])
                    tt("dve", mtmp[i][:], dd[:, qi, :], pr["sub2"][:], ALU.mult, [r_dd[qi], pr["r"]], [r_mtmp[i]])
                    tt("dve", mixb[i][:], mtmp[i][:], sg[:, t, :], ALU.mult, [r_mtmp[i], r_sg], [r_mixb[i]])
                    bank, rb = nxt("proj")
                    bv = bank[:, 0:128].bitcast(BF16)
                    for c in range(2):
                        tr(bv[:, c * 128:(c + 1) * 128], mixb[i][:, c * 128:(c + 1) * 128], identb[:], [r_mixb[i], rconst], [rb])
                    cp("act", mixT[:, :, t * 128:(t + 1) * 128], bv.rearrange("p (c n) -> p c n", c=2), [rb], [r_mixT])
            p.phase = "odd_out"
            act(ss[:], ss[:], AF.Sqrt, [r_ss, r_eps], [r_ss], bias=epst[:], scale=1.0 / 256.0)
            p.op("dve", lambda e: e.reciprocal(out=ss[:], in_=ss[:]), [r_ss], [r_ss])
            out_proj([mixT[:, 0, :], mixT[:, 1, :]], [r_mixT], wo, rwo, scale=ss, rscale=r_ss)

    def even_layer(l):
        pr = prm[l]
        off = [0]

        def carve(nbytes, dt, shape):
            a = arena[:, off[0] // 4:(off[0] + nbytes) // 4]
            off[0] += nbytes
            if dt == BF16:
                a = a.bitcast(BF16)
            if len(shape) == 3:
                a = a.rearrange("p (a b) -> p a b", a=shape[1])
            return a

        kE = carve(NT * 128 * 2, BF16, [128, NT, 128]); r_kE = Res("kE")
        vv = carve(NT * 128 * 2, BF16, [128, NT, 128]); r_vv = Res("vv")
        sgr = carve(NT * 128 * 2, BF16, [128, NT, 128]); r_sgr = [Res(f"sgr{t}") for t in range(NT)]
        Rbs = [carve(128 * 2, BF16, [128, 128]) for _ in range(3)]; r_Rbs = [Res(f"Rb{i}") for i in range(3)]
        R32 = carve(128 * 4, F32, [128, 128]); r_R32 = Res("R32")
        ro = qk[:, 2:4, :].bitcast(F32).rearrange("p a (b c) -> p (a b) c", c=128)
        r_ro = [Res(f"ro{t}") for t in range(NT)]
        dmk = carve(128 * 4, F32, [128, 128]); r_dmk = Res("dmk")
        scs = [carve(128 * 2, BF16, [128, 128]) for _ in range(3)]; r_sc = [Res(f"sc{i}") for i in range(3)]
        mixT = carve(2 * S * 2, BF16, [128, 2, S]); r_mixR = Res("mixTr"); r_mixL = Res("mixTl")
        gnt = carve(2 * 128 * 4, F32, [128, 2, 128]); r_gnt = Res("gnt")
        gwt = carve(256 * 2, BF16, [128, 256]); r_gwt = Res("gwt")
        st6 = carve(NT * 6 * 4, F32, [128, NT, 6]); r_st6 = Res("st6")
        mv2 = carve(NT * 2 * 4, F32, [128, NT, 2]); r_mv2 = Res("mv2")
        rs2 = carve(NT * 4, F32, [128, NT]); r_rs2 = Res("rs2")
        nm2 = carve(NT * 4, F32, [128, NT]); r_nm2 = Res("nm2")
        xb32 = carve(516 * 4, F32, [128, 516]); r_xb = Res("xb32")
        names = ["xc", "rr", "ig", "aa", "a2", "hh0", "hh1"]
        lt = {n: carve(512 * 4, F32, [128, 512]) for n in names}
        rlt = {n: Res(n) for n in names}
        xcb = carve(512 * 2, BF16, [128, 512]); r_xcb = Res("xcb")
        sgl = carve(512 * 4, F32, [128, 512]); r_sgl = Res("sgl")
        assert off[0] <= ARENA, off[0]
        sc_ctr = [0]
        for u in range(8):
            if STOP <= 0:
                continue
            gC = float((1.0 - 2.0 ** (-5.0 - u)) ** 128)
            p.dma("sp", gnt[:], wd[l]["gn"][:, :, u * 128:(u + 1) * 128], writes=[r_gnt])
            p.dma("pool", gwt[:], wd[l]["gw"][u], writes=[r_gwt])
            p.dma("sp", dmk[:], dmask_d[:, u, :], writes=[r_dmk])
            p.phase = "ev_qk"
            for c in range(2):
                for G in range(4):
                    bank, rb = proj_feat(slabA, rslabA, c * 128, G)
                    rope(bank, rb, qk[:, c, G * 512:(G + 1) * 512], rqk[c], G)
            if STOP <= 1:
                continue
            p.phase = "ev_vg"
            for t in range(NT):
                bank, rb = proj_tok(slabA, rslabA, 256, 256, t)
                cp("dve", vv[:, t, :], bank[:, 0:128], [rb], [r_vv])
                act(sgr[:, t, :], bank[:, 128:256], AF.Silu, [rb], [r_sgr[t]])
            load_next("A")
            if STOP <= 2:
                continue
            p.phase = "ev_kE"
            for t4 in range(4):
                bank, rb = nxt("proj")
                bv = bank[:, 0:256].bitcast(BF16)
                for k in range(4):
                    t = t4 * 4 + k
                    tr(bv[:, k * 128:(k + 1) * 128], qk[:, 1, t * 128:(t + 1) * 128], identb[:], [rqk[1], rconst], [rb])
                ts("dve", kE[:, t4 * 4:t4 * 4 + 4, :], bv.rearrange("p (k n) -> p k n", k=4), kend[:, u:u + 1], None, ALU.mult, None,
                   [rb, rconst], [r_kE])
            if STOP <= 3:
                continue
            p.phase = "ev_ret"
            stbs = {}

            def emit_a(t):
                bank, rb = nxt("st")
                mm(bank[:, 0:128], qk[:, 1, t * 128:(t + 1) * 128], qk[:, 0, t * 128:(t + 1) * 128], True, True, [rqk[0], rqk[1]], [rb])
                mm(bank[:, 128:256], kE[:, t, :], vv[:, t, :], True, True, [r_kE, r_vv], [rb])
                stbs[t] = (bank, rb)

            emit_a(0)
            for t in range(NT):
                if t + 1 < NT:
                    emit_a(t + 1)
                bank, rb = stbs.pop(t)
                i = sc_ctr[0] % 3
                sc_ctr[0] += 1
                sc, rsc = scs[i], r_sc[i]
                tt("dve", sc[:], bank[:, 0:128], dmk[:], ALU.mult, [rb, r_dmk], [rsc])
                if t + 1 < NT:
                    if t == 0:
                        cp("act", R32[:], bank[:, 128:256], [rb], [r_R32])
                    else:
                        stt("dve", R32[:], R32[:], gC, bank[:, 128:256], ALU.mult, ALU.add, [rb, r_R32], [r_R32])
                    cp("act", Rbs[(t + 1) % 3][:], R32[:], [r_R32], [r_Rbs[(t + 1) % 3]])
                ob, rob = nxt("o")
                mm(ob[:, 0:128], sc[:], vv[:, t, :], True, True, [rsc, r_vv], [rob])
                if t > 0:
                    mm(ob[:, 128:256], qk[:, 0, t * 128:(t + 1) * 128], Rbs[t % 3][:], True, True, [rqk[0], r_Rbs[t % 3]], [rob])
                cp("act", ro[:, t, :], ob[:, 0:128], [rob], [r_ro[t]])
                if t > 0:
                    stt("dve", ro[:, t, :], ob[:, 128:256], gpow[:, u:u + 1], ro[:, t, :], ALU.mult, ALU.add,
                        [rob, r_ro[t], rconst], [r_ro[t]])
                p.op("dve", lambda e, t=t: e.bn_stats(out=st6[:, t, :], in_=ro[:, t, :]), [r_ro[t]], [r_st6])
                p.op("dve", lambda e, t=t: e.bn_aggr(out=mv2[:, t, :], in_=st6[:, t, :]), [r_st6], [r_mv2])
            if STOP <= 4:
                continue
            p.phase = "ev_gn"
            act(rs2[:], mv2[:, :, 1], AF.Sqrt, [r_mv2, r_eps], [r_rs2], bias=epst[:])
            p.op("dve", lambda e: e.reciprocal(out=rs2[:], in_=rs2[:]), [r_rs2], [r_rs2])
            stt("dve", nm2[:], mv2[:, :, 0], -1.0, rs2[:], ALU.mult, ALU.mult, [r_mv2, r_rs2], [r_nm2])
            for t in range(NT):
                rv = ro[:, t, :]
                ts("dve", rv, rv, rs2[:, t:t + 1], nm2[:, t:t + 1], ALU.mult, ALU.add, [r_ro[t], r_rs2, r_nm2], [r_ro[t]])
                tt("pool", rv, rv, gnt[:, 0, :], ALU.mult, [r_ro[t], r_gnt], [r_ro[t]])
                tt("pool", rv, rv, gnt[:, 1, :], ALU.add, [r_ro[t], r_gnt], [r_ro[t]])
                tt("dve", sgr[:, t, :], rv, sgr[:, t, :], ALU.mult, [r_ro[t], r_sgr[t]], [r_sgr[t]])
            for t4 in range(4):
                bank, rb = nxt("proj")
                bv = bank[:, 0:256].bitcast(BF16)
                for k in range(4):
                    t = t4 * 4 + k
                    tr(bv[:, k * 128:(k + 1) * 128], sgr[:, t, :], identb[:], [r_sgr[t], rconst], [rb])
                cp("act", mixT[:, 0, t4 * 512:(t4 + 1) * 512], bv, [rb], [r_mixR])
            if STOP <= 5:
                continue
            p.phase = "ev_lru"
            L = pr["lrup"]
            p.op("dve", lambda e: e.memset(xb32[:, 0:3], 0.0), [], [r_xb])
            for G in range(4):
                hh, r_hh = lt[f"hh{G % 2}"], rlt[f"hh{G % 2}"]
                hp, r_hp = lt[f"hh{(G + 1) % 2}"], rlt[f"hh{(G + 1) % 2}"]
                if G > 0:
                    cp("dve", xb32[:, 0:3], xb32[:, 512:515], [r_xb], [r_xb])
                bank, rb = proj_feat(slabB, rslabB, 0, G)
                cp("act", xb32[:, 3:515], bank[:], [rb], [r_xb])
                bank, rb = proj_feat(slabB, rslabB, 128, G)
                act(sgl[:], bank[:], AF.Silu, [rb], [r_sgl])
                xc = lt["xc"]
                ts("dve", xc[:], xb32[:, 3:515], L[:, u, 3:4], L[:, u, 4:5], ALU.mult, ALU.add, [r_xb, pr["r"]], [rlt["xc"]])
                for w in (2, 1, 0):
                    stt("dve", xc[:], xb32[:, w:w + 512], L[:, u, w:w + 1], xc[:], ALU.mult, ALU.add, [r_xb, pr["r"], rlt["xc"]], [rlt["xc"]])
                cp("pool", xcb[:], xc[:], [rlt["xc"]], [r_xcb])
                bank, rb = nxt("proj")
                mm(bank[:], gwt[:, 0:128], xcb[:], True, True, [r_gwt, r_xcb], [rb])
                act(lt["rr"][:], bank[:], AF.Sigmoid, [rb, pr["r"]], [rlt["rr"]], bias=L[:, u, 5:6])
                bank, rb = nxt("proj")
                mm(bank[:], gwt[:, 128:256], xcb[:], True, True, [r_gwt, r_xcb], [rb])
                act(lt["ig"][:], bank[:], AF.Sigmoid, [rb, pr["r"]], [rlt["ig"]], bias=L[:, u, 6:7])
                act(lt["aa"][:], lt["rr"][:], AF.Exp, [rlt["rr"], pr["r"]], [rlt["aa"]], scale=pr["c"][:, u:u + 1])
                act(lt["a2"][:], lt["rr"][:], AF.Exp, [rlt["rr"], pr["r"]], [rlt["a2"]], scale=pr["c2"][:, u:u + 1])
                act(lt["rr"][:], lt["rr"][:], AF.Tanh, [rlt["rr"], pr["r"]], [rlt["rr"]], scale=pr["cn"][:, u:u + 1])
                stt("dve", lt["a2"][:], lt["a2"][:], 1.0, lt["rr"][:], ALU.add, ALU.mult, [rlt["a2"], rlt["rr"]], [rlt["a2"]])
                act(lt["a2"][:], lt["a2"][:], AF.Sqrt, [rlt["a2"]], [rlt["a2"]])
                tt("pool", lt["ig"][:], lt["ig"][:], xc[:], ALU.mult, [rlt["ig"], rlt["xc"]], [rlt["ig"]])
                tt("dve", lt["ig"][:], lt["ig"][:], lt["a2"][:], ALU.mult, [rlt["ig"], rlt["a2"]], [rlt["ig"]])
                init = 0.0 if G == 0 else hp[:, 511:512]
                p.op("dve", lambda e, hh=hh, init=init: e.tensor_tensor_scan(out=hh[:], data0=lt["aa"][:], data1=lt["ig"][:], initial=init,
                                                                              op0=ALU.mult, op1=ALU.add),
                     [rlt["aa"], rlt["ig"], r_hp], [r_hh])
                tt("pool", mixT[:, 1, G * 512:(G + 1) * 512], hh[:], sgl[:], ALU.mult, [r_hh, r_sgl], [r_mixL])
            load_next("B")
            if STOP <= 6:
                continue
            p.phase = "ev_out"
            wo, rwo = load_wout(l, u)
            out_proj([mixT[:, 0, :], mixT[:, 1, :]], [r_mixR, r_mixL], wo, rwo)

    load_next("A")
    load_next("B")
    for s in range(nseq):
        xv = x_d[s].rearrange("(t p) d -> p t d", p=128)
        for t in range(NT):
            p.dma("sp", h32[:, t, :], xv[:, t, :], writes=[rh[t]])
        for t in range(NT):
            make_hT(t)
        if not layers:
            for t in range(NT):
                p.dma("sp", out_d[s].rearrange("(t p) d -> p t d", p=128)[:, t, :], h32[:, t, :], reads=[rh[t]])
        for li_, l in enumerate(layers):
            p.barrier()
            prescale()
            if l % 2 == 0:
                even_layer(l)
            else:
                odd_layer(l)
            if STOP <= -1:
                for t in range(NT):
                    p.dma("sp", out_d[s].rearrange("(t p) d -> p t d", p=128)[:, t, :], h32[:, t, :], reads=[rh[t]])
                continue
            layer_norm(l, s, last=(li_ == len(layers) - 1))
    p.wait_all_dma("sp", rh)
    p.emit()
    p.stats["pe_phases"] = [o.get("ph", "") for o in p.ops["pe"] if o["fn"] is not None]
    return nc, p.stats


def host_constants():
    half = 64
    inv_freq = (10000.0 ** (-np.arange(half, dtype=np.float32) / half)).astype(np.float32)
    pos = np.arange(S, dtype=np.float32)
    ang = pos[None, :] * inv_freq[:, None]
    cos = np.cos(ang).astype(np.float32)
    sin = np.sin(ang).astype(np.float32)
    cosT = np.concatenate([cos, cos], axis=0)
    sinS = np.concatenate([-sin, sin], axis=0)
    identf = np.eye(128, dtype=np.float32)
    j = np.arange(128)[:, None]
    i = np.arange(128)[None, :]
    cmask = (i >= j).astype(np.float32)
    g = (1.0 - 2.0 ** (-5.0 - np.arange(8, dtype=np.float64)))
    dmask = np.zeros((128, 8, 128), np.float64)
    for h in range(8):
        dmask[:, h, :] = np.where(i >= j, g[h] ** np.maximum(i - j, 0), 0.0) * SCALE
    gpow = g[None, :] ** (np.arange(128)[:, None] + 1.0)
    kend = (g[None, :] ** (127.0 - np.arange(128)[:, None])) * SCALE
    return dict(cosT=np.ascontiguousarray(cosT), sinS=np.ascontiguousarray(sinS), identf=identf, cmask=cmask,
                dmask=dmask.astype(np.float32), gpow=gpow.astype(np.float32), kend=kend.astype(np.float32))


def rep(a):
    return np.ascontiguousarray(np.broadcast_to(a, (128,) + a.shape))


def host_weights(inp, layers):
    m = {}
    for l in layers:
        j = l // 2
        if l % 2 == 0:
            w = inp["ev_w_in"][j]
            cols = []
            for u in range(8):
                cols.append(np.concatenate([w[:, k * 1024 + u * 128:k * 1024 + (u + 1) * 128] for k in range(6)], axis=1))
            m[f"w_in{l}"] = np.ascontiguousarray(np.stack(cols, 0))
            wo = inp["ev_w_out"][j]
            m[f"w_out{l}"] = np.ascontiguousarray(np.stack(
                [np.concatenate([wo[u * 128:(u + 1) * 128], wo[1024 + u * 128:1024 + (u + 1) * 128]], 0) for u in range(8)], 0))
            m[f"gw{l}"] = np.ascontiguousarray(np.concatenate([inp["ev_gate_a_w"][j], inp["ev_gate_x_w"][j]], axis=2))
            lr = np.stack([inp["ev_conv_w"][j][0], inp["ev_conv_w"][j][1], inp["ev_conv_w"][j][2], inp["ev_conv_w"][j][3],
                           inp["ev_conv_b"][j], inp["ev_gate_a_b"][j], inp["ev_gate_x_b"][j], inp["ev_lru_lambda"][j]], axis=1)
            m[f"lrup{l}"] = np.ascontiguousarray(lr.reshape(8, 128, 8).transpose(1, 0, 2))
            m[f"gn{l}"] = rep(np.stack([inp["ev_ret_gn_g"][j], inp["ev_ret_gn_b"][j]], 0))
            m[f"ln{l}"] = rep(np.stack([inp["ev_ln_g"][j], inp["ev_ln_b"][j]], 0))
        else:
            w = inp["od_w_in"][j]
            cols = []
            for h in range(8):
                q1 = w[:, (2 * h) * 128:(2 * h + 1) * 128]
                q2 = w[:, (2 * h + 1) * 128:(2 * h + 2) * 128]
                k1 = w[:, 2048 + (2 * h) * 128:2048 + (2 * h + 1) * 128]
                k2 = w[:, 2048 + (2 * h + 1) * 128:2048 + (2 * h + 2) * 128]
                v = w[:, 4096 + h * 256:4096 + (h + 1) * 256]
                gg = w[:, 6144 + h * 256:6144 + (h + 1) * 256]
                cols.append(np.concatenate([q1, q2, k1, k2, v, gg], axis=1))
            m[f"w_in{l}"] = np.ascontiguousarray(np.stack(cols, 0))
            m[f"w_out{l}"] = np.ascontiguousarray(inp["od_w_out"][j].reshape(8, 256, 1024))
            m[f"lam{l}"] = rep(np.stack([inp["od_lambda_q1"][j], inp["od_lambda_k1"][j], inp["od_lambda_q2"][j], inp["od_lambda_k2"][j]], 0))
            m[f"sub{l}"] = rep(inp["od_subln_g"][j])
            m[f"ln{l}"] = rep(np.stack([inp["od_ln_g"][j], inp["od_ln_b"][j]], 0))
    return m


_PROGS = {}


def run_layers(x, inp, layers, ncores=NCORES, trace=False):
    B = x.shape[0]
    nseq = B // ncores
    key = (tuple(layers), nseq)
    if key not in _PROGS:
        _PROGS[key] = build_program(layers, nseq)
    nc, stats = _PROGS[key]
    consts = host_constants()
    wts = host_weights(inp, layers)
    in_maps = []
    for c in range(ncores):
        mdict = dict(consts)
        mdict.update(wts)
        mdict["x"] = np.ascontiguousarray(x[c * nseq:(c + 1) * nseq])
        in_maps.append(mdict)
    res = run_bass_kernel_spmd(nc, in_maps, core_ids=list(range(ncores)))
    return np.concatenate([r["out"] for r in res.results], axis=0)


FUSED = True
import os
STOP = int(os.environ.get("KSTOP", "99"))


def kernel(**inputs):
    inp = {k: np.asarray(v) for k, v in inputs.items()}
    x = np.ascontiguousarray(inp["x"], dtype=np.float32)
    if FUSED:
        return run_layers(x, inp, [0, 1, 2, 3]).astype(np.float32)
    h = x
    for l in range(DEPTH):
        h = run_layers(h, inp, [l])
    return h.astype(np.float32)
```

```python
import math
import numpy as np
from contextlib import ExitStack
import concourse.bass as bass
import concourse.mybir as mybir
from concourse.bass_utils import run_bass_kernel_spmd

F32 = mybir.dt.float32
BF16 = mybir.dt.bfloat16
ALU = mybir.AluOpType
AF = mybir.ActivationFunctionType
AX = mybir.AxisListType

NCORES = 8
S = 2048
D = 1024
NT = 16
DEPTH = 4
EPS = 1e-5
ALPHA = (2.0 * DEPTH) ** 0.25
SCALE = 128.0 ** -0.5

ENG = ("pe", "act", "dve", "pool", "sp")
EPOCH = 30000


class Res:
    __slots__ = ("name", "w", "r", "sem", "excl")

    def __init__(self, name, excl=False):
        self.excl = excl
        self.name = name
        self.w = None
        self.r = {}
        self.sem = None


class Prog:
    def __init__(self, nc):
        self.nc = nc
        self.ops = {e: [] for e in ENG}
        self.waited = {e: {} for e in ENG}
        self.dma_cnt = []
        self.stack = ExitStack()
        self.n_sb = 0

    def sb(self, shape, dt, name=None):
        self.n_sb += 1
        return self.stack.enter_context(self.nc.sbuf_tensor("S_" + (name or f"sb{self.n_sb}"), list(shape), dt))

    def ps(self, shape, dt, name=None):
        self.n_sb += 1
        return self.stack.enter_context(self.nc.psum_tensor("P_" + (name or f"ps{self.n_sb}"), list(shape), dt))

    def _need(self, eng, dep, waits):
        if dep is None:
            return
        key = (dep[0], dep[1])
        val = dep[2]
        if self.waited[eng].get(key, -1) >= val:
            return
        if waits.get(key, -1) >= val:
            return
        waits[key] = val

    def op(self, eng, fn, reads=(), writes=()):
        ex = [r for r in reads if r.excl]
        if ex:
            reads = [r for r in reads if not r.excl]
            writes = list(writes) + [r for r in ex if r not in writes]
        idx = len(self.ops[eng])
        waits = {}
        for r in reads:
            d = r.w
            if d is None:
                continue
            if d[0] == "op" and d[1] == eng and eng in ("pe", "sp"):
                continue
            self._need(eng, d, waits)
        for w in writes:
            for d in ([w.w] if w.w is not None else []) + list(w.r.values()):
                if d[0] == "op" and d[1] == eng:
                    continue
                self._need(eng, d, waits)
        for k, v in waits.items():
            self.waited[eng][k] = v
        me = ("op", eng, idx)
        for r in reads:
            r.r[("op", eng)] = me
        for w in writes:
            w.w = me
            w.r = {}
        self.ops[eng].append(dict(fn=fn, waits=waits, dma=None, inc=False, ph=getattr(self, "phase", "")))
        return me

    def dma(self, eng, out, in_, reads=(), writes=(), sem_res=None):
        if sem_res is None:
            sem_res = writes[0] if writes else reads[0]
        if sem_res.sem is None:
            sem_res.sem = len(self.dma_cnt)
            self.dma_cnt.append(0)
        sid = sem_res.sem
        waits = {}
        for r in reads:
            self._need(eng, r.w, waits)
        for w in writes:
            for d in ([w.w] if w.w is not None else []) + list(w.r.values()):
                self._need(eng, d, waits)
        for k, v in waits.items():
            self.waited[eng][k] = v
        self.dma_cnt[sid] += 16
        me = ("dma", sid, self.dma_cnt[sid])
        for r in reads:
            r.r[("dma", sid)] = me
        for w in writes:
            w.w = me
            w.r = {}
        self.ops[eng].append(dict(fn=lambda e, o=out, i=in_: e.dma_start(out=o, in_=i), waits=waits, dma=sid, inc=False))
        return me

    def barrier(self, engines=("pe", "act", "dve", "pool")):
        last = {}
        for e in engines:
            i = len(self.ops[e]) - 1
            while i >= 0 and (self.ops[e][i]["fn"] is None or self.ops[e][i]["dma"] is not None):
                i -= 1
            if i >= 0:
                last[e] = i
        for e in tuple(engines) + ("sp",):
            waits = {}
            for e2, i2 in last.items():
                if e2 != e:
                    self._need(e, ("op", e2, i2), waits)
            for k, v in waits.items():
                self.waited[e][k] = v
            self.ops[e].append(dict(fn=None, waits=waits, dma=None, inc=False))

    def wait_all_dma(self, eng, resources):
        waits = {}
        for r in resources:
            for d in ([r.w] if r.w is not None else []) + list(r.r.values()):
                if d[0] == "dma":
                    self._need(eng, d, waits)
        for k, v in waits.items():
            self.waited[eng][k] = v
        self.ops[eng].append(dict(fn=None, waits=waits, dma=None, inc=False))

    def emit(self):
        nc = self.nc
        for e in ENG:
            for o in self.ops[e]:
                for (kind, src), val in o["waits"].items():
                    if kind == "op":
                        self.ops[src][val]["inc"] = True
        semval = {e: {} for e in ENG}
        nsem = {}
        for e in ENG:
            c = 0
            for i, o in enumerate(self.ops[e]):
                if o["inc"]:
                    semval[e][i] = (c // EPOCH, c % EPOCH + 1)
                    c += 1
            nsem[e] = (c + EPOCH - 1) // EPOCH
        st = self.stack
        esems = {e: [st.enter_context(nc.semaphore(f"s_{e}{k}")) for k in range(nsem[e])] for e in ENG}
        dsems = [st.enter_context(nc.semaphore(f"d{k}")) for k in range(len(self.dma_cnt))]
        self.stats = {e: (len(self.ops[e]), sum(len(o["waits"]) for o in self.ops[e])) for e in ENG}
        block = st.enter_context(nc.Block())

        def runner(e):
            def run(eng):
                for i, o in enumerate(self.ops[e]):
                    for (kind, src), val in o["waits"].items():
                        if kind == "op":
                            ep, v = semval[src][val]
                            eng.wait_ge(esems[src][ep], v)
                        else:
                            eng.wait_ge(dsems[src], val)
                    if o["fn"] is None:
                        continue
                    ins = o["fn"](eng)
                    if o["dma"] is not None:
                        ins.then_inc(dsems[o["dma"]], 16)
                    elif o["inc"]:
                        ep, v = semval[e][i]
                        ins.then_inc(esems[e][ep], 1)
            return run

        block.tensor(runner("pe"))
        block.scalar(runner("act"))
        block.vector(runner("dve"))
        block.gpsimd(runner("pool"))
        block.sync(runner("sp"))
        st.close()


def lambda_init(l):
    return 0.8 - 0.6 * math.exp(-0.3 * l)


def build_program(layers, nseq, first_is_input=True):
    nc = bass.Bass("TRN2", target_bir_lowering=False)
    p = Prog(nc)

    def din(name, shape):
        return nc.dram_tensor(name, list(shape), F32, kind="ExternalInput").ap()

    x_d = din("x", [nseq, S, D])
    out_d = nc.dram_tensor("out", [nseq, S, D], F32, kind="ExternalOutput").ap()
    cos_d = din("cosT", [128, S])
    sin_d = din("sinS", [128, S])
    identf_d = din("identf", [128, 128])
    cmask_d = din("cmask", [128, 128])
    dmask_d = din("dmask", [128, 8, 128])
    gpow_d = din("gpow", [128, 8])
    kend_d = din("kend", [128, 8])
    wd = {}
    for l in layers:
        if l % 2 == 0:
            wd[l] = dict(w_in=din(f"w_in{l}", [8, D, 768]), w_out=din(f"w_out{l}", [8, 256, D]),
                         gw=din(f"gw{l}", [8, 128, 256]), lrup=din(f"lrup{l}", [128, 8, 8]),
                         gn=din(f"gn{l}", [128, 2, D]), ln=din(f"ln{l}", [128, 2, D]))
        else:
            wd[l] = dict(w_in=din(f"w_in{l}", [8, D, 1024]), w_out=din(f"w_out{l}", [8, 256, D]),
                         lam=din(f"lam{l}", [128, 4, 128]), sub=din(f"sub{l}", [128, 256]),
                         ln=din(f"ln{l}", [128, 2, D]))

    h32 = p.sb([128, NT, D], F32, "h32")
    rh = [Res(f"h32_{t}") for t in range(NT)]
    hT = p.sb([128, 8, S], BF16, "hT")
    rhT = [Res(f"hT_{t}") for t in range(NT)]
    slabA = p.sb([128, 8, 512], BF16, "slabA"); rslabA = Res("slabA")
    slabB = p.sb([128, 8, 512], BF16, "slabB"); rslabB = Res("slabB")
    wout = [p.sb([128, 2, D], BF16, f"wout{i}") for i in range(1)]
    rwout = [Res(f"wout{i}") for i in range(1)]
    qk = p.sb([128, 4, S], BF16, "qk")
    rqk = [Res(f"qk{i}") for i in range(4)]
    cosT = p.sb([128, S], F32, "cosT"); sinS = p.sb([128, S], F32, "sinS")
    rconst = Res("const")
    identf = p.sb([128, 128], F32, "identf")
    identb = p.sb([128, 128], BF16, "identb")
    cmask = p.sb([128, 128], BF16, "cmask")
    gpow = p.sb([128, 8], F32, "gpow")
    kend = p.sb([128, 8], F32, "kend")
    epst = p.sb([128, 1], F32, "eps")
    onet = p.sb([128, 1], F32, "one")
    rxs = [p.sb([128, 512], F32, f"rxs{i}") for i in range(2)]; r_rxs = [Res(f"rxs{i}") for i in range(2)]
    rtt = [p.sb([128, 512], F32, f"rtt{i}") for i in range(2)]; r_rtt = [Res(f"rtt{i}") for i in range(2)]
    ARENA = 44800
    arena = p.sb([128, ARENA // 4], F32, "arena")
    st12 = p.sb([128, NT, 12], F32, "st12"); r_st12 = Res("st12")
    mvall = p.sb([128, NT, 2], F32, "mvall"); r_mv = Res("mvall")
    rstd = p.sb([128, NT], F32, "rstd"); r_rstd = Res("rstd")
    nmr = p.sb([128, NT], F32, "nmr"); r_nmr = Res("nmr")
    prm = {}
    for l in layers:
        if l % 2 == 0:
            prm[l] = dict(lrup=p.sb([128, 8, 8], F32, f"lrup{l}"), c=p.sb([128, 8], F32, f"c{l}"),
                          c2=p.sb([128, 8], F32, f"c2{l}"), cn=p.sb([128, 8], F32, f"cn{l}"), r=Res(f"prm{l}"))
        else:
            prm[l] = dict(lamt=rxs[0][:].rearrange("p (a b) -> p a b", a=4), nlam=p.sb([128, 1], F32, f"nlam{l}"),
                          sub2=p.sb([128, 256], F32, f"sub2{l}"), tmp=rtt[0][:, 0:256].rearrange("p (a b) -> p a b", a=2),
                          s12=p.sb([128, 2], F32, f"s12{l}"), r=Res(f"prm{l}"))

    banks = [p.ps([128, 512], F32, f"bank{i}") for i in range(8)]
    rbank = [Res(f"bank{i}", excl=True) for i in range(8)]
    grp = {"proj": [0, 1, 2, 3], "st": [0, 1, 2], "o": [4, 5, 6, 7]}
    gcnt = {k: 0 for k in grp}

    def nxt(g):
        i = grp[g][gcnt[g] % len(grp[g])]
        gcnt[g] += 1
        return banks[i], rbank[i]

    def mm(out, lhsT, rhs, start, stop, reads, writes):
        p.op("pe", lambda e: e.matmul(out, lhsT=lhsT, rhs=rhs, start=start, stop=stop), reads, writes)

    def tr(out, in_, ident, reads, writes):
        p.op("pe", lambda e: e.transpose(out=out, in_=in_, identity=ident), reads, writes)

    def act(out, in_, func, reads, writes, bias=None, scale=1.0, accum=None):
        kw = {}
        if bias is not None:
            kw["bias"] = bias
        if accum is not None:
            kw["accum_out"] = accum
        p.op("act", lambda e: e.activation(out=out, in_=in_, func=func, scale=scale, **kw), reads, writes)

    def tt(eng, out, in0, in1, op, reads, writes):
        p.op(eng, lambda e: e.tensor_tensor(out=out, in0=in0, in1=in1, op=op), reads, writes)

    def ts(eng, out, in0, s1, s2, op0, op1, reads, writes):
        if s2 is None:
            p.op(eng, lambda e: e.tensor_single_scalar(out=out, in_=in0, scalar=s1, op=op0), reads, writes)
        else:
            p.op(eng, lambda e: e.tensor_scalar(out=out, in0=in0, scalar1=s1, scalar2=s2, op0=op0, op1=op1), reads, writes)

    def stt(eng, out, in0, scalar, in1, op0, op1, reads, writes):
        p.op(eng, lambda e: e.scalar_tensor_tensor(out=out, in0=in0, scalar=scalar, in1=in1, op0=op0, op1=op1), reads, writes)

    def cp(eng, out, in_, reads, writes):
        if eng == "act":
            p.op("act", lambda e: e.copy(out=out, in_=in_), reads, writes)
        else:
            p.op(eng, lambda e: e.tensor_copy(out=out, in_=in_), reads, writes)

    p.dma("sp", cosT[:], cos_d, writes=[rconst])
    p.dma("sp", sinS[:], sin_d, writes=[rconst])
    p.dma("sp", identf[:], identf_d, writes=[rconst])
    p.dma("pool", identb[:], identf_d, writes=[rconst])
    p.dma("pool", cmask[:], cmask_d, writes=[rconst])
    p.dma("sp", gpow[:], gpow_d, writes=[rconst])
    p.dma("sp", kend[:], kend_d, writes=[rconst])
    r_eps = Res("eps")
    p.op("dve", lambda e: e.memset(epst[:], EPS), writes=[r_eps])
    p.op("dve", lambda e: e.memset(onet[:], 1.0), writes=[r_eps])
    for l in layers:
        pr = prm[l]
        if STOP <= -2:
            continue
        if l % 2 == 0:
            p.dma("sp", pr["lrup"][:], wd[l]["lrup"], writes=[pr["r"]])
            act(pr["c"][:], pr["lrup"][:, :, 7], AF.Exp, [pr["r"]], [pr["r"]], scale=-1.0)
            act(pr["c"][:], pr["c"][:], AF.Ln, [pr["r"], r_eps], [pr["r"]], bias=onet[:])
            ts("dve", pr["cn"][:], pr["c"][:], 8.0, None, ALU.mult, None, [pr["r"]], [pr["r"]])
            ts("dve", pr["c"][:], pr["cn"][:], -1.0, None, ALU.mult, None, [pr["r"]], [pr["r"]])
            ts("dve", pr["c2"][:], pr["c"][:], 2.0, None, ALU.mult, None, [pr["r"]], [pr["r"]])
        else:
            li = lambda_init(l)
            p.dma("sp", pr["lamt"], wd[l]["lam"], writes=[r_rxs[0]])
            p.dma("sp", pr["sub2"][:], wd[l]["sub"], writes=[pr["r"]])
            tt("dve", pr["tmp"][:, 0, :], pr["lamt"][:, 0, :], pr["lamt"][:, 1, :], ALU.mult, [r_rxs[0]], [r_rtt[0]])
            tt("dve", pr["tmp"][:, 1, :], pr["lamt"][:, 2, :], pr["lamt"][:, 3, :], ALU.mult, [r_rxs[0]], [r_rtt[0]])
            p.op("dve", lambda e, pr=pr: e.reduce_sum(out=pr["s12"][:], in_=pr["tmp"], axis=AX.X), [r_rtt[0]], [pr["r"]])
            act(pr["s12"][:], pr["s12"][:], AF.Exp, [pr["r"]], [pr["r"]])
            tt("dve", pr["nlam"][:], pr["s12"][:, 1:2], pr["s12"][:, 0:1], ALU.subtract, [pr["r"]], [pr["r"]])
            ts("dve", pr["nlam"][:], pr["nlam"][:], -li, None, ALU.add, None, [pr["r"]], [pr["r"]])
            ts("dve", pr["sub2"][:], pr["sub2"][:], 1.0 - li, None, ALU.mult, None, [pr["r"]], [pr["r"]])

    sched = []
    for s_ in range(nseq):
        for l_ in layers:
            for u_ in range(8):
                sched.append((l_, u_))
    pos = {"A": 0, "B": 0}

    def load_next(which):
        i = pos[which]
        pos[which] += 1
        if i >= len(sched):
            return
        l, u = sched[i]
        src = wd[l]["w_in"][u].rearrange("(kc p) c -> p kc c", p=128)
        if which == "A":
            dst, rdst, c0, w = slabA, rslabA, 0, 512
        else:
            dst, rdst, c0, w = slabB, rslabB, 512, (512 if l % 2 == 1 else 256)
        for kc0 in range(0, 8, 4):
            p.dma("pool", dst[:, kc0:kc0 + 4, 0:w], src[:, kc0:kc0 + 4, c0:c0 + w], writes=[rdst])

    def load_wout(l, u):
        src = wd[l]["w_out"][u].rearrange("(c p) n -> p c n", p=128)
        p.dma("pool", wout[0][:], src, writes=[rwout[0]])
        return wout[0], rwout[0]

    rope_ctr = [0]

    def rope(bank, rb, dst, rdst, G):
        i = rope_ctr[0] % 2
        rope_ctr[0] += 1
        xs, r1 = rxs[i], r_rxs[i]
        t1, r2 = rtt[i], r_rtt[i]
        cs = cosT[:, G * 512:(G + 1) * 512]
        sn = sinS[:, G * 512:(G + 1) * 512]
        p.op("act", lambda e: e.copy(out=xs[0:64, :], in_=bank[64:128, :]), [rb], [r1])
        p.op("act", lambda e: e.copy(out=xs[64:128, :], in_=bank[0:64, :]), [rb], [r1])
        tt("dve", t1[:], bank[:], cs, ALU.mult, [rb, rconst], [r2])
        tt("pool", xs[:], xs[:], sn, ALU.mult, [r1, rconst], [r1])
        tt("dve", dst, t1[:], xs[:], ALU.add, [r1, r2], [rdst])

    def proj_feat(sl, rsl, c0, G):
        bank, rb = nxt("proj")
        for kc in range(8):
            mm(bank[:], sl[:, kc, c0:c0 + 128], hT[:, kc, G * 512:(G + 1) * 512], kc == 0, kc == 7,
               [rsl] + rhT[4 * G:4 * G + 4], [rb])
        return bank, rb

    def proj_tok(sl, rsl, c0, width, t):
        bank, rb = nxt("proj")
        for kc in range(8):
            mm(bank[:, 0:width], hT[:, kc, t * 128:(t + 1) * 128], sl[:, kc, c0:c0 + width], kc == 0, kc == 7,
               [rsl, rhT[t]], [rb])
        return bank, rb

    def out_proj(mixT_views, rmix, wo, rwo, scale=None, rscale=None):
        k = 0
        for t in range(NT):
            for cg in range(2):
                bank, rb = nxt("proj")
                for c in range(2):
                    mm(bank[:], mixT_views[c][:, t * 128:(t + 1) * 128], wo[:, c, cg * 512:(cg + 1) * 512],
                       c == 0, c == 1, [rwo] + rmix, [rb])
                hv = h32[:, t, cg * 512:(cg + 1) * 512]
                extra = [rscale] if rscale is not None else []
                if k % 2 == 0:
                    if scale is None:
                        tt("dve", hv, hv, bank[:], ALU.add, [rb, rh[t]], [rh[t]])
                    else:
                        stt("dve", hv, bank[:], scale[:, t:t + 1], hv, ALU.mult, ALU.add, [rb, rh[t]] + extra, [rh[t]])
                else:
                    i = (k // 2) % 2
                    if scale is None:
                        cp("act", rxs[i][:], bank[:], [rb], [r_rxs[i]])
                    else:
                        act(rxs[i][:], bank[:], AF.Copy, [rb] + extra, [r_rxs[i]], scale=scale[:, t:t + 1])
                    tt("pool", hv, hv, rxs[i][:], ALU.add, [r_rxs[i], rh[t]], [rh[t]])
                k += 1

    def prescale():
        for t in range(NT):
            ts("dve", h32[:, t, :], h32[:, t, :], ALPHA, None, ALU.mult, None, [rh[t]], [rh[t]])

    def make_hT(t):
        for half in range(2):
            bank, rb = nxt("proj")
            for k in range(4):
                kc = half * 4 + k
                tr(bank[:, k * 128:(k + 1) * 128], h32[:, t, kc * 128:(kc + 1) * 128], identf[:], [rh[t], rconst], [rb])
            dst = hT[:, half * 4:half * 4 + 4, t * 128:(t + 1) * 128]
            src = bank[:].rearrange("p (k n) -> p k n", k=4)
            cp("act", dst, src, [rb], [rhT[t]])

    def layer_norm(l, s, last):
        p.phase = "ln"
        lnt = qk[:, 0:2, :].bitcast(F32)
        p.dma("sp", lnt, wd[l]["ln"], writes=[rqk[0], rqk[1]])
        rl = [rqk[0], rqk[1]]
        for t in range(NT):
            for hf in range(2):
                p.op("dve", lambda e, t=t, hf=hf: e.bn_stats(out=st12[:, t, hf * 6:(hf + 1) * 6], in_=h32[:, t, hf * 512:(hf + 1) * 512]),
                     [rh[t]], [r_st12])
            p.op("dve", lambda e, t=t: e.bn_aggr(out=mvall[:, t, :], in_=st12[:, t, :]), [r_st12], [r_mv])
        act(rstd[:], mvall[:, :, 1], AF.Sqrt, [r_mv, r_eps], [r_rstd], bias=epst[:])
        p.op("dve", lambda e: e.reciprocal(out=rstd[:], in_=rstd[:]), [r_rstd], [r_rstd])
        stt("dve", nmr[:], mvall[:, :, 0], -1.0, rstd[:], ALU.mult, ALU.mult, [r_mv, r_rstd], [r_nmr])
        for t in range(NT):
            hv = h32[:, t, :]
            ts("dve", hv, hv, rstd[:, t:t + 1], nmr[:, t:t + 1], ALU.mult, ALU.add, [rh[t], r_rstd, r_nmr], [rh[t]])
            tt("pool", hv, hv, lnt[:, 0, :], ALU.mult, [rh[t]] + rl, [rh[t]])
            tt("dve", hv, hv, lnt[:, 1, :], ALU.add, [rh[t]] + rl, [rh[t]])
            if last:
                p.dma("sp", out_d[s].rearrange("(t p) d -> p t d", p=128)[:, t, :], hv, reads=[rh[t]])
            else:
                make_hT(t)

    def odd_layer(l):
        grp.update({"proj": [0, 1, 2, 3], "st": [0, 1, 2], "o": [4, 5, 6, 7]})
        pr = prm[l]
        off = [0]

        def carve(nbytes, dt, shape):
            a = arena[:, off[0] // 4:(off[0] + nbytes) // 4]
            off[0] += nbytes
            if dt == BF16:
                a = a.bitcast(BF16)
            if len(shape) == 3:
                a = a.rearrange("p (a b) -> p a b", a=shape[1])
            return a

        vaug = carve(NT * 264 * 2, BF16, [128, NT, 264]); r_v = Res("vaug")
        sg = carve(NT * 256 * 2, BF16, [128, NT, 256]); r_sg = Res("sg")
        dd = carve(4 * 256 * 4, F32, [128, 4, 256]); r_dd = [Res(f"dd{i}") for i in range(4)]
        pTs = [carve(512 * 2, BF16, [128, 512]) for _ in range(3)]; r_pT = [Res(f"pT{i}") for i in range(3)]
        mixb = [carve(256 * 2, BF16, [128, 256]) for _ in range(2)]; r_mixb = [Res(f"mixb{i}") for i in range(2)]
        mtmp = [carve(256 * 4, F32, [128, 256]) for _ in range(2)]; r_mtmp = [Res(f"mtmp{i}") for i in range(2)]
        mixT = carve(2 * S * 2, BF16, [128, 2, S]); r_mixT = Res("mixT")
        sq = carve(256 * 4, F32, [128, 256]); r_sq = Res("sqj")
        ss = carve(16 * 4, F32, [128, 16]); r_ss = Res("ss")
        rec = carve(16 * 4, F32, [128, 16]); r_rec = Res("rec")
        assert off[0] <= ARENA, off[0]
        p.op("pool", lambda e: e.memset(vaug[:, :, 256:257], 1.0), [], [r_v])
        pT_ctr = [0]
        mix_ctr = [0]
        for h in range(8):
            p.phase = "odd_qk"
            for c in range(4):
                for G in range(4):
                    bank, rb = proj_feat(slabA, rslabA, c * 128, G)
                    rope(bank, rb, qk[:, c, G * 512:(G + 1) * 512], rqk[c], G)
            load_next("A")
            p.phase = "odd_vg"
            for t in range(NT):
                bank, rb = proj_tok(slabB, rslabB, 0, 512, t)
                cp("dve", vaug[:, t, 0:256], bank[:, 0:256], [rb], [r_v])
                act(sg[:, t, :], bank[:, 256:512], AF.Silu, [rb], [r_sg])
            load_next("B")
            wo, rwo = load_wout(l, h)
            for G in range(4):
                p.phase = "odd_attn"
                nk = 4 * G + 4
                for m in range(2):
                    qc, kc_ = m, 2 + m
                    obs = [nxt("o") for _ in range(4)]
                    stb = {}

                    def emit_s(kt):
                        q0 = max(4 * G, kt)
                        o_ = (q0 - 4 * G) * 128
                        bank, rb = nxt("st")
                        mm(bank[:, o_:512], qk[:, kc_, kt * 128:(kt + 1) * 128], qk[:, qc, G * 512 + o_:(G + 1) * 512],
                           True, True, [rqk[kc_], rqk[qc]], [rb])
                        stb[kt] = (bank, rb, q0, o_)

                    emit_s(0)
                    emit_s(1)
                    for kt in range(nk):
                        if kt + 2 < nk:
                            emit_s(kt + 2)
                        bank, rb, q0, o_ = stb.pop(kt)
                        i = pT_ctr[0] % 3
                        pT_ctr[0] += 1
                        pT, rp = pTs[i], r_pT[i]
                        act(pT[:, o_:512], bank[:, o_:512], AF.Exp, [rb], [rp], scale=SCALE)
                        if kt >= 4 * G:
                            tt("dve", pT[:, o_:o_ + 128], pT[:, o_:o_ + 128], cmask[:], ALU.mult, [rp, rconst], [rp])
                        for qt in range(q0, 4 * G + 4):
                            qi = qt - 4 * G
                            ob, rob = obs[qi]
                            mm(ob[:, 0:257], pT[:, qi * 128:(qi + 1) * 128], vaug[:, kt, 0:257], kt == 0, kt == qt,
                               [rp, r_v], [rob])
                    for qi in range(4):
                        ob, rob = obs[qi]
                        t = 4 * G + qi
                        col = m * 8 + qi
                        p.op("dve", lambda e, ob=ob, col=col: e.reciprocal(out=rec[:, col:col + 1], in_=ob[:, 256:257]), [rob], [r_rec])
                        if m == 0:
                            ts("dve", dd[:, qi, :], ob[:, 0:256], rec[:, col:col + 1], None, ALU.mult, None, [rob, r_rec], [r_dd[qi]])
                        else:
                            tt("dve", rec[:, col + 4:col + 5], rec[:, col:col + 1], pr["nlam"][:], ALU.mult, [r_rec, pr["r"]], [r_rec])
                            stt("dve", dd[:, qi, :], ob[:, 0:256], rec[:, col + 4:col + 5], dd[:, qi, :], ALU.mult, ALU.add,
                                [rob, r_rec, r_dd[qi]], [r_dd[qi]])
                p.phase = "odd_fin"
                for qi in range(4):
                    t = 4 * G + qi
                    i = mix_ctr[0] % 2
                    mix_ctr[0] += 1
                    tt("dve", sq[:], dd[:, qi, :], dd[:, qi, :], ALU.mult, [r_dd[qi]], [r_sq])
                    p.op("dve", lambda e, t=t: e.reduce_sum(out=ss[:, t:t + 1], in_=sq[:], axis=AX.X), [r_sq], [r_ss])
                    tt("dve", mtmp[i][:], dd[:, qi, :], pr["sub2"][:], ALU.mult, [r_dd[qi], pr["r"]], [r_mtmp[i]])
                    tt("dve", mixb[i][:], mtmp[i][:], sg[:, t, :], ALU.mult, [r_mtmp[i], r_sg], [r_mixb[i]])
                    bank, rb = nxt("proj")
                    bv = bank[:, 0:128].bitcast(BF16)
                    for c in range(2):
                        tr(bv[:, c * 128:(c + 1) * 128], mixb[i][:, c * 128:(c + 1) * 128], identb[:], [r_mixb[i], rconst], [rb])
                    cp("act", mixT[:, :, t * 128:(t + 1) * 128], bv.rearrange("p (c n) -> p c n", c=2), [rb], [r_mixT])
            p.phase = "odd_out"
            act(ss[:], ss[:], AF.Sqrt, [r_ss, r_eps], [r_ss], bias=epst[:], scale=1.0 / 256.0)
            p.op("dve", lambda e: e.reciprocal(out=ss[:], in_=ss[:]), [r_ss], [r_ss])
            out_proj([mixT[:, 0, :], mixT[:, 1, :]], [r_mixT], wo, rwo, scale=ss, rscale=r_ss)

    def even_layer(l):
        grp.update({"proj": [0, 1, 2, 3], "st": [4, 5], "o": [6, 7]})
        pr = prm[l]
        off = [0]

        def carve(nbytes, dt, shape):
            a = arena[:, off[0] // 4:(off[0] + nbytes) // 4]
            off[0] += nbytes
            if dt == BF16:
                a = a.bitcast(BF16)
            if len(shape) == 3:
                a = a.rearrange("p (a b) -> p a b", a=shape[1])
            return a

        kE = carve(NT * 128 * 2, BF16, [128, NT, 128]); r_kE = Res("kE")
        vv = carve(NT * 128 * 2, BF16, [128, NT, 128]); r_vv = Res("vv")
        sgr = carve(NT * 128 * 2, BF16, [128, NT, 128]); r_sgr = [Res(f"sgr{t}") for t in range(NT)]
        Rbs = [carve(128 * 2, BF16, [128, 128]) for _ in range(3)]; r_Rbs = [Res(f"Rb{i}") for i in range(3)]
        R32 = carve(128 * 4, F32, [128, 128]); r_R32 = Res("R32")
        ro = qk[:, 2:4, :].bitcast(F32).rearrange("p a (b c) -> p (a b) c", c=128)
        r_ro = [Res(f"ro{t}") for t in range(NT)]
        dmk = carve(128 * 4, F32, [128, 128]); r_dmk = Res("dmk")
        scs = [carve(128 * 2, BF16, [128, 128]) for _ in range(3)]; r_sc = [Res(f"sc{i}") for i in range(3)]
        mixT = carve(2 * S * 2, BF16, [128, 2, S]); r_mixR = Res("mixTr"); r_mixL = Res("mixTl")
        gnt = carve(2 * 128 * 4, F32, [128, 2, 128]); r_gnt = Res("gnt")
        gwt = carve(256 * 2, BF16, [128, 256]); r_gwt = Res("gwt")
        st6 = carve(NT * 6 * 4, F32, [128, NT, 6]); r_st6 = Res("st6")
        mv2 = carve(NT * 2 * 4, F32, [128, NT, 2]); r_mv2 = Res("mv2")
        rs2 = carve(NT * 4, F32, [128, NT]); r_rs2 = Res("rs2")
        nm2 = carve(NT * 4, F32, [128, NT]); r_nm2 = Res("nm2")
        xb32 = carve(516 * 4, F32, [128, 516]); r_xb = Res("xb32")
        names = ["xc", "rr", "ig", "aa", "a2", "hh0", "hh1"]
        lt = {n: carve(512 * 4, F32, [128, 512]) for n in names}
        rlt = {n: Res(n) for n in names}
        xcb = carve(512 * 2, BF16, [128, 512]); r_xcb = Res("xcb")
        sgl = carve(512 * 4, F32, [128, 512]); r_sgl = Res("sgl")
        assert off[0] <= ARENA, off[0]
        sc_ctr = [0]
        for u in range(8):
            if STOP <= 0:
                continue
            gC = float((1.0 - 2.0 ** (-5.0 - u)) ** 128)
            p.dma("sp", gnt[:], wd[l]["gn"][:, :, u * 128:(u + 1) * 128], writes=[r_gnt])
            p.dma("pool", gwt[:], wd[l]["gw"][u], writes=[r_gwt])
            p.dma("sp", dmk[:], dmask_d[:, u, :], writes=[r_dmk])
            p.phase = "ev_qk"
            for c in range(2):
                for G in range(4):
                    bank, rb = proj_feat(slabA, rslabA, c * 128, G)
                    rope(bank, rb, qk[:, c, G * 512:(G + 1) * 512], rqk[c], G)
            if STOP <= 1:
                continue
            p.phase = "ev_vg"
            for t in range(NT):
                bank, rb = proj_tok(slabA, rslabA, 256, 256, t)
                cp("dve", vv[:, t, :], bank[:, 0:128], [rb], [r_vv])
                act(sgr[:, t, :], bank[:, 128:256], AF.Silu, [rb], [r_sgr[t]])
            load_next("A")
            if STOP <= 2:
                continue
            p.phase = "ev_kE"
            for t4 in range(4):
                bank, rb = nxt("proj")
                bv = bank[:, 0:256].bitcast(BF16)
                for k in range(4):
                    t = t4 * 4 + k
                    tr(bv[:, k * 128:(k + 1) * 128], qk[:, 1, t * 128:(t + 1) * 128], identb[:], [rqk[1], rconst], [rb])
                ts("dve", kE[:, t4 * 4:t4 * 4 + 4, :], bv.rearrange("p (k n) -> p k n", k=4), kend[:, u:u + 1], None, ALU.mult, None,
                   [rb, rconst], [r_kE])
            if STOP <= 3:
                continue
            L = pr["lrup"]
            p.op("dve", lambda e: e.memset(xb32[:, 0:3], 0.0), [], [r_xb])

            def lru_stage(G, k):
                hh, r_hh = lt[f"hh{G % 2}"], rlt[f"hh{G % 2}"]
                hp, r_hp = lt[f"hh{(G + 1) % 2}"], rlt[f"hh{(G + 1) % 2}"]
                xc = lt["xc"]
                if k == 0:
                    if G > 0:
                        cp("dve", xb32[:, 0:3], xb32[:, 512:515], [r_xb], [r_xb])
                    bank, rb = proj_feat(slabB, rslabB, 0, G)
                    cp("act", xb32[:, 3:515], bank[:], [rb], [r_xb])
                    bank, rb = proj_feat(slabB, rslabB, 128, G)
                    act(sgl[:], bank[:], AF.Silu, [rb], [r_sgl])
                    ts("dve", xc[:], xb32[:, 3:515], L[:, u, 3:4], L[:, u, 4:5], ALU.mult, ALU.add, [r_xb, pr["r"]], [rlt["xc"]])
                    for w in (2, 1, 0):
                        stt("dve", xc[:], xb32[:, w:w + 512], L[:, u, w:w + 1], xc[:], ALU.mult, ALU.add, [r_xb, pr["r"], rlt["xc"]], [rlt["xc"]])
                    cp("pool", xcb[:], xc[:], [rlt["xc"]], [r_xcb])
                elif k == 1:
                    bank, rb = nxt("proj")
                    mm(bank[:], gwt[:, 0:128], xcb[:], True, True, [r_gwt, r_xcb], [rb])
                    act(lt["rr"][:], bank[:], AF.Sigmoid, [rb, pr["r"]], [rlt["rr"]], bias=L[:, u, 5:6])
                    bank, rb = nxt("proj")
                    mm(bank[:], gwt[:, 128:256], xcb[:], True, True, [r_gwt, r_xcb], [rb])
                    act(lt["ig"][:], bank[:], AF.Sigmoid, [rb, pr["r"]], [rlt["ig"]], bias=L[:, u, 6:7])
                elif k == 2:
                    act(lt["aa"][:], lt["rr"][:], AF.Exp, [rlt["rr"], pr["r"]], [rlt["aa"]], scale=pr["c"][:, u:u + 1])
                    act(lt["a2"][:], lt["rr"][:], AF.Exp, [rlt["rr"], pr["r"]], [rlt["a2"]], scale=pr["c2"][:, u:u + 1])
                    act(lt["rr"][:], lt["rr"][:], AF.Tanh, [rlt["rr"], pr["r"]], [rlt["rr"]], scale=pr["cn"][:, u:u + 1])
                    stt("dve", lt["a2"][:], lt["a2"][:], 1.0, lt["rr"][:], ALU.add, ALU.mult, [rlt["a2"], rlt["rr"]], [rlt["a2"]])
                    tt("pool", lt["ig"][:], lt["ig"][:], xc[:], ALU.mult, [rlt["ig"], rlt["xc"]], [rlt["ig"]])
                else:
                    act(lt["a2"][:], lt["a2"][:], AF.Sqrt, [rlt["a2"]], [rlt["a2"]])
                    tt("dve", lt["ig"][:], lt["ig"][:], lt["a2"][:], ALU.mult, [rlt["ig"], rlt["a2"]], [rlt["ig"]])
                    init = 0.0 if G == 0 else hp[:, 511:512]
                    p.op("dve", lambda e, hh=hh, init=init: e.tensor_tensor_scan(out=hh[:], data0=lt["aa"][:], data1=lt["ig"][:], initial=init,
                                                                                  op0=ALU.mult, op1=ALU.add),
                         [rlt["aa"], rlt["ig"], r_hp], [r_hh])
                    tt("pool", mixT[:, 1, G * 512:(G + 1) * 512], hh[:], sgl[:], ALU.mult, [r_hh, r_sgl], [r_mixL])

            p.phase = "ev_ret"
            stbs = {}

            def emit_a(t):
                bank, rb = nxt("st")
                mm(bank[:, 0:128], qk[:, 1, t * 128:(t + 1) * 128], qk[:, 0, t * 128:(t + 1) * 128], True, True, [rqk[0], rqk[1]], [rb])
                mm(bank[:, 128:256], kE[:, t, :], vv[:, t, :], True, True, [r_kE, r_vv], [rb])
                stbs[t] = (bank, rb)

            emit_a(0)
            for t in range(NT):
                if t + 1 < NT:
                    emit_a(t + 1)
                bank, rb = stbs.pop(t)
                i = sc_ctr[0] % 3
                sc_ctr[0] += 1
                sc, rsc = scs[i], r_sc[i]
                tt("dve", sc[:], bank[:, 0:128], dmk[:], ALU.mult, [rb, r_dmk], [rsc])
                if t + 1 < NT:
                    if t == 0:
                        cp("act", R32[:], bank[:, 128:256], [rb], [r_R32])
                    else:
                        stt("dve", R32[:], R32[:], gC, bank[:, 128:256], ALU.mult, ALU.add, [rb, r_R32], [r_R32])
                    cp("act", Rbs[(t + 1) % 3][:], R32[:], [r_R32], [r_Rbs[(t + 1) % 3]])
                ob, rob = nxt("o")
                mm(ob[:, 0:128], sc[:], vv[:, t, :], True, True, [rsc, r_vv], [rob])
                if t > 0:
                    mm(ob[:, 128:256], qk[:, 0, t * 128:(t + 1) * 128], Rbs[t % 3][:], True, True, [rqk[0], r_Rbs[t % 3]], [rob])
                cp("act", ro[:, t, :], ob[:, 0:128], [rob], [r_ro[t]])
                if t > 0:
                    stt("dve", ro[:, t, :], ob[:, 128:256], gpow[:, u:u + 1], ro[:, t, :], ALU.mult, ALU.add,
                        [rob, r_ro[t], rconst], [r_ro[t]])
                p.op("dve", lambda e, t=t: e.bn_stats(out=st6[:, t, :], in_=ro[:, t, :]), [r_ro[t]], [r_st6])
                p.op("dve", lambda e, t=t: e.bn_aggr(out=mv2[:, t, :], in_=st6[:, t, :]), [r_st6], [r_mv2])
                lru_stage(t // 4, t % 4)
            if STOP <= 4:
                continue
            p.phase = "ev_gn"
            act(rs2[:], mv2[:, :, 1], AF.Sqrt, [r_mv2, r_eps], [r_rs2], bias=epst[:])
            p.op("dve", lambda e: e.reciprocal(out=rs2[:], in_=rs2[:]), [r_rs2], [r_rs2])
            stt("dve", nm2[:], mv2[:, :, 0], -1.0, rs2[:], ALU.mult, ALU.mult, [r_mv2, r_rs2], [r_nm2])
            for t in range(NT):
                rv = ro[:, t, :]
                ts("dve", rv, rv, rs2[:, t:t + 1], nm2[:, t:t + 1], ALU.mult, ALU.add, [r_ro[t], r_rs2, r_nm2], [r_ro[t]])
                tt("pool", rv, rv, gnt[:, 0, :], ALU.mult, [r_ro[t], r_gnt], [r_ro[t]])
                tt("pool", rv, rv, gnt[:, 1, :], ALU.add, [r_ro[t], r_gnt], [r_ro[t]])
                tt("dve", sgr[:, t, :], rv, sgr[:, t, :], ALU.mult, [r_ro[t], r_sgr[t]], [r_sgr[t]])
            for t4 in range(4):
                bank, rb = nxt("proj")
                bv = bank[:, 0:256].bitcast(BF16)
                for k in range(4):
                    t = t4 * 4 + k
                    tr(bv[:, k * 128:(k + 1) * 128], sgr[:, t, :], identb[:], [r_sgr[t], rconst], [rb])
                cp("act", mixT[:, 0, t4 * 512:(t4 + 1) * 512], bv, [rb], [r_mixR])
            load_next("B")
            if STOP <= 6:
                continue
            p.phase = "ev_out"
            wo, rwo = load_wout(l, u)
            out_proj([mixT[:, 0, :], mixT[:, 1, :]], [r_mixR, r_mixL], wo, rwo)

    load_next("A")
    load_next("B")
    for s in range(nseq):
        xv = x_d[s].rearrange("(t p) d -> p t d", p=128)
        for t in range(NT):
            p.dma("sp", h32[:, t, :], xv[:, t, :], writes=[rh[t]])
        for t in range(NT):
            make_hT(t)
        if not layers:
            for t in range(NT):
                p.dma("sp", out_d[s].rearrange("(t p) d -> p t d", p=128)[:, t, :], h32[:, t, :], reads=[rh[t]])
        for li_, l in enumerate(layers):
            p.barrier()
            prescale()
            if l % 2 == 0:
                even_layer(l)
            else:
                odd_layer(l)
            if STOP <= -1:
                for t in range(NT):
                    p.dma("sp", out_d[s].rearrange("(t p) d -> p t d", p=128)[:, t, :], h32[:, t, :], reads=[rh[t]])
                continue
            layer_norm(l, s, last=(li_ == len(layers) - 1))
    p.wait_all_dma("sp", rh)
    p.emit()
    p.stats["pe_phases"] = [o.get("ph", "") for o in p.ops["pe"] if o["fn"] is not None]
    return nc, p.stats


def host_constants():
    half = 64
    inv_freq = (10000.0 ** (-np.arange(half, dtype=np.float32) / half)).astype(np.float32)
    pos = np.arange(S, dtype=np.float32)
    ang = pos[None, :] * inv_freq[:, None]
    cos = np.cos(ang).astype(np.float32)
    sin = np.sin(ang).astype(np.float32)
    cosT = np.concatenate([cos, cos], axis=0)
    sinS = np.concatenate([-sin, sin], axis=0)
    identf = np.eye(128, dtype=np.float32)
    j = np.arange(128)[:, None]
    i = np.arange(128)[None, :]
    cmask = (i >= j).astype(np.float32)
    g = (1.0 - 2.0 ** (-5.0 - np.arange(8, dtype=np.float64)))
    dmask = np.zeros((128, 8, 128), np.float64)
    for h in range(8):
        dmask[:, h, :] = np.where(i >= j, g[h] ** np.maximum(i - j, 0), 0.0) * SCALE
    gpow = g[None, :] ** (np.arange(128)[:, None] + 1.0)
    kend = (g[None, :] ** (127.0 - np.arange(128)[:, None])) * SCALE
    return dict(cosT=np.ascontiguousarray(cosT), sinS=np.ascontiguousarray(sinS), identf=identf, cmask=cmask,
                dmask=dmask.astype(np.float32), gpow=gpow.astype(np.float32), kend=kend.astype(np.float32))


def rep(a):
    return np.ascontiguousarray(np.broadcast_to(a, (128,) + a.shape))


def host_weights(inp, layers):
    m = {}
    for l in layers:
        j = l // 2
        if l % 2 == 0:
            w = inp["ev_w_in"][j]
            cols = []
            for u in range(8):
                cols.append(np.concatenate([w[:, k * 1024 + u * 128:k * 1024 + (u + 1) * 128] for k in range(6)], axis=1))
            m[f"w_in{l}"] = np.ascontiguousarray(np.stack(cols, 0))
            wo = inp["ev_w_out"][j]
            m[f"w_out{l}"] = np.ascontiguousarray(np.stack(
                [np.concatenate([wo[u * 128:(u + 1) * 128], wo[1024 + u * 128:1024 + (u + 1) * 128]], 0) for u in range(8)], 0))
            m[f"gw{l}"] = np.ascontiguousarray(np.concatenate([inp["ev_gate_a_w"][j], inp["ev_gate_x_w"][j]], axis=2))
            lr = np.stack([inp["ev_conv_w"][j][0], inp["ev_conv_w"][j][1], inp["ev_conv_w"][j][2], inp["ev_conv_w"][j][3],
                           inp["ev_conv_b"][j], inp["ev_gate_a_b"][j], inp["ev_gate_x_b"][j], inp["ev_lru_lambda"][j]], axis=1)
            m[f"lrup{l}"] = np.ascontiguousarray(lr.reshape(8, 128, 8).transpose(1, 0, 2))
            m[f"gn{l}"] = rep(np.stack([inp["ev_ret_gn_g"][j], inp["ev_ret_gn_b"][j]], 0))
            m[f"ln{l}"] = rep(np.stack([inp["ev_ln_g"][j], inp["ev_ln_b"][j]], 0))
        else:
            w = inp["od_w_in"][j]
            cols = []
            for h in range(8):
                q1 = w[:, (2 * h) * 128:(2 * h + 1) * 128]
                q2 = w[:, (2 * h + 1) * 128:(2 * h + 2) * 128]
                k1 = w[:, 2048 + (2 * h) * 128:2048 + (2 * h + 1) * 128]
                k2 = w[:, 2048 + (2 * h + 1) * 128:2048 + (2 * h + 2) * 128]
                v = w[:, 4096 + h * 256:4096 + (h + 1) * 256]
                gg = w[:, 6144 + h * 256:6144 + (h + 1) * 256]
                cols.append(np.concatenate([q1, q2, k1, k2, v, gg], axis=1))
            m[f"w_in{l}"] = np.ascontiguousarray(np.stack(cols, 0))
            m[f"w_out{l}"] = np.ascontiguousarray(inp["od_w_out"][j].reshape(8, 256, 1024))
            m[f"lam{l}"] = rep(np.stack([inp["od_lambda_q1"][j], inp["od_lambda_k1"][j], inp["od_lambda_q2"][j], inp["od_lambda_k2"][j]], 0))
            m[f"sub{l}"] = rep(inp["od_subln_g"][j])
            m[f"ln{l}"] = rep(np.stack([inp["od_ln_g"][j], inp["od_ln_b"][j]], 0))
    return m


_PROGS = {}


def run_layers(x, inp, layers, ncores=NCORES, trace=False):
    B = x.shape[0]
    nseq = B // ncores
    key = (tuple(layers), nseq)
    if key not in _PROGS:
        _PROGS[key] = build_program(layers, nseq)
    nc, stats = _PROGS[key]
    consts = host_constants()
    wts = host_weights(inp, layers)
    in_maps = []
    for c in range(ncores):
        mdict = dict(consts)
        mdict.update(wts)
        mdict["x"] = np.ascontiguousarray(x[c * nseq:(c + 1) * nseq])
        in_maps.append(mdict)
    res = run_bass_kernel_spmd(nc, in_maps, core_ids=list(range(ncores)))
    return np.concatenate([r["out"] for r in res.results], axis=0)


FUSED = True
import os
STOP = int(os.environ.get("KSTOP", "99"))


def kernel(**inputs):
    inp = {k: np.asarray(v) for k, v in inputs.items()}
    x = np.ascontiguousarray(inp["x"], dtype=np.float32)
    if FUSED:
        return run_layers(x, inp, [0, 1, 2, 3]).astype(np.float32)
    h = x
    for l in range(DEPTH):
        h = run_layers(h, inp, [l])
    return h.astype(np.float32)
```

```python
import math
import numpy as np
from contextlib import ExitStack
import concourse.bass as bass
import concourse.mybir as mybir
from concourse.bass_utils import run_bass_kernel_spmd

F32 = mybir.dt.float32
BF16 = mybir.dt.bfloat16
ALU = mybir.AluOpType
AF = mybir.ActivationFunctionType
AX = mybir.AxisListType

NCORES = 8
S = 2048
D = 1024
NT = 16
DEPTH = 4
EPS = 1e-5
ALPHA = (2.0 * DEPTH) ** 0.25
SCALE = 128.0 ** -0.5

ENG = ("pe", "act", "dve", "pool", "sp")
EPOCH = 30000


class Res:
    __slots__ = ("name", "w", "r", "sem", "excl")

    def __init__(self, name, excl=False):
        self.excl = excl
        self.name = name
        self.w = None
        self.r = {}
        self.sem = None


class Prog:
    def __init__(self, nc):
        self.nc = nc
        self.ops = {e: [] for e in ENG}
        self.waited = {e: {} for e in ENG}
        self.dma_cnt = []
        self.stack = ExitStack()
        self.n_sb = 0

    def sb(self, shape, dt, name=None):
        self.n_sb += 1
        return self.stack.enter_context(self.nc.sbuf_tensor("S_" + (name or f"sb{self.n_sb}"), list(shape), dt))

    def ps(self, shape, dt, name=None):
        self.n_sb += 1
        return self.stack.enter_context(self.nc.psum_tensor("P_" + (name or f"ps{self.n_sb}"), list(shape), dt))

    def _need(self, eng, dep, waits):
        if dep is None:
            return
        key = (dep[0], dep[1])
        val = dep[2]
        if self.waited[eng].get(key, -1) >= val:
            return
        if waits.get(key, -1) >= val:
            return
        waits[key] = val

    def op(self, eng, fn, reads=(), writes=()):
        ex = [r for r in reads if r.excl]
        if ex:
            reads = [r for r in reads if not r.excl]
            writes = list(writes) + [r for r in ex if r not in writes]
        idx = len(self.ops[eng])
        waits = {}
        for r in reads:
            d = r.w
            if d is None:
                continue
            if d[0] == "op" and d[1] == eng and eng in ("pe", "sp"):
                continue
            self._need(eng, d, waits)
        for w in writes:
            for d in ([w.w] if w.w is not None else []) + list(w.r.values()):
                if d[0] == "op" and d[1] == eng:
                    continue
                self._need(eng, d, waits)
        for k, v in waits.items():
            self.waited[eng][k] = v
        me = ("op", eng, idx)
        for r in reads:
            r.r[("op", eng)] = me
        for w in writes:
            w.w = me
            w.r = {}
        self.ops[eng].append(dict(fn=fn, waits=waits, dma=None, inc=False, ph=getattr(self, "phase", "")))
        return me

    def dma(self, eng, out, in_, reads=(), writes=(), sem_res=None):
        if sem_res is None:
            sem_res = writes[0] if writes else reads[0]
        if sem_res.sem is None:
            sem_res.sem = len(self.dma_cnt)
            self.dma_cnt.append(0)
        sid = sem_res.sem
        waits = {}
        for r in reads:
            self._need(eng, r.w, waits)
        for w in writes:
            for d in ([w.w] if w.w is not None else []) + list(w.r.values()):
                self._need(eng, d, waits)
        for k, v in waits.items():
            self.waited[eng][k] = v
        self.dma_cnt[sid] += 16
        me = ("dma", sid, self.dma_cnt[sid])
        for r in reads:
            r.r[("dma", sid)] = me
        for w in writes:
            w.w = me
            w.r = {}
        self.ops[eng].append(dict(fn=lambda e, o=out, i=in_: e.dma_start(out=o, in_=i), waits=waits, dma=sid, inc=False))
        return me

    def barrier(self, engines=("pe", "act", "dve", "pool")):
        last = {}
        for e in engines:
            i = len(self.ops[e]) - 1
            while i >= 0 and (self.ops[e][i]["fn"] is None or self.ops[e][i]["dma"] is not None):
                i -= 1
            if i >= 0:
                last[e] = i
        for e in tuple(engines) + ("sp",):
            waits = {}
            for e2, i2 in last.items():
                if e2 != e:
                    self._need(e, ("op", e2, i2), waits)
            for k, v in waits.items():
                self.waited[e][k] = v
            self.ops[e].append(dict(fn=None, waits=waits, dma=None, inc=False))

    def wait_all_dma(self, eng, resources):
        waits = {}
        for r in resources:
            for d in ([r.w] if r.w is not None else []) + list(r.r.values()):
                if d[0] == "dma":
                    self._need(eng, d, waits)
        for k, v in waits.items():
            self.waited[eng][k] = v
        self.ops[eng].append(dict(fn=None, waits=waits, dma=None, inc=False))

    def emit(self):
        nc = self.nc
        for e in ENG:
            for o in self.ops[e]:
                for (kind, src), val in o["waits"].items():
                    if kind == "op":
                        self.ops[src][val]["inc"] = True
        semval = {e: {} for e in ENG}
        nsem = {}
        for e in ENG:
            c = 0
            for i, o in enumerate(self.ops[e]):
                if o["inc"]:
                    semval[e][i] = (c // EPOCH, c % EPOCH + 1)
                    c += 1
            nsem[e] = (c + EPOCH - 1) // EPOCH
        st = self.stack
        esems = {e: [st.enter_context(nc.semaphore(f"s_{e}{k}")) for k in range(nsem[e])] for e in ENG}
        dsems = [st.enter_context(nc.semaphore(f"d{k}")) for k in range(len(self.dma_cnt))]
        self.stats = {e: (len(self.ops[e]), sum(len(o["waits"]) for o in self.ops[e])) for e in ENG}
        block = st.enter_context(nc.Block())

        def runner(e):
            def run(eng):
                for i, o in enumerate(self.ops[e]):
                    for (kind, src), val in o["waits"].items():
                        if kind == "op":
                            ep, v = semval[src][val]
                            eng.wait_ge(esems[src][ep], v)
                        else:
                            eng.wait_ge(dsems[src], val)
                    if o["fn"] is None:
                        continue
                    ins = o["fn"](eng)
                    if o["dma"] is not None:
                        ins.then_inc(dsems[o["dma"]], 16)
                    elif o["inc"]:
                        ep, v = semval[e][i]
                        ins.then_inc(esems[e][ep], 1)
            return run

        block.tensor(runner("pe"))
        block.scalar(runner("act"))
        block.vector(runner("dve"))
        block.gpsimd(runner("pool"))
        block.sync(runner("sp"))
        st.close()


def lambda_init(l):
    return 0.8 - 0.6 * math.exp(-0.3 * l)


def build_program(layers, nseq, first_is_input=True):
    nc = bass.Bass("TRN2", target_bir_lowering=False)
    p = Prog(nc)

    def din(name, shape):
        return nc.dram_tensor(name, list(shape), F32, kind="ExternalInput").ap()

    x_d = din("x", [nseq, S, D])
    out_d = nc.dram_tensor("out", [nseq, S, D], F32, kind="ExternalOutput").ap()
    cos_d = din("cosT", [128, S])
    sin_d = din("sinS", [128, S])
    identf_d = din("identf", [128, 128])
    cmask_d = din("cmask", [128, 128])
    dmask_d = din("dmask", [128, 8, 128])
    gpow_d = din("gpow", [128, 8])
    kend_d = din("kend", [128, 8])
    wd = {}
    for l in layers:
        if l % 2 == 0:
            wd[l] = dict(w_in=din(f"w_in{l}", [8, D, 768]), w_out=din(f"w_out{l}", [8, 256, D]),
                         gw=din(f"gw{l}", [8, 128, 256]), lrup=din(f"lrup{l}", [128, 8, 8]),
                         gn=din(f"gn{l}", [128, 2, D]), ln=din(f"ln{l}", [128, 2, D]))
        else:
            wd[l] = dict(w_in=din(f"w_in{l}", [8, D, 1024]), w_out=din(f"w_out{l}", [8, 256, D]),
                         lam=din(f"lam{l}", [128, 4, 128]), sub=din(f"sub{l}", [128, 256]),
                         ln=din(f"ln{l}", [128, 2, D]))

    h32 = p.sb([128, NT, D], F32, "h32")
    rh = [Res(f"h32_{t}") for t in range(NT)]
    hT = p.sb([128, 8, S], BF16, "hT")
    rhT = [Res(f"hT_{t}") for t in range(NT)]
    slabA = p.sb([128, 8, 512], BF16, "slabA"); rslabA = Res("slabA")
    slabB = p.sb([128, 8, 512], BF16, "slabB"); rslabB = Res("slabB")
    wout = [p.sb([128, 2, D], BF16, f"wout{i}") for i in range(1)]
    rwout = [Res(f"wout{i}") for i in range(1)]
    qk = p.sb([128, 4, S], BF16, "qk")
    rqk = [Res(f"qk{i}") for i in range(4)]
    cosT = p.sb([128, S], F32, "cosT"); sinS = p.sb([128, S], F32, "sinS")
    rconst = Res("const")
    identf = p.sb([128, 128], F32, "identf")
    identb = p.sb([128, 128], BF16, "identb")
    cmask = p.sb([128, 128], BF16, "cmask")
    gpow = p.sb([128, 8], F32, "gpow")
    kend = p.sb([128, 8], F32, "kend")
    epst = p.sb([128, 1], F32, "eps")
    onet = p.sb([128, 1], F32, "one")
    rxs = [p.sb([128, 512], F32, f"rxs{i}") for i in range(2)]; r_rxs = [Res(f"rxs{i}") for i in range(2)]
    rtt = [p.sb([128, 512], F32, f"rtt{i}") for i in range(2)]; r_rtt = [Res(f"rtt{i}") for i in range(2)]
    ARENA = 44800
    arena = p.sb([128, ARENA // 4], F32, "arena")
    st12 = p.sb([128, NT, 12], F32, "st12"); r_st12 = Res("st12")
    mvall = p.sb([128, NT, 2], F32, "mvall"); r_mv = Res("mvall")
    rstd = p.sb([128, NT], F32, "rstd"); r_rstd = Res("rstd")
    nmr = p.sb([128, NT], F32, "nmr"); r_nmr = Res("nmr")
    prm = {}
    for l in layers:
        if l % 2 == 0:
            prm[l] = dict(lrup=p.sb([128, 8, 8], F32, f"lrup{l}"), c=p.sb([128, 8], F32, f"c{l}"),
                          c2=p.sb([128, 8], F32, f"c2{l}"), cn=p.sb([128, 8], F32, f"cn{l}"), r=Res(f"prm{l}"))
        else:
            prm[l] = dict(lamt=rxs[0][:].rearrange("p (a b) -> p a b", a=4), nlam=p.sb([128, 1], F32, f"nlam{l}"),
                          sub2=p.sb([128, 256], F32, f"sub2{l}"), tmp=rtt[0][:, 0:256].rearrange("p (a b) -> p a b", a=2),
                          s12=p.sb([128, 2], F32, f"s12{l}"), r=Res(f"prm{l}"))

    banks = [p.ps([128, 512], F32, f"bank{i}") for i in range(8)]
    rbank = [Res(f"bank{i}", excl=True) for i in range(8)]
    grp = {"proj": [0, 1, 2, 3], "st": [0, 1, 2], "o": [4, 5, 6, 7]}
    gcnt = {k: 0 for k in grp}

    def nxt(g):
        i = grp[g][gcnt[g] % len(grp[g])]
        gcnt[g] += 1
        return banks[i], rbank[i]

    def mm(out, lhsT, rhs, start, stop, reads, writes):
        p.op("pe", lambda e: e.matmul(out, lhsT=lhsT, rhs=rhs, start=start, stop=stop), reads, writes)

    def tr(out, in_, ident, reads, writes):
        p.op("pe", lambda e: e.transpose(out=out, in_=in_, identity=ident), reads, writes)

    def act(out, in_, func, reads, writes, bias=None, scale=1.0, accum=None):
        kw = {}
        if bias is not None:
            kw["bias"] = bias
        if accum is not None:
            kw["accum_out"] = accum
        p.op("act", lambda e: e.activation(out=out, in_=in_, func=func, scale=scale, **kw), reads, writes)

    def tt(eng, out, in0, in1, op, reads, writes):
        p.op(eng, lambda e: e.tensor_tensor(out=out, in0=in0, in1=in1, op=op), reads, writes)

    def ts(eng, out, in0, s1, s2, op0, op1, reads, writes):
        if s2 is None:
            p.op(eng, lambda e: e.tensor_single_scalar(out=out, in_=in0, scalar=s1, op=op0), reads, writes)
        else:
            p.op(eng, lambda e: e.tensor_scalar(out=out, in0=in0, scalar1=s1, scalar2=s2, op0=op0, op1=op1), reads, writes)

    def stt(eng, out, in0, scalar, in1, op0, op1, reads, writes):
        p.op(eng, lambda e: e.scalar_tensor_tensor(out=out, in0=in0, scalar=scalar, in1=in1, op0=op0, op1=op1), reads, writes)

    def cp(eng, out, in_, reads, writes):
        if eng == "act":
            p.op("act", lambda e: e.copy(out=out, in_=in_), reads, writes)
        else:
            p.op(eng, lambda e: e.tensor_copy(out=out, in_=in_), reads, writes)

    p.dma("sp", cosT[:], cos_d, writes=[rconst])
    p.dma("sp", sinS[:], sin_d, writes=[rconst])
    p.dma("sp", identf[:], identf_d, writes=[rconst])
    p.dma("pool", identb[:], identf_d, writes=[rconst])
    p.dma("pool", cmask[:], cmask_d, writes=[rconst])
    p.dma("sp", gpow[:], gpow_d, writes=[rconst])
    p.dma("sp", kend[:], kend_d, writes=[rconst])
    r_eps = Res("eps")
    p.op("dve", lambda e: e.memset(epst[:], EPS), writes=[r_eps])
    p.op("dve", lambda e: e.memset(onet[:], 1.0), writes=[r_eps])
    for l in layers:
        pr = prm[l]
        if STOP <= -2:
            continue
        if l % 2 == 0:
            p.dma("sp", pr["lrup"][:], wd[l]["lrup"], writes=[pr["r"]])
            act(pr["c"][:], pr["lrup"][:, :, 7], AF.Exp, [pr["r"]], [pr["r"]], scale=-1.0)
            act(pr["c"][:], pr["c"][:], AF.Ln, [pr["r"], r_eps], [pr["r"]], bias=onet[:])
            ts("dve", pr["cn"][:], pr["c"][:], 8.0, None, ALU.mult, None, [pr["r"]], [pr["r"]])
            ts("dve", pr["c"][:], pr["cn"][:], -1.0, None, ALU.mult, None, [pr["r"]], [pr["r"]])
            ts("dve", pr["c2"][:], pr["c"][:], 2.0, None, ALU.mult, None, [pr["r"]], [pr["r"]])
        else:
            li = lambda_init(l)
            p.dma("sp", pr["lamt"], wd[l]["lam"], writes=[r_rxs[0]])
            p.dma("sp", pr["sub2"][:], wd[l]["sub"], writes=[pr["r"]])
            tt("dve", pr["tmp"][:, 0, :], pr["lamt"][:, 0, :], pr["lamt"][:, 1, :], ALU.mult, [r_rxs[0]], [r_rtt[0]])
            tt("dve", pr["tmp"][:, 1, :], pr["lamt"][:, 2, :], pr["lamt"][:, 3, :], ALU.mult, [r_rxs[0]], [r_rtt[0]])
            p.op("dve", lambda e, pr=pr: e.reduce_sum(out=pr["s12"][:], in_=pr["tmp"], axis=AX.X), [r_rtt[0]], [pr["r"]])
            act(pr["s12"][:], pr["s12"][:], AF.Exp, [pr["r"]], [pr["r"]])
            tt("dve", pr["nlam"][:], pr["s12"][:, 1:2], pr["s12"][:, 0:1], ALU.subtract, [pr["r"]], [pr["r"]])
            ts("dve", pr["nlam"][:], pr["nlam"][:], -li, None, ALU.add, None, [pr["r"]], [pr["r"]])
            ts("dve", pr["sub2"][:], pr["sub2"][:], 1.0 - li, None, ALU.mult, None, [pr["r"]], [pr["r"]])

    sched = []
    for s_ in range(nseq):
        for l_ in layers:
            for u_ in range(8):
                sched.append((l_, u_))
    pos = {"A": 0, "B": 0}

    def load_next(which):
        i = pos[which]
        pos[which] += 1
        if i >= len(sched):
            return
        l, u = sched[i]
        src = wd[l]["w_in"][u].rearrange("(kc p) c -> p kc c", p=128)
        if which == "A":
            dst, rdst, c0, w = slabA, rslabA, 0, 512
        else:
            dst, rdst, c0, w = slabB, rslabB, 512, (512 if l % 2 == 1 else 256)
        for kc0 in range(0, 8, 4):
            p.dma("pool", dst[:, kc0:kc0 + 4, 0:w], src[:, kc0:kc0 + 4, c0:c0 + w], writes=[rdst])

    def load_wout(l, u):
        src = wd[l]["w_out"][u].rearrange("(c p) n -> p c n", p=128)
        p.dma("pool", wout[0][:], src, writes=[rwout[0]])
        return wout[0], rwout[0]

    rope_ctr = [0]

    def rope(bank, rb, dst, rdst, G):
        i = rope_ctr[0] % 2
        rope_ctr[0] += 1
        xs, r1 = rxs[i], r_rxs[i]
        t1, r2 = rtt[i], r_rtt[i]
        cs = cosT[:, G * 512:(G + 1) * 512]
        sn = sinS[:, G * 512:(G + 1) * 512]
        p.op("act", lambda e: e.copy(out=xs[0:64, :], in_=bank[64:128, :]), [rb], [r1])
        p.op("act", lambda e: e.copy(out=xs[64:128, :], in_=bank[0:64, :]), [rb], [r1])
        tt("dve", t1[:], bank[:], cs, ALU.mult, [rb, rconst], [r2])
        tt("pool", xs[:], xs[:], sn, ALU.mult, [r1, rconst], [r1])
        tt("dve", dst, t1[:], xs[:], ALU.add, [r1, r2], [rdst])

    def proj_feat(sl, rsl, c0, G):
        bank, rb = nxt("proj")
        for kc in range(8):
            mm(bank[:], sl[:, kc, c0:c0 + 128], hT[:, kc, G * 512:(G + 1) * 512], kc == 0, kc == 7,
               [rsl] + rhT[4 * G:4 * G + 4], [rb])
        return bank, rb

    def proj_tok(sl, rsl, c0, width, t):
        bank, rb = nxt("proj")
        for kc in range(8):
            mm(bank[:, 0:width], hT[:, kc, t * 128:(t + 1) * 128], sl[:, kc, c0:c0 + width], kc == 0, kc == 7,
               [rsl, rhT[t]], [rb])
        return bank, rb

    def out_proj(mixT_views, rmix, wo, rwo, scale=None, rscale=None):
        k = 0
        for t in range(NT):
            for cg in range(2):
                bank, rb = nxt("proj")
                for c in range(2):
                    mm(bank[:], mixT_views[c][:, t * 128:(t + 1) * 128], wo[:, c, cg * 512:(cg + 1) * 512],
                       c == 0, c == 1, [rwo] + rmix, [rb])
                hv = h32[:, t, cg * 512:(cg + 1) * 512]
                extra = [rscale] if rscale is not None else []
                if k % 2 == 0:
                    if scale is None:
                        tt("dve", hv, hv, bank[:], ALU.add, [rb, rh[t]], [rh[t]])
                    else:
                        stt("dve", hv, bank[:], scale[:, t:t + 1], hv, ALU.mult, ALU.add, [rb, rh[t]] + extra, [rh[t]])
                else:
                    i = (k // 2) % 2
                    if scale is None:
                        cp("act", rxs[i][:], bank[:], [rb], [r_rxs[i]])
                    else:
                        act(rxs[i][:], bank[:], AF.Copy, [rb] + extra, [r_rxs[i]], scale=scale[:, t:t + 1])
                    tt("pool", hv, hv, rxs[i][:], ALU.add, [r_rxs[i], rh[t]], [rh[t]])
                k += 1

    def prescale():
        for t in range(NT):
            ts("dve", h32[:, t, :], h32[:, t, :], ALPHA, None, ALU.mult, None, [rh[t]], [rh[t]])

    def make_hT(t):
        for half in range(2):
            bank, rb = nxt("proj")
            for k in range(4):
                kc = half * 4 + k
                tr(bank[:, k * 128:(k + 1) * 128], h32[:, t, kc * 128:(kc + 1) * 128], identf[:], [rh[t], rconst], [rb])
            dst = hT[:, half * 4:half * 4 + 4, t * 128:(t + 1) * 128]
            src = bank[:].rearrange("p (k n) -> p k n", k=4)
            cp("act", dst, src, [rb], [rhT[t]])

    def layer_norm(l, s, last):
        p.phase = "ln"
        lnt = qk[:, 0:2, :].bitcast(F32)
        p.dma("sp", lnt, wd[l]["ln"], writes=[rqk[0], rqk[1]])
        rl = [rqk[0], rqk[1]]
        for t in range(NT):
            for hf in range(2):
                p.op("dve", lambda e, t=t, hf=hf: e.bn_stats(out=st12[:, t, hf * 6:(hf + 1) * 6], in_=h32[:, t, hf * 512:(hf + 1) * 512]),
                     [rh[t]], [r_st12])
            p.op("dve", lambda e, t=t: e.bn_aggr(out=mvall[:, t, :], in_=st12[:, t, :]), [r_st12], [r_mv])
        act(rstd[:], mvall[:, :, 1], AF.Sqrt, [r_mv, r_eps], [r_rstd], bias=epst[:])
        p.op("dve", lambda e: e.reciprocal(out=rstd[:], in_=rstd[:]), [r_rstd], [r_rstd])
        stt("dve", nmr[:], mvall[:, :, 0], -1.0, rstd[:], ALU.mult, ALU.mult, [r_mv, r_rstd], [r_nmr])
        for t in range(NT):
            hv = h32[:, t, :]
            ts("dve", hv, hv, rstd[:, t:t + 1], nmr[:, t:t + 1], ALU.mult, ALU.add, [rh[t], r_rstd, r_nmr], [rh[t]])
            tt("pool", hv, hv, lnt[:, 0, :], ALU.mult, [rh[t]] + rl, [rh[t]])
            tt("dve", hv, hv, lnt[:, 1, :], ALU.add, [rh[t]] + rl, [rh[t]])
            if last:
                p.dma("sp", out_d[s].rearrange("(t p) d -> p t d", p=128)[:, t, :], hv, reads=[rh[t]])
            else:
                make_hT(t)

    def odd_layer(l):
        grp.update({"proj": [0, 1, 2, 3], "st": [0, 1, 2], "o": [4, 5, 6, 7]})
        pr = prm[l]
        off = [0]

        def carve(nbytes, dt, shape):
            a = arena[:, off[0] // 4:(off[0] + nbytes) // 4]
            off[0] += nbytes
            if dt == BF16:
                a = a.bitcast(BF16)
            if len(shape) == 3:
                a = a.rearrange("p (a b) -> p a b", a=shape[1])
            return a

        vaug = carve(NT * 264 * 2, BF16, [128, NT, 264]); r_v = Res("vaug")
        sg = carve(NT * 256 * 2, BF16, [128, NT, 256]); r_sg = Res("sg")
        dd = carve(4 * 256 * 4, F32, [128, 4, 256]); r_dd = [Res(f"dd{i}") for i in range(4)]
        pTs = [carve(512 * 2, BF16, [128, 512]) for _ in range(3)]; r_pT = [Res(f"pT{i}") for i in range(3)]
        mixb = [carve(256 * 2, BF16, [128, 256]) for _ in range(2)]; r_mixb = [Res(f"mixb{i}") for i in range(2)]
        mtmp = [carve(256 * 4, F32, [128, 256]) for _ in range(2)]; r_mtmp = [Res(f"mtmp{i}") for i in range(2)]
        mixT = carve(2 * S * 2, BF16, [128, 2, S]); r_mixT = Res("mixT")
        sq = carve(256 * 4, F32, [128, 256]); r_sq = Res("sqj")
        ss = carve(16 * 4, F32, [128, 16]); r_ss = Res("ss")
        rec = carve(16 * 4, F32, [128, 16]); r_rec = Res("rec")
        assert off[0] <= ARENA, off[0]
        p.op("pool", lambda e: e.memset(vaug[:, :, 256:257], 1.0), [], [r_v])
        pT_ctr = [0]
        mix_ctr = [0]
        def odd_qk():
            p.phase = "odd_qk"
            for c in range(4):
                for G in range(4):
                    bank, rb = proj_feat(slabA, rslabA, c * 128, G)
                    rope(bank, rb, qk[:, c, G * 512:(G + 1) * 512], rqk[c], G)
            load_next("A")

        odd_qk()
        for h in range(8):
            p.phase = "odd_vg"
            for t in range(NT):
                bank, rb = proj_tok(slabB, rslabB, 0, 512, t)
                cp("dve", vaug[:, t, 0:256], bank[:, 0:256], [rb], [r_v])
                act(sg[:, t, :], bank[:, 256:512], AF.Silu, [rb], [r_sg])
            load_next("B")
            wo, rwo = load_wout(l, h)
            for G in range(4):
                p.phase = "odd_attn"
                nk = 4 * G + 4
                for m in range(2):
                    qc, kc_ = m, 2 + m
                    obs = [nxt("o") for _ in range(4)]
                    stb = {}

                    def emit_s(kt):
                        q0 = max(4 * G, kt)
                        o_ = (q0 - 4 * G) * 128
                        bank, rb = nxt("st")
                        mm(bank[:, o_:512], qk[:, kc_, kt * 128:(kt + 1) * 128], qk[:, qc, G * 512 + o_:(G + 1) * 512],
                           True, True, [rqk[kc_], rqk[qc]], [rb])
                        stb[kt] = (bank, rb, q0, o_)

                    emit_s(0)
                    emit_s(1)
                    for kt in range(nk):
                        if kt + 2 < nk:
                            emit_s(kt + 2)
                        bank, rb, q0, o_ = stb.pop(kt)
                        i = pT_ctr[0] % 3
                        pT_ctr[0] += 1
                        pT, rp = pTs[i], r_pT[i]
                        act(pT[:, o_:512], bank[:, o_:512], AF.Exp, [rb], [rp], scale=SCALE)
                        if kt >= 4 * G:
                            tt("dve", pT[:, o_:o_ + 128], pT[:, o_:o_ + 128], cmask[:], ALU.mult, [rp, rconst], [rp])
                        for qt in range(q0, 4 * G + 4):
                            qi = qt - 4 * G
                            ob, rob = obs[qi]
                            mm(ob[:, 0:257], pT[:, qi * 128:(qi + 1) * 128], vaug[:, kt, 0:257], kt == 0, kt == qt,
                               [rp, r_v], [rob])
                    for qi in range(4):
                        ob, rob = obs[qi]
                        t = 4 * G + qi
                        col = m * 8 + qi
                        p.op("dve", lambda e, ob=ob, col=col: e.reciprocal(out=rec[:, col:col + 1], in_=ob[:, 256:257]), [rob], [r_rec])
                        if m == 0:
                            ts("dve", dd[:, qi, :], ob[:, 0:256], rec[:, col:col + 1], None, ALU.mult, None, [rob, r_rec], [r_dd[qi]])
                        else:
                            tt("dve", rec[:, col + 4:col + 5], rec[:, col:col + 1], pr["nlam"][:], ALU.mult, [r_rec, pr["r"]], [r_rec])
                            stt("dve", dd[:, qi, :], ob[:, 0:256], rec[:, col + 4:col + 5], dd[:, qi, :], ALU.mult, ALU.add,
                                [rob, r_rec, r_dd[qi]], [r_dd[qi]])
                p.phase = "odd_fin"
                for qi in range(4):
                    t = 4 * G + qi
                    i = mix_ctr[0] % 2
                    mix_ctr[0] += 1
                    tt("dve", sq[:], dd[:, qi, :], dd[:, qi, :], ALU.mult, [r_dd[qi]], [r_sq])
                    p.op("dve", lambda e, t=t: e.reduce_sum(out=ss[:, t:t + 1], in_=sq[:], axis=AX.X), [r_sq], [r_ss])
                    tt("dve", mtmp[i][:], dd[:, qi, :], pr["sub2"][:], ALU.mult, [r_dd[qi], pr["r"]], [r_mtmp[i]])
                    tt("dve", mixb[i][:], mtmp[i][:], sg[:, t, :], ALU.mult, [r_mtmp[i], r_sg], [r_mixb[i]])
                    bank, rb = nxt("proj")
                    bv = bank[:, 0:128].bitcast(BF16)
                    for c in range(2):
                        tr(bv[:, c * 128:(c + 1) * 128], mixb[i][:, c * 128:(c + 1) * 128], identb[:], [r_mixb[i], rconst], [rb])
                    cp("act", mixT[:, :, t * 128:(t + 1) * 128], bv.rearrange("p (c n) -> p c n", c=2), [rb], [r_mixT])
            if h + 1 < 8:
                odd_qk()
            p.phase = "odd_out"
            act(ss[:], ss[:], AF.Sqrt, [r_ss, r_eps], [r_ss], bias=epst[:], scale=1.0 / 256.0)
            p.op("dve", lambda e: e.reciprocal(out=ss[:], in_=ss[:]), [r_ss], [r_ss])
            out_proj([mixT[:, 0, :], mixT[:, 1, :]], [r_mixT], wo, rwo, scale=ss, rscale=r_ss)

    def even_layer(l):
        grp.update({"proj": [0, 1, 2, 3], "st": [4, 5], "o": [6, 7]})
        pr = prm[l]
        off = [0]

        def carve(nbytes, dt, shape):
            a = arena[:, off[0] // 4:(off[0] + nbytes) // 4]
            off[0] += nbytes
            if dt == BF16:
                a = a.bitcast(BF16)
            if len(shape) == 3:
                a = a.rearrange("p (a b) -> p a b", a=shape[1])
            return a

        kE = carve(NT * 128 * 2, BF16, [128, NT, 128]); r_kE = Res("kE")
        vv = carve(NT * 128 * 2, BF16, [128, NT, 128]); r_vv = Res("vv")
        sgr = carve(NT * 128 * 2, BF16, [128, NT, 128]); r_sgr = [Res(f"sgr{t}") for t in range(NT)]
        Rbs = [carve(128 * 2, BF16, [128, 128]) for _ in range(3)]; r_Rbs = [Res(f"Rb{i}") for i in range(3)]
        R32 = carve(128 * 4, F32, [128, 128]); r_R32 = Res("R32")
        ro = qk[:, 2:4, :].bitcast(F32).rearrange("p a (b c) -> p (a b) c", c=128)
        r_ro = [Res(f"ro{t}") for t in range(NT)]
        dmk = carve(128 * 4, F32, [128, 128]); r_dmk = Res("dmk")
        scs = [carve(128 * 2, BF16, [128, 128]) for _ in range(3)]; r_sc = [Res(f"sc{i}") for i in range(3)]
        mixT = carve(2 * S * 2, BF16, [128, 2, S]); r_mixR = Res("mixTr"); r_mixL = Res("mixTl")
        gnt = carve(2 * 128 * 4, F32, [128, 2, 128]); r_gnt = Res("gnt")
        gwt = carve(256 * 2, BF16, [128, 256]); r_gwt = Res("gwt")
        st6 = carve(NT * 6 * 4, F32, [128, NT, 6]); r_st6 = Res("st6")
        mv2 = carve(NT * 2 * 4, F32, [128, NT, 2]); r_mv2 = Res("mv2")
        rs2 = carve(NT * 4, F32, [128, NT]); r_rs2 = Res("rs2")
        nm2 = carve(NT * 4, F32, [128, NT]); r_nm2 = Res("nm2")
        xb32 = carve(516 * 4, F32, [128, 516]); r_xb = Res("xb32")
        names = ["xc", "rr", "ig", "aa", "a2", "hh0", "hh1"]
        lt = {n: carve(512 * 4, F32, [128, 512]) for n in names}
        rlt = {n: Res(n) for n in names}
        xcb = carve(512 * 2, BF16, [128, 512]); r_xcb = Res("xcb")
        sgl = carve(512 * 4, F32, [128, 512]); r_sgl = Res("sgl")
        assert off[0] <= ARENA, off[0]
        sc_ctr = [0]

        def ev_qk():
            p.phase = "ev_qk"
            for c in range(2):
                for G in range(4):
                    bank, rb = proj_feat(slabA, rslabA, c * 128, G)
                    rope(bank, rb, qk[:, c, G * 512:(G + 1) * 512], rqk[c], G)

        for u in range(8):
            gC = float((1.0 - 2.0 ** (-5.0 - u)) ** 128)
            p.dma("sp", gnt[:], wd[l]["gn"][:, :, u * 128:(u + 1) * 128], writes=[r_gnt])
            p.dma("pool", gwt[:], wd[l]["gw"][u], writes=[r_gwt])
            p.dma("sp", dmk[:], dmask_d[:, u, :], writes=[r_dmk])
            if u == 0:
                ev_qk()
            p.phase = "ev_vg"
            for t in range(NT):
                bank, rb = proj_tok(slabA, rslabA, 256, 256, t)
                cp("dve", vv[:, t, :], bank[:, 0:128], [rb], [r_vv])
                act(sgr[:, t, :], bank[:, 128:256], AF.Silu, [rb], [r_sgr[t]])
            load_next("A")
            if STOP <= 2:
                continue
            p.phase = "ev_kE"
            for t4 in range(4):
                bank, rb = nxt("proj")
                bv = bank[:, 0:256].bitcast(BF16)
                for k in range(4):
                    t = t4 * 4 + k
                    tr(bv[:, k * 128:(k + 1) * 128], qk[:, 1, t * 128:(t + 1) * 128], identb[:], [rqk[1], rconst], [rb])
                ts("dve", kE[:, t4 * 4:t4 * 4 + 4, :], bv.rearrange("p (k n) -> p k n", k=4), kend[:, u:u + 1], None, ALU.mult, None,
                   [rb, rconst], [r_kE])
            if STOP <= 3:
                continue
            L = pr["lrup"]
            p.op("dve", lambda e: e.memset(xb32[:, 0:3], 0.0), [], [r_xb])

            def lru_stage(G, k):
                hh, r_hh = lt[f"hh{G % 2}"], rlt[f"hh{G % 2}"]
                hp, r_hp = lt[f"hh{(G + 1) % 2}"], rlt[f"hh{(G + 1) % 2}"]
                xc = lt["xc"]
                if k == 0:
                    if G > 0:
                        cp("dve", xb32[:, 0:3], xb32[:, 512:515], [r_xb], [r_xb])
                    bank, rb = proj_feat(slabB, rslabB, 0, G)
                    cp("act", xb32[:, 3:515], bank[:], [rb], [r_xb])
                    bank, rb = proj_feat(slabB, rslabB, 128, G)
                    act(sgl[:], bank[:], AF.Silu, [rb], [r_sgl])
                    ts("dve", xc[:], xb32[:, 3:515], L[:, u, 3:4], L[:, u, 4:5], ALU.mult, ALU.add, [r_xb, pr["r"]], [rlt["xc"]])
                    for w in (2, 1, 0):
                        stt("dve", xc[:], xb32[:, w:w + 512], L[:, u, w:w + 1], xc[:], ALU.mult, ALU.add, [r_xb, pr["r"], rlt["xc"]], [rlt["xc"]])
                    cp("pool", xcb[:], xc[:], [rlt["xc"]], [r_xcb])
                elif k == 1:
                    bank, rb = nxt("proj")
                    mm(bank[:], gwt[:, 0:128], xcb[:], True, True, [r_gwt, r_xcb], [rb])
                    act(lt["rr"][:], bank[:], AF.Sigmoid, [rb, pr["r"]], [rlt["rr"]], bias=L[:, u, 5:6])
                    bank, rb = nxt("proj")
                    mm(bank[:], gwt[:, 128:256], xcb[:], True, True, [r_gwt, r_xcb], [rb])
                    act(lt["ig"][:], bank[:], AF.Sigmoid, [rb, pr["r"]], [rlt["ig"]], bias=L[:, u, 6:7])
                elif k == 2:
                    act(lt["aa"][:], lt["rr"][:], AF.Exp, [rlt["rr"], pr["r"]], [rlt["aa"]], scale=pr["c"][:, u:u + 1])
                    act(lt["a2"][:], lt["rr"][:], AF.Exp, [rlt["rr"], pr["r"]], [rlt["a2"]], scale=pr["c2"][:, u:u + 1])
                    act(lt["rr"][:], lt["rr"][:], AF.Tanh, [rlt["rr"], pr["r"]], [rlt["rr"]], scale=pr["cn"][:, u:u + 1])
                    stt("dve", lt["a2"][:], lt["a2"][:], 1.0, lt["rr"][:], ALU.add, ALU.mult, [rlt["a2"], rlt["rr"]], [rlt["a2"]])
                    tt("pool", lt["ig"][:], lt["ig"][:], xc[:], ALU.mult, [rlt["ig"], rlt["xc"]], [rlt["ig"]])
                else:
                    act(lt["a2"][:], lt["a2"][:], AF.Sqrt, [rlt["a2"]], [rlt["a2"]])
                    tt("dve", lt["ig"][:], lt["ig"][:], lt["a2"][:], ALU.mult, [rlt["ig"], rlt["a2"]], [rlt["ig"]])
                    init = 0.0 if G == 0 else hp[:, 511:512]
                    p.op("dve", lambda e, hh=hh, init=init: e.tensor_tensor_scan(out=hh[:], data0=lt["aa"][:], data1=lt["ig"][:], initial=init,
                                                                                  op0=ALU.mult, op1=ALU.add),
                         [rlt["aa"], rlt["ig"], r_hp], [r_hh])
                    tt("pool", mixT[:, 1, G * 512:(G + 1) * 512], hh[:], sgl[:], ALU.mult, [r_hh, r_sgl], [r_mixL])

            p.phase = "ev_ret"
            stbs = {}

            def emit_a(t):
                bank, rb = nxt("st")
                mm(bank[:, 0:128], qk[:, 1, t * 128:(t + 1) * 128], qk[:, 0, t * 128:(t + 1) * 128], True, True, [rqk[0], rqk[1]], [rb])
                mm(bank[:, 128:256], kE[:, t, :], vv[:, t, :], True, True, [r_kE, r_vv], [rb])
                stbs[t] = (bank, rb)

            emit_a(0)
            for t in range(NT):
                if t + 1 < NT:
                    emit_a(t + 1)
                bank, rb = stbs.pop(t)
                i = sc_ctr[0] % 3
                sc_ctr[0] += 1
                sc, rsc = scs[i], r_sc[i]
                tt("dve", sc[:], bank[:, 0:128], dmk[:], ALU.mult, [rb, r_dmk], [rsc])
                if t + 1 < NT:
                    if t == 0:
                        cp("act", R32[:], bank[:, 128:256], [rb], [r_R32])
                    else:
                        stt("dve", R32[:], R32[:], gC, bank[:, 128:256], ALU.mult, ALU.add, [rb, r_R32], [r_R32])
                    cp("act", Rbs[(t + 1) % 3][:], R32[:], [r_R32], [r_Rbs[(t + 1) % 3]])
                ob, rob = nxt("o")
                mm(ob[:, 0:128], sc[:], vv[:, t, :], True, True, [rsc, r_vv], [rob])
                if t > 0:
                    mm(ob[:, 128:256], qk[:, 0, t * 128:(t + 1) * 128], Rbs[t % 3][:], True, True, [rqk[0], r_Rbs[t % 3]], [rob])
                cp("act", ro[:, t, :], ob[:, 0:128], [rob], [r_ro[t]])
                if t > 0:
                    stt("dve", ro[:, t, :], ob[:, 128:256], gpow[:, u:u + 1], ro[:, t, :], ALU.mult, ALU.add,
                        [rob, r_ro[t], rconst], [r_ro[t]])
                p.op("dve", lambda e, t=t: e.bn_stats(out=st6[:, t, :], in_=ro[:, t, :]), [r_ro[t]], [r_st6])
                p.op("dve", lambda e, t=t: e.bn_aggr(out=mv2[:, t, :], in_=st6[:, t, :]), [r_st6], [r_mv2])
                lru_stage(t // 4, t % 4)
            if u + 1 < 8:
                ev_qk()
            p.phase = "ev_gn"
            act(rs2[:], mv2[:, :, 1], AF.Sqrt, [r_mv2, r_eps], [r_rs2], bias=epst[:])
            p.op("dve", lambda e: e.reciprocal(out=rs2[:], in_=rs2[:]), [r_rs2], [r_rs2])
            stt("dve", nm2[:], mv2[:, :, 0], -1.0, rs2[:], ALU.mult, ALU.mult, [r_mv2, r_rs2], [r_nm2])
            for t in range(NT):
                rv = ro[:, t, :]
                ts("dve", rv, rv, rs2[:, t:t + 1], nm2[:, t:t + 1], ALU.mult, ALU.add, [r_ro[t], r_rs2, r_nm2], [r_ro[t]])
                tt("pool", rv, rv, gnt[:, 0, :], ALU.mult, [r_ro[t], r_gnt], [r_ro[t]])
                tt("pool", rv, rv, gnt[:, 1, :], ALU.add, [r_ro[t], r_gnt], [r_ro[t]])
                tt("dve", sgr[:, t, :], rv, sgr[:, t, :], ALU.mult, [r_ro[t], r_sgr[t]], [r_sgr[t]])
            for t4 in range(4):
                bank, rb = nxt("proj")
                bv = bank[:, 0:256].bitcast(BF16)
                for k in range(4):
                    t = t4 * 4 + k
                    tr(bv[:, k * 128:(k + 1) * 128], sgr[:, t, :], identb[:], [r_sgr[t], rconst], [rb])
                cp("act", mixT[:, 0, t4 * 512:(t4 + 1) * 512], bv, [rb], [r_mixR])
            load_next("B")
            if STOP <= 6:
                continue
            p.phase = "ev_out"
            wo, rwo = load_wout(l, u)
            out_proj([mixT[:, 0, :], mixT[:, 1, :]], [r_mixR, r_mixL], wo, rwo)

    load_next("A")
    load_next("B")
    for s in range(nseq):
        xv = x_d[s].rearrange("(t p) d -> p t d", p=128)
        for t in range(NT):
            p.dma("sp", h32[:, t, :], xv[:, t, :], writes=[rh[t]])
        for t in range(NT):
            make_hT(t)
        if not layers:
            for t in range(NT):
                p.dma("sp", out_d[s].rearrange("(t p) d -> p t d", p=128)[:, t, :], h32[:, t, :], reads=[rh[t]])
        for li_, l in enumerate(layers):
            p.barrier()
            prescale()
            if l % 2 == 0:
                even_layer(l)
            else:
                odd_layer(l)
            if STOP <= -1:
                for t in range(NT):
                    p.dma("sp", out_d[s].rearrange("(t p) d -> p t d", p=128)[:, t, :], h32[:, t, :], reads=[rh[t]])
                continue
            layer_norm(l, s, last=(li_ == len(layers) - 1))
    p.wait_all_dma("sp", rh)
    p.emit()
    p.stats["pe_phases"] = [o.get("ph", "") for o in p.ops["pe"] if o["fn"] is not None]
    return nc, p.stats


def host_constants():
    half = 64
    inv_freq = (10000.0 ** (-np.arange(half, dtype=np.float32) / half)).astype(np.float32)
    pos = np.arange(S, dtype=np.float32)
    ang = pos[None, :] * inv_freq[:, None]
    cos = np.cos(ang).astype(np.float32)
    sin = np.sin(ang).astype(np.float32)
    cosT = np.concatenate([cos, cos], axis=0)
    sinS = np.concatenate([-sin, sin], axis=0)
    identf = np.eye(128, dtype=np.float32)
    j = np.arange(128)[:, None]
    i = np.arange(128)[None, :]
    cmask = (i >= j).astype(np.float32)
    g = (1.0 - 2.0 ** (-5.0 - np.arange(8, dtype=np.float64)))
    dmask = np.zeros((128, 8, 128), np.float64)
    for h in range(8):
        dmask[:, h, :] = np.where(i >= j, g[h] ** np.maximum(i - j, 0), 0.0) * SCALE
    gpow = g[None, :] ** (np.arange(128)[:, None] + 1.0)
    kend = (g[None, :] ** (127.0 - np.arange(128)[:, None])) * SCALE
    return dict(cosT=np.ascontiguousarray(cosT), sinS=np.ascontiguousarray(sinS), identf=identf, cmask=cmask,
                dmask=dmask.astype(np.float32), gpow=gpow.astype(np.float32), kend=kend.astype(np.float32))


def rep(a):
    return np.ascontiguousarray(np.broadcast_to(a, (128,) + a.shape))


def host_weights(inp, layers):
    m = {}
    for l in layers:
        j = l // 2
        if l % 2 == 0:
            w = inp["ev_w_in"][j]
            cols = []
            for u in range(8):
                cols.append(np.concatenate([w[:, k * 1024 + u * 128:k * 1024 + (u + 1) * 128] for k in range(6)], axis=1))
            m[f"w_in{l}"] = np.ascontiguousarray(np.stack(cols, 0))
            wo = inp["ev_w_out"][j]
            m[f"w_out{l}"] = np.ascontiguousarray(np.stack(
                [np.concatenate([wo[u * 128:(u + 1) * 128], wo[1024 + u * 128:1024 + (u + 1) * 128]], 0) for u in range(8)], 0))
            m[f"gw{l}"] = np.ascontiguousarray(np.concatenate([inp["ev_gate_a_w"][j], inp["ev_gate_x_w"][j]], axis=2))
            lr = np.stack([inp["ev_conv_w"][j][0], inp["ev_conv_w"][j][1], inp["ev_conv_w"][j][2], inp["ev_conv_w"][j][3],
                           inp["ev_conv_b"][j], inp["ev_gate_a_b"][j], inp["ev_gate_x_b"][j], inp["ev_lru_lambda"][j]], axis=1)
            m[f"lrup{l}"] = np.ascontiguousarray(lr.reshape(8, 128, 8).transpose(1, 0, 2))
            m[f"gn{l}"] = rep(np.stack([inp["ev_ret_gn_g"][j], inp["ev_ret_gn_b"][j]], 0))
            m[f"ln{l}"] = rep(np.stack([inp["ev_ln_g"][j], inp["ev_ln_b"][j]], 0))
        else:
            w = inp["od_w_in"][j]
            cols = []
            for h in range(8):
                q1 = w[:, (2 * h) * 128:(2 * h + 1) * 128]
                q2 = w[:, (2 * h + 1) * 128:(2 * h + 2) * 128]
                k1 = w[:, 2048 + (2 * h) * 128:2048 + (2 * h + 1) * 128]
                k2 = w[:, 2048 + (2 * h + 1) * 128:2048 + (2 * h + 2) * 128]
                v = w[:, 4096 + h * 256:4096 + (h + 1) * 256]
                gg = w[:, 6144 + h * 256:6144 + (h + 1) * 256]
                cols.append(np.concatenate([q1, q2, k1, k2, v, gg], axis=1))
            m[f"w_in{l}"] = np.ascontiguousarray(np.stack(cols, 0))
            m[f"w_out{l}"] = np.ascontiguousarray(inp["od_w_out"][j].reshape(8, 256, 1024))
            m[f"lam{l}"] = rep(np.stack([inp["od_lambda_q1"][j], inp["od_lambda_k1"][j], inp["od_lambda_q2"][j], inp["od_lambda_k2"][j]], 0))
            m[f"sub{l}"] = rep(inp["od_subln_g"][j])
            m[f"ln{l}"] = rep(np.stack([inp["od_ln_g"][j], inp["od_ln_b"][j]], 0))
    return m


_PROGS = {}


def run_layers(x, inp, layers, ncores=NCORES, trace=False):
    B = x.shape[0]
    nseq = B // ncores
    key = (tuple(layers), nseq)
    if key not in _PROGS:
        _PROGS[key] = build_program(layers, nseq)
    nc, stats = _PROGS[key]
    consts = host_constants()
    wts = host_weights(inp, layers)
    in_maps = []
    for c in range(ncores):
        mdict = dict(consts)
        mdict.update(wts)
        mdict["x"] = np.ascontiguousarray(x[c * nseq:(c + 1) * nseq])
        in_maps.append(mdict)
    res = run_bass_kernel_spmd(nc, in_maps, core_ids=list(range(ncores)))
    return np.concatenate([r["out"] for r in res.results], axis=0)


FUSED = True
import os
STOP = int(os.environ.get("KSTOP", "99"))


def kernel(**inputs):
    inp = {k: np.asarray(v) for k, v in inputs.items()}
    x = np.ascontiguousarray(inp["x"], dtype=np.float32)
    if FUSED:
        return run_layers(x, inp, [0, 1, 2, 3]).astype(np.float32)
    h = x
    for l in range(DEPTH):
        h = run_layers(h, inp, [l])
    return h.astype(np.float32)
```

```python
import math
import numpy as np
from contextlib import ExitStack
import concourse.bass as bass
import concourse.mybir as mybir
from concourse.bass_utils import run_bass_kernel_spmd

F32 = mybir.dt.float32
BF16 = mybir.dt.bfloat16
ALU = mybir.AluOpType
AF = mybir.ActivationFunctionType
AX = mybir.AxisListType

NCORES = 8
S = 2048
D = 1024
NT = 16
DEPTH = 4
EPS = 1e-5
ALPHA = (2.0 * DEPTH) ** 0.25
SCALE = 128.0 ** -0.5

ENG = ("pe", "act", "dve", "pool", "sp")
EPOCH = 30000


class Res:
    __slots__ = ("name", "w", "r", "sem", "excl")

    def __init__(self, name, excl=False):
        self.excl = excl
        self.name = name
        self.w = None
        self.r = {}
        self.sem = None


class Prog:
    def __init__(self, nc):
        self.nc = nc
        self.ops = {e: [] for e in ENG}
        self.waited = {e: {} for e in ENG}
        self.dma_cnt = []
        self.stack = ExitStack()
        self.n_sb = 0

    def sb(self, shape, dt, name=None):
        self.n_sb += 1
        return self.stack.enter_context(self.nc.sbuf_tensor("S_" + (name or f"sb{self.n_sb}"), list(shape), dt))

    def ps(self, shape, dt, name=None):
        self.n_sb += 1
        return self.stack.enter_context(self.nc.psum_tensor("P_" + (name or f"ps{self.n_sb}"), list(shape), dt))

    def _need(self, eng, dep, waits):
        if dep is None:
            return
        key = (dep[0], dep[1])
        val = dep[2]
        if self.waited[eng].get(key, -1) >= val:
            return
        if waits.get(key, -1) >= val:
            return
        waits[key] = val

    def op(self, eng, fn, reads=(), writes=()):
        ex = [r for r in reads if r.excl]
        if ex:
            reads = [r for r in reads if not r.excl]
            writes = list(writes) + [r for r in ex if r not in writes]
        idx = len(self.ops[eng])
        waits = {}
        for r in reads:
            d = r.w
            if d is None:
                continue
            if d[0] == "op" and d[1] == eng and eng in ("pe", "sp"):
                continue
            self._need(eng, d, waits)
        for w in writes:
            for d in ([w.w] if w.w is not None else []) + list(w.r.values()):
                if d[0] == "op" and d[1] == eng:
                    continue
                self._need(eng, d, waits)
        for k, v in waits.items():
            self.waited[eng][k] = v
        me = ("op", eng, idx)
        for r in reads:
            r.r[("op", eng)] = me
        for w in writes:
            w.w = me
            w.r = {}
        self.ops[eng].append(dict(fn=fn, waits=waits, dma=None, inc=False, ph=getattr(self, "phase", "")))
        return me

    def dma(self, eng, out, in_, reads=(), writes=(), sem_res=None):
        if sem_res is None:
            sem_res = writes[0] if writes else reads[0]
        if sem_res.sem is None:
            sem_res.sem = len(self.dma_cnt)
            self.dma_cnt.append(0)
        sid = sem_res.sem
        waits = {}
        for r in reads:
            self._need(eng, r.w, waits)
        for w in writes:
            for d in ([w.w] if w.w is not None else []) + list(w.r.values()):
                self._need(eng, d, waits)
        for k, v in waits.items():
            self.waited[eng][k] = v
        self.dma_cnt[sid] += 16
        me = ("dma", sid, self.dma_cnt[sid])
        for r in reads:
            r.r[("dma", sid)] = me
        for w in writes:
            w.w = me
            w.r = {}
        self.ops[eng].append(dict(fn=lambda e, o=out, i=in_: e.dma_start(out=o, in_=i), waits=waits, dma=sid, inc=False))
        return me

    def barrier(self, engines=("pe", "act", "dve", "pool")):
        last = {}
        for e in engines:
            i = len(self.ops[e]) - 1
            while i >= 0 and (self.ops[e][i]["fn"] is None or self.ops[e][i]["dma"] is not None):
                i -= 1
            if i >= 0:
                last[e] = i
        for e in tuple(engines) + ("sp",):
            waits = {}
            for e2, i2 in last.items():
                if e2 != e:
                    self._need(e, ("op", e2, i2), waits)
            for k, v in waits.items():
                self.waited[e][k] = v
            self.ops[e].append(dict(fn=None, waits=waits, dma=None, inc=False))

    def wait_all_dma(self, eng, resources):
        waits = {}
        for r in resources:
            for d in ([r.w] if r.w is not None else []) + list(r.r.values()):
                if d[0] == "dma":
                    self._need(eng, d, waits)
        for k, v in waits.items():
            self.waited[eng][k] = v
        self.ops[eng].append(dict(fn=None, waits=waits, dma=None, inc=False))

    def emit(self):
        nc = self.nc
        for e in ENG:
            for o in self.ops[e]:
                for (kind, src), val in o["waits"].items():
                    if kind == "op":
                        self.ops[src][val]["inc"] = True
        semval = {e: {} for e in ENG}
        nsem = {}
        for e in ENG:
            c = 0
            for i, o in enumerate(self.ops[e]):
                if o["inc"]:
                    semval[e][i] = (c // EPOCH, c % EPOCH + 1)
                    c += 1
            nsem[e] = (c + EPOCH - 1) // EPOCH
        st = self.stack
        esems = {e: [st.enter_context(nc.semaphore(f"s_{e}{k}")) for k in range(nsem[e])] for e in ENG}
        dsems = [st.enter_context(nc.semaphore(f"d{k}")) for k in range(len(self.dma_cnt))]
        self.stats = {e: (len(self.ops[e]), sum(len(o["waits"]) for o in self.ops[e])) for e in ENG}
        block = st.enter_context(nc.Block())

        def runner(e):
            def run(eng):
                for i, o in enumerate(self.ops[e]):
                    for (kind, src), val in o["waits"].items():
                        if kind == "op":
                            ep, v = semval[src][val]
                            eng.wait_ge(esems[src][ep], v)
                        else:
                            eng.wait_ge(dsems[src], val)
                    if o["fn"] is None:
                        continue
                    ins = o["fn"](eng)
                    if o["dma"] is not None:
                        ins.then_inc(dsems[o["dma"]], 16)
                    elif o["inc"]:
                        ep, v = semval[e][i]
                        ins.then_inc(esems[e][ep], 1)
            return run

        block.tensor(runner("pe"))
        block.scalar(runner("act"))
        block.vector(runner("dve"))
        block.gpsimd(runner("pool"))
        block.sync(runner("sp"))
        st.close()


def lambda_init(l):
    return 0.8 - 0.6 * math.exp(-0.3 * l)


def build_program(layers, nseq, first_is_input=True):
    nc = bass.Bass("TRN2", target_bir_lowering=False)
    p = Prog(nc)

    def din(name, shape):
        return nc.dram_tensor(name, list(shape), F32, kind="ExternalInput").ap()

    x_d = din("x", [nseq, S, D])
    out_d = nc.dram_tensor("out", [nseq, S, D], F32, kind="ExternalOutput").ap()
    cos_d = din("cosT", [128, S])
    sin_d = din("sinS", [128, S])
    identf_d = din("identf", [128, 128])
    cmask_d = din("cmask", [128, 128])
    dmask_d = din("dmask", [128, 8, 128])
    gpow_d = din("gpow", [128, 8])
    kend_d = din("kend", [128, 8])
    wd = {}
    for l in layers:
        if l % 2 == 0:
            wd[l] = dict(w_in=din(f"w_in{l}", [8, D, 768]), w_out=din(f"w_out{l}", [8, 256, D]),
                         gw=din(f"gw{l}", [8, 128, 256]), lrup=din(f"lrup{l}", [128, 8, 8]),
                         gn=din(f"gn{l}", [128, 2, D]), ln=din(f"ln{l}", [128, 2, D]))
        else:
            wd[l] = dict(w_in=din(f"w_in{l}", [8, D, 1024]), w_out=din(f"w_out{l}", [8, 256, D]),
                         lam=din(f"lam{l}", [128, 4, 128]), sub=din(f"sub{l}", [128, 256]),
                         ln=din(f"ln{l}", [128, 2, D]))

    h32 = p.sb([128, NT, D], F32, "h32")
    rh = [Res(f"h32_{t}") for t in range(NT)]
    hT = p.sb([128, 8, S], BF16, "hT")
    rhT = [Res(f"hT_{t}") for t in range(NT)]
    slabA = p.sb([128, 8, 512], BF16, "slabA"); rslabA = Res("slabA")
    slabB = p.sb([128, 8, 512], BF16, "slabB"); rslabB = Res("slabB")
    wout = [p.sb([128, 2, D], BF16, f"wout{i}") for i in range(1)]
    rwout = [Res(f"wout{i}") for i in range(1)]
    qk = p.sb([128, 4, S], BF16, "qk")
    rqk = [Res(f"qk{i}") for i in range(4)]
    cosT = p.sb([128, S], F32, "cosT"); sinS = p.sb([128, S], F32, "sinS")
    rconst = Res("const")
    identf = p.sb([128, 128], F32, "identf")
    identb = p.sb([128, 128], BF16, "identb")
    cmask = p.sb([128, 128], BF16, "cmask")
    gpow = p.sb([128, 8], F32, "gpow")
    kend = p.sb([128, 8], F32, "kend")
    epst = p.sb([128, 1], F32, "eps")
    onet = p.sb([128, 1], F32, "one")
    rxs = [p.sb([128, 512], F32, f"rxs{i}") for i in range(2)]; r_rxs = [Res(f"rxs{i}") for i in range(2)]
    rtt = [p.sb([128, 512], F32, f"rtt{i}") for i in range(2)]; r_rtt = [Res(f"rtt{i}") for i in range(2)]
    ARENA = 44800
    arena = p.sb([128, ARENA // 4], F32, "arena")
    st12 = p.sb([128, NT, 12], F32, "st12"); r_st12 = Res("st12")
    mvall = p.sb([128, NT, 2], F32, "mvall"); r_mv = Res("mvall")
    rstd = p.sb([128, NT], F32, "rstd"); r_rstd = Res("rstd")
    nmr = p.sb([128, NT], F32, "nmr"); r_nmr = Res("nmr")
    prm = {}
    for l in layers:
        if l % 2 == 0:
            prm[l] = dict(lrup=p.sb([128, 8, 8], F32, f"lrup{l}"), c=p.sb([128, 8], F32, f"c{l}"),
                          c2=p.sb([128, 8], F32, f"c2{l}"), cn=p.sb([128, 8], F32, f"cn{l}"), r=Res(f"prm{l}"))
        else:
            prm[l] = dict(lamt=rxs[0][:].rearrange("p (a b) -> p a b", a=4), nlam=p.sb([128, 1], F32, f"nlam{l}"),
                          sub2=p.sb([128, 256], F32, f"sub2{l}"), tmp=rtt[0][:, 0:256].rearrange("p (a b) -> p a b", a=2),
                          s12=p.sb([128, 2], F32, f"s12{l}"), r=Res(f"prm{l}"))

    banks = [p.ps([128, 512], F32, f"bank{i}") for i in range(8)]
    rbank = [Res(f"bank{i}", excl=True) for i in range(8)]
    grp = {"proj": [0, 1, 2, 3], "st": [0, 1, 2], "o": [4, 5, 6, 7]}
    gcnt = {k: 0 for k in grp}

    def nxt(g):
        i = grp[g][gcnt[g] % len(grp[g])]
        gcnt[g] += 1
        return banks[i], rbank[i]

    def mm(out, lhsT, rhs, start, stop, reads, writes):
        p.op("pe", lambda e: e.matmul(out, lhsT=lhsT, rhs=rhs, start=start, stop=stop), reads, writes)

    def tr(out, in_, ident, reads, writes):
        p.op("pe", lambda e: e.transpose(out=out, in_=in_, identity=ident), reads, writes)

    def act(out, in_, func, reads, writes, bias=None, scale=1.0, accum=None):
        kw = {}
        if bias is not None:
            kw["bias"] = bias
        if accum is not None:
            kw["accum_out"] = accum
        p.op("act", lambda e: e.activation(out=out, in_=in_, func=func, scale=scale, **kw), reads, writes)

    def tt(eng, out, in0, in1, op, reads, writes):
        p.op(eng, lambda e: e.tensor_tensor(out=out, in0=in0, in1=in1, op=op), reads, writes)

    def ts(eng, out, in0, s1, s2, op0, op1, reads, writes):
        if s2 is None:
            p.op(eng, lambda e: e.tensor_single_scalar(out=out, in_=in0, scalar=s1, op=op0), reads, writes)
        else:
            p.op(eng, lambda e: e.tensor_scalar(out=out, in0=in0, scalar1=s1, scalar2=s2, op0=op0, op1=op1), reads, writes)

    def stt(eng, out, in0, scalar, in1, op0, op1, reads, writes):
        p.op(eng, lambda e: e.scalar_tensor_tensor(out=out, in0=in0, scalar=scalar, in1=in1, op0=op0, op1=op1), reads, writes)

    def cp(eng, out, in_, reads, writes):
        if eng == "act":
            p.op("act", lambda e: e.copy(out=out, in_=in_), reads, writes)
        else:
            p.op(eng, lambda e: e.tensor_copy(out=out, in_=in_), reads, writes)

    p.dma("sp", cosT[:], cos_d, writes=[rconst])
    p.dma("sp", sinS[:], sin_d, writes=[rconst])
    p.dma("sp", identf[:], identf_d, writes=[rconst])
    p.dma("pool", identb[:], identf_d, writes=[rconst])
    p.dma("pool", cmask[:], cmask_d, writes=[rconst])
    p.dma("sp", gpow[:], gpow_d, writes=[rconst])
    p.dma("sp", kend[:], kend_d, writes=[rconst])
    r_eps = Res("eps")
    p.op("dve", lambda e: e.memset(epst[:], EPS), writes=[r_eps])
    p.op("dve", lambda e: e.memset(onet[:], 1.0), writes=[r_eps])
    for l in layers:
        pr = prm[l]
        if STOP <= -2:
            continue
        if l % 2 == 0:
            p.dma("sp", pr["lrup"][:], wd[l]["lrup"], writes=[pr["r"]])
            act(pr["c"][:], pr["lrup"][:, :, 7], AF.Exp, [pr["r"]], [pr["r"]], scale=-1.0)
            act(pr["c"][:], pr["c"][:], AF.Ln, [pr["r"], r_eps], [pr["r"]], bias=onet[:])
            ts("dve", pr["cn"][:], pr["c"][:], 8.0, None, ALU.mult, None, [pr["r"]], [pr["r"]])
            ts("dve", pr["c"][:], pr["cn"][:], -1.0, None, ALU.mult, None, [pr["r"]], [pr["r"]])
            ts("dve", pr["c2"][:], pr["c"][:], 2.0, None, ALU.mult, None, [pr["r"]], [pr["r"]])
        else:
            li = lambda_init(l)
            p.dma("sp", pr["lamt"], wd[l]["lam"], writes=[r_rxs[0]])
            p.dma("sp", pr["sub2"][:], wd[l]["sub"], writes=[pr["r"]])
            tt("dve", pr["tmp"][:, 0, :], pr["lamt"][:, 0, :], pr["lamt"][:, 1, :], ALU.mult, [r_rxs[0]], [r_rtt[0]])
            tt("dve", pr["tmp"][:, 1, :], pr["lamt"][:, 2, :], pr["lamt"][:, 3, :], ALU.mult, [r_rxs[0]], [r_rtt[0]])
            p.op("dve", lambda e, pr=pr: e.reduce_sum(out=pr["s12"][:], in_=pr["tmp"], axis=AX.X), [r_rtt[0]], [pr["r"]])
            act(pr["s12"][:], pr["s12"][:], AF.Exp, [pr["r"]], [pr["r"]])
            tt("dve", pr["nlam"][:], pr["s12"][:, 1:2], pr["s12"][:, 0:1], ALU.subtract, [pr["r"]], [pr["r"]])
            ts("dve", pr["nlam"][:], pr["nlam"][:], -li, None, ALU.add, None, [pr["r"]], [pr["r"]])
            ts("dve", pr["sub2"][:], pr["sub2"][:], 1.0 - li, None, ALU.mult, None, [pr["r"]], [pr["r"]])

    sched = []
    for s_ in range(nseq):
        for l_ in layers:
            for u_ in range(8):
                sched.append((l_, u_))
    pos = {"A": 0, "B": 0}

    def load_next(which):
        i = pos[which]
        pos[which] += 1
        if i >= len(sched):
            return
        l, u = sched[i]
        src = wd[l]["w_in"][u].rearrange("(kc p) c -> p kc c", p=128)
        if which == "A":
            dst, rdst, c0, w = slabA, rslabA, 0, 512
        else:
            dst, rdst, c0, w = slabB, rslabB, 512, (512 if l % 2 == 1 else 256)
        for kc0 in range(0, 8, 4):
            p.dma("pool", dst[:, kc0:kc0 + 4, 0:w], src[:, kc0:kc0 + 4, c0:c0 + w], writes=[rdst])

    def load_wout(l, u):
        src = wd[l]["w_out"][u].rearrange("(c p) n -> p c n", p=128)
        p.dma("pool", wout[0][:], src, writes=[rwout[0]])
        return wout[0], rwout[0]

    rope_ctr = [0]
    rope_bufs = {"xs": list(zip(rxs, r_rxs)), "tt": list(zip(rtt, r_rtt))}

    def rope(bank, rb, dst, rdst, G):
        i = rope_ctr[0] % len(rope_bufs["xs"])
        rope_ctr[0] += 1
        xs, r1 = rope_bufs["xs"][i]
        t1, r2 = rope_bufs["tt"][i]
        cs = cosT[:, G * 512:(G + 1) * 512]
        sn = sinS[:, G * 512:(G + 1) * 512]
        p.op("act", lambda e: e.copy(out=xs[0:64, :], in_=bank[64:128, :]), [rb], [r1])
        p.op("act", lambda e: e.copy(out=xs[64:128, :], in_=bank[0:64, :]), [rb], [r1])
        tt("dve", t1[:], bank[:], cs, ALU.mult, [rb, rconst], [r2])
        tt("pool", xs[:], xs[:], sn, ALU.mult, [r1, rconst], [r1])
        tt("dve", dst, t1[:], xs[:], ALU.add, [r1, r2], [rdst])

    def proj_feat(sl, rsl, c0, G):
        bank, rb = nxt("proj")
        for kc in range(8):
            mm(bank[:], sl[:, kc, c0:c0 + 128], hT[:, kc, G * 512:(G + 1) * 512], kc == 0, kc == 7,
               [rsl] + rhT[4 * G:4 * G + 4], [rb])
        return bank, rb

    def proj_tok(sl, rsl, c0, width, t):
        bank, rb = nxt("proj")
        for kc in range(8):
            mm(bank[:, 0:width], hT[:, kc, t * 128:(t + 1) * 128], sl[:, kc, c0:c0 + width], kc == 0, kc == 7,
               [rsl, rhT[t]], [rb])
        return bank, rb

    def out_proj(mixT_views, rmix, wo, rwo, scale=None, rscale=None):
        k = 0
        for t in range(NT):
            for cg in range(2):
                bank, rb = nxt("proj")
                for c in range(2):
                    mm(bank[:], mixT_views[c][:, t * 128:(t + 1) * 128], wo[:, c, cg * 512:(cg + 1) * 512],
                       c == 0, c == 1, [rwo] + rmix, [rb])
                hv = h32[:, t, cg * 512:(cg + 1) * 512]
                extra = [rscale] if rscale is not None else []
                if k % 2 == 0:
                    if scale is None:
                        tt("dve", hv, hv, bank[:], ALU.add, [rb, rh[t]], [rh[t]])
                    else:
                        stt("dve", hv, bank[:], scale[:, t:t + 1], hv, ALU.mult, ALU.add, [rb, rh[t]] + extra, [rh[t]])
                else:
                    i = (k // 2) % 2
                    if scale is None:
                        cp("act", rxs[i][:], bank[:], [rb], [r_rxs[i]])
                    else:
                        act(rxs[i][:], bank[:], AF.Copy, [rb] + extra, [r_rxs[i]], scale=scale[:, t:t + 1])
                    tt("pool", hv, hv, rxs[i][:], ALU.add, [r_rxs[i], rh[t]], [rh[t]])
                k += 1

    def prescale():
        for t in range(NT):
            ts("dve", h32[:, t, :], h32[:, t, :], ALPHA, None, ALU.mult, None, [rh[t]], [rh[t]])

    def make_hT(t):
        for half in range(2):
            bank, rb = nxt("proj")
            for k in range(4):
                kc = half * 4 + k
                tr(bank[:, k * 128:(k + 1) * 128], h32[:, t, kc * 128:(kc + 1) * 128], identf[:], [rh[t], rconst], [rb])
            dst = hT[:, half * 4:half * 4 + 4, t * 128:(t + 1) * 128]
            src = bank[:].rearrange("p (k n) -> p k n", k=4)
            cp("act", dst, src, [rb], [rhT[t]])

    def layer_norm(l, s, last):
        p.phase = "ln"
        lnt = qk[:, 0:2, :].bitcast(F32)
        p.dma("sp", lnt, wd[l]["ln"], writes=[rqk[0], rqk[1]])
        rl = [rqk[0], rqk[1]]
        for t in range(NT):
            for hf in range(2):
                p.op("dve", lambda e, t=t, hf=hf: e.bn_stats(out=st12[:, t, hf * 6:(hf + 1) * 6], in_=h32[:, t, hf * 512:(hf + 1) * 512]),
                     [rh[t]], [r_st12])
            p.op("dve", lambda e, t=t: e.bn_aggr(out=mvall[:, t, :], in_=st12[:, t, :]), [r_st12], [r_mv])
        act(rstd[:], mvall[:, :, 1], AF.Sqrt, [r_mv, r_eps], [r_rstd], bias=epst[:])
        p.op("dve", lambda e: e.reciprocal(out=rstd[:], in_=rstd[:]), [r_rstd], [r_rstd])
        stt("dve", nmr[:], mvall[:, :, 0], -1.0, rstd[:], ALU.mult, ALU.mult, [r_mv, r_rstd], [r_nmr])
        for t in range(NT):
            hv = h32[:, t, :]
            ts("dve", hv, hv, rstd[:, t:t + 1], nmr[:, t:t + 1], ALU.mult, ALU.add, [rh[t], r_rstd, r_nmr], [rh[t]])
            tt("pool", hv, hv, lnt[:, 0, :], ALU.mult, [rh[t]] + rl, [rh[t]])
            tt("dve", hv, hv, lnt[:, 1, :], ALU.add, [rh[t]] + rl, [rh[t]])
            if last:
                p.dma("sp", out_d[s].rearrange("(t p) d -> p t d", p=128)[:, t, :], hv, reads=[rh[t]])
            else:
                make_hT(t)

    def odd_layer(l):
        grp.update({"proj": [0, 1, 2, 3], "st": [0, 1, 2], "o": [4, 5, 6, 7]})
        pr = prm[l]
        off = [0]

        def carve(nbytes, dt, shape):
            a = arena[:, off[0] // 4:(off[0] + nbytes) // 4]
            off[0] += nbytes
            if dt == BF16:
                a = a.bitcast(BF16)
            if len(shape) == 3:
                a = a.rearrange("p (a b) -> p a b", a=shape[1])
            return a

        vaug = carve(NT * 264 * 2, BF16, [128, NT, 264]); r_v = Res("vaug")
        sg = carve(NT * 256 * 2, BF16, [128, NT, 256]); r_sg = Res("sg")
        dd = carve(4 * 256 * 4, F32, [128, 4, 256]); r_dd = [Res(f"dd{i}") for i in range(4)]
        pTs = [carve(512 * 2, BF16, [128, 512]) for _ in range(3)]; r_pT = [Res(f"pT{i}") for i in range(3)]
        mixb = [carve(256 * 2, BF16, [128, 256]) for _ in range(2)]; r_mixb = [Res(f"mixb{i}") for i in range(2)]
        mtmp = [carve(256 * 4, F32, [128, 256]) for _ in range(2)]; r_mtmp = [Res(f"mtmp{i}") for i in range(2)]
        mixT = carve(2 * S * 2, BF16, [128, 2, S]); r_mixT = Res("mixT")
        sq = carve(256 * 4, F32, [128, 256]); r_sq = Res("sqj")
        ss = carve(16 * 4, F32, [128, 16]); r_ss = Res("ss")
        rec = carve(16 * 4, F32, [128, 16]); r_rec = Res("rec")
        rope_bufs["xs"] = list(zip(rxs, r_rxs)) + [(carve(2048, F32, [128, 512]), Res(f"rxs_x{i}")) for i in range(2)]
        rope_bufs["tt"] = list(zip(rtt, r_rtt)) + [(carve(2048, F32, [128, 512]), Res(f"rtt_x{i}")) for i in range(2)]
        assert off[0] <= ARENA, off[0]
        p.op("pool", lambda e: e.memset(vaug[:, :, 256:257], 1.0), [], [r_v])
        pT_ctr = [0]
        mix_ctr = [0]
        for h in range(8):
            p.phase = "odd_qk"
            for c in range(4):
                for G in range(4):
                    bank, rb = proj_feat(slabA, rslabA, c * 128, G)
                    rope(bank, rb, qk[:, c, G * 512:(G + 1) * 512], rqk[c], G)
            load_next("A")
            p.phase = "odd_vg"
            for t in range(NT):
                bank, rb = proj_tok(slabB, rslabB, 0, 512, t)
                cp("dve", vaug[:, t, 0:256], bank[:, 0:256], [rb], [r_v])
                act(sg[:, t, :], bank[:, 256:512], AF.Silu, [rb], [r_sg])
            load_next("B")
            wo, rwo = load_wout(l, h)
            for G in range(4):
                p.phase = "odd_attn"
                nk = 4 * G + 4
                for m in range(2):
                    qc, kc_ = m, 2 + m
                    obs = [nxt("o") for _ in range(4)]
                    stb = {}

                    def emit_s(kt):
                        q0 = max(4 * G, kt)
                        o_ = (q0 - 4 * G) * 128
                        bank, rb = nxt("st")
                        diag = kt >= 4 * G
                        mm(bank[:, o_:512], qk[:, kc_, kt * 128:(kt + 1) * 128], qk[:, qc, G * 512 + o_:(G + 1) * 512],
                           True, not diag, [rqk[kc_], rqk[qc]], [rb])
                        if diag:
                            mm(bank[:, o_:o_ + 128], identb[:], cmask[:], False, True, [rconst], [rb])
                        stb[kt] = (bank, rb, q0, o_)

                    emit_s(0)
                    emit_s(1)
                    for kt in range(nk):
                        if kt + 2 < nk:
                            emit_s(kt + 2)
                        bank, rb, q0, o_ = stb.pop(kt)
                        i = pT_ctr[0] % 3
                        pT_ctr[0] += 1
                        pT, rp = pTs[i], r_pT[i]
                        act(pT[:, o_:512], bank[:, o_:512], AF.Exp, [rb], [rp], scale=SCALE)
                        for qt in range(q0, 4 * G + 4):
                            qi = qt - 4 * G
                            ob, rob = obs[qi]
                            mm(ob[:, 0:257], pT[:, qi * 128:(qi + 1) * 128], vaug[:, kt, 0:257], kt == 0, kt == qt,
                               [rp, r_v], [rob])
                    for qi in range(4):
                        ob, rob = obs[qi]
                        t = 4 * G + qi
                        col = m * 8 + qi
                        p.op("dve", lambda e, ob=ob, col=col: e.reciprocal(out=rec[:, col:col + 1], in_=ob[:, 256:257]), [rob], [r_rec])
                        if m == 0:
                            ts("dve", dd[:, qi, :], ob[:, 0:256], rec[:, col:col + 1], None, ALU.mult, None, [rob, r_rec], [r_dd[qi]])
                        else:
                            tt("dve", rec[:, col + 4:col + 5], rec[:, col:col + 1], pr["nlam"][:], ALU.mult, [r_rec, pr["r"]], [r_rec])
                            stt("dve", dd[:, qi, :], ob[:, 0:256], rec[:, col + 4:col + 5], dd[:, qi, :], ALU.mult, ALU.add,
                                [rob, r_rec, r_dd[qi]], [r_dd[qi]])
                p.phase = "odd_fin"
                for qi in range(4):
                    t = 4 * G + qi
                    i = mix_ctr[0] % 2
                    mix_ctr[0] += 1
                    tt("dve", sq[:], dd[:, qi, :], dd[:, qi, :], ALU.mult, [r_dd[qi]], [r_sq])
                    p.op("dve", lambda e, t=t: e.reduce_sum(out=ss[:, t:t + 1], in_=sq[:], axis=AX.X), [r_sq], [r_ss])
                    tt("dve", mtmp[i][:], dd[:, qi, :], pr["sub2"][:], ALU.mult, [r_dd[qi], pr["r"]], [r_mtmp[i]])
                    tt("dve", mixb[i][:], mtmp[i][:], sg[:, t, :], ALU.mult, [r_mtmp[i], r_sg], [r_mixb[i]])
                    bank, rb = nxt("proj")
                    bv = bank[:, 0:128].bitcast(BF16)
                    for c in range(2):
                        tr(bv[:, c * 128:(c + 1) * 128], mixb[i][:, c * 128:(c + 1) * 128], identb[:], [r_mixb[i], rconst], [rb])
                    cp("act", mixT[:, :, t * 128:(t + 1) * 128], bv.rearrange("p (c n) -> p c n", c=2), [rb], [r_mixT])
            p.phase = "odd_out"
            act(ss[:], ss[:], AF.Sqrt, [r_ss, r_eps], [r_ss], bias=epst[:], scale=1.0 / 256.0)
            p.op("dve", lambda e: e.reciprocal(out=ss[:], in_=ss[:]), [r_ss], [r_ss])
            out_proj([mixT[:, 0, :], mixT[:, 1, :]], [r_mixT], wo, rwo, scale=ss, rscale=r_ss)

    def even_layer(l):
        rope_bufs["xs"] = list(zip(rxs, r_rxs))
        rope_bufs["tt"] = list(zip(rtt, r_rtt))
        grp.update({"proj": [0, 1, 2, 3], "st": [4, 5], "o": [6, 7]})
        pr = prm[l]
        off = [0]

        def carve(nbytes, dt, shape):
            a = arena[:, off[0] // 4:(off[0] + nbytes) // 4]
            off[0] += nbytes
            if dt == BF16:
                a = a.bitcast(BF16)
            if len(shape) == 3:
                a = a.rearrange("p (a b) -> p a b", a=shape[1])
            return a

        kE = carve(NT * 128 * 2, BF16, [128, NT, 128]); r_kE = Res("kE")
        vv = carve(NT * 128 * 2, BF16, [128, NT, 128]); r_vv = Res("vv")
        sgr = carve(NT * 128 * 2, BF16, [128, NT, 128]); r_sgr = [Res(f"sgr{t}") for t in range(NT)]
        Rbs = [carve(128 * 2, BF16, [128, 128]) for _ in range(3)]; r_Rbs = [Res(f"Rb{i}") for i in range(3)]
        R32 = carve(128 * 4, F32, [128, 128]); r_R32 = Res("R32")
        ro = qk[:, 2:4, :].bitcast(F32).rearrange("p a (b c) -> p (a b) c", c=128)
        r_ro = [Res(f"ro{t}") for t in range(NT)]
        dmk = carve(128 * 4, F32, [128, 128]); r_dmk = Res("dmk")
        scs = [carve(128 * 2, BF16, [128, 128]) for _ in range(3)]; r_sc = [Res(f"sc{i}") for i in range(3)]
        mixT = carve(2 * S * 2, BF16, [128, 2, S]); r_mixR = Res("mixTr"); r_mixL = Res("mixTl")
        gnt = carve(2 * 128 * 4, F32, [128, 2, 128]); r_gnt = Res("gnt")
        gwt = carve(256 * 2, BF16, [128, 256]); r_gwt = Res("gwt")
        st6 = carve(NT * 6 * 4, F32, [128, NT, 6]); r_st6 = Res("st6")
        mv2 = carve(NT * 2 * 4, F32, [128, NT, 2]); r_mv2 = Res("mv2")
        rs2 = carve(NT * 4, F32, [128, NT]); r_rs2 = Res("rs2")
        nm2 = carve(NT * 4, F32, [128, NT]); r_nm2 = Res("nm2")
        xb32 = carve(516 * 4, F32, [128, 516]); r_xb = Res("xb32")
        names = ["xc", "rr", "ig", "aa", "a2", "hh0", "hh1"]
        lt = {n: carve(512 * 4, F32, [128, 512]) for n in names}
        rlt = {n: Res(n) for n in names}
        xcb = carve(512 * 2, BF16, [128, 512]); r_xcb = Res("xcb")
        sgl = carve(512 * 4, F32, [128, 512]); r_sgl = Res("sgl")
        assert off[0] <= ARENA, off[0]
        sc_ctr = [0]
        for u in range(8):
            if STOP <= 0:
                continue
            gC = float((1.0 - 2.0 ** (-5.0 - u)) ** 128)
            p.dma("sp", gnt[:], wd[l]["gn"][:, :, u * 128:(u + 1) * 128], writes=[r_gnt])
            p.dma("pool", gwt[:], wd[l]["gw"][u], writes=[r_gwt])
            p.dma("sp", dmk[:], dmask_d[:, u, :], writes=[r_dmk])
            p.phase = "ev_qk"
            for c in range(2):
                for G in range(4):
                    bank, rb = proj_feat(slabA, rslabA, c * 128, G)
                    rope(bank, rb, qk[:, c, G * 512:(G + 1) * 512], rqk[c], G)
            if STOP <= 1:
                continue
            p.phase = "ev_vg"
            for t in range(NT):
                bank, rb = proj_tok(slabA, rslabA, 256, 256, t)
                cp("dve", vv[:, t, :], bank[:, 0:128], [rb], [r_vv])
                act(sgr[:, t, :], bank[:, 128:256], AF.Silu, [rb], [r_sgr[t]])
            load_next("A")
            if STOP <= 2:
                continue
            p.phase = "ev_kE"
            for t4 in range(4):
                bank, rb = nxt("proj")
                bv = bank[:, 0:256].bitcast(BF16)
                for k in range(4):
                    t = t4 * 4 + k
                    tr(bv[:, k * 128:(k + 1) * 128], qk[:, 1, t * 128:(t + 1) * 128], identb[:], [rqk[1], rconst], [rb])
                ts("dve", kE[:, t4 * 4:t4 * 4 + 4, :], bv.rearrange("p (k n) -> p k n", k=4), kend[:, u:u + 1], None, ALU.mult, None,
                   [rb, rconst], [r_kE])
            if STOP <= 3:
                continue
            L = pr["lrup"]
            p.op("dve", lambda e: e.memset(xb32[:, 0:3], 0.0), [], [r_xb])

            def lru_stage(G, k):
                hh, r_hh = lt[f"hh{G % 2}"], rlt[f"hh{G % 2}"]
                hp, r_hp = lt[f"hh{(G + 1) % 2}"], rlt[f"hh{(G + 1) % 2}"]
                xc = lt["xc"]
                if k == 0:
                    if G > 0:
                        cp("dve", xb32[:, 0:3], xb32[:, 512:515], [r_xb], [r_xb])
                    bank, rb = proj_feat(slabB, rslabB, 0, G)
                    cp("act", xb32[:, 3:515], bank[:], [rb], [r_xb])
                    bank, rb = proj_feat(slabB, rslabB, 128, G)
                    act(sgl[:], bank[:], AF.Silu, [rb], [r_sgl])
                    ts("dve", xc[:], xb32[:, 3:515], L[:, u, 3:4], L[:, u, 4:5], ALU.mult, ALU.add, [r_xb, pr["r"]], [rlt["xc"]])
                    for w in (2, 1, 0):
                        stt("dve", xc[:], xb32[:, w:w + 512], L[:, u, w:w + 1], xc[:], ALU.mult, ALU.add, [r_xb, pr["r"], rlt["xc"]], [rlt["xc"]])
                    cp("pool", xcb[:], xc[:], [rlt["xc"]], [r_xcb])
                elif k == 1:
                    bank, rb = nxt("proj")
                    mm(bank[:], gwt[:, 0:128], xcb[:], True, True, [r_gwt, r_xcb], [rb])
                    act(lt["rr"][:], bank[:], AF.Sigmoid, [rb, pr["r"]], [rlt["rr"]], bias=L[:, u, 5:6])
                    bank, rb = nxt("proj")
                    mm(bank[:], gwt[:, 128:256], xcb[:], True, True, [r_gwt, r_xcb], [rb])
                    act(lt["ig"][:], bank[:], AF.Sigmoid, [rb, pr["r"]], [rlt["ig"]], bias=L[:, u, 6:7])
                elif k == 2:
                    act(lt["aa"][:], lt["rr"][:], AF.Exp, [rlt["rr"], pr["r"]], [rlt["aa"]], scale=pr["c"][:, u:u + 1])
                    act(lt["a2"][:], lt["rr"][:], AF.Exp, [rlt["rr"], pr["r"]], [rlt["a2"]], scale=pr["c2"][:, u:u + 1])
                    act(lt["rr"][:], lt["rr"][:], AF.Tanh, [rlt["rr"], pr["r"]], [rlt["rr"]], scale=pr["cn"][:, u:u + 1])
                    stt("dve", lt["a2"][:], lt["a2"][:], 1.0, lt["rr"][:], ALU.add, ALU.mult, [rlt["a2"], rlt["rr"]], [rlt["a2"]])
                    tt("pool", lt["ig"][:], lt["ig"][:], xc[:], ALU.mult, [rlt["ig"], rlt["xc"]], [rlt["ig"]])
                else:
                    act(lt["a2"][:], lt["a2"][:], AF.Sqrt, [rlt["a2"]], [rlt["a2"]])
                    tt("dve", lt["ig"][:], lt["ig"][:], lt["a2"][:], ALU.mult, [rlt["ig"], rlt["a2"]], [rlt["ig"]])
                    init = 0.0 if G == 0 else hp[:, 511:512]
                    p.op("dve", lambda e, hh=hh, init=init: e.tensor_tensor_scan(out=hh[:], data0=lt["aa"][:], data1=lt["ig"][:], initial=init,
                                                                                  op0=ALU.mult, op1=ALU.add),
                         [rlt["aa"], rlt["ig"], r_hp], [r_hh])
                    tt("pool", mixT[:, 1, G * 512:(G + 1) * 512], hh[:], sgl[:], ALU.mult, [r_hh, r_sgl], [r_mixL])

            p.phase = "ev_ret"
            stbs = {}

            def emit_a(t):
                bank, rb = nxt("st")
                mm(bank[:, 0:128], qk[:, 1, t * 128:(t + 1) * 128], qk[:, 0, t * 128:(t + 1) * 128], True, True, [rqk[0], rqk[1]], [rb])
                mm(bank[:, 128:256], kE[:, t, :], vv[:, t, :], True, True, [r_kE, r_vv], [rb])
                stbs[t] = (bank, rb)

            emit_a(0)
            for t in range(NT):
                if t + 1 < NT:
                    emit_a(t + 1)
                bank, rb = stbs.pop(t)
                i = sc_ctr[0] % 3
                sc_ctr[0] += 1
                sc, rsc = scs[i], r_sc[i]
                tt("dve", sc[:], bank[:, 0:128], dmk[:], ALU.mult, [rb, r_dmk], [rsc])
                if t + 1 < NT:
                    if t == 0:
                        cp("act", R32[:], bank[:, 128:256], [rb], [r_R32])
                    else:
                        stt("dve", R32[:], R32[:], gC, bank[:, 128:256], ALU.mult, ALU.add, [rb, r_R32], [r_R32])
                    cp("act", Rbs[(t + 1) % 3][:], R32[:], [r_R32], [r_Rbs[(t + 1) % 3]])
                ob, rob = nxt("o")
                mm(ob[:, 0:128], sc[:], vv[:, t, :], True, True, [rsc, r_vv], [rob])
                if t > 0:
                    mm(ob[:, 128:256], qk[:, 0, t * 128:(t + 1) * 128], Rbs[t % 3][:], True, True, [rqk[0], r_Rbs[t % 3]], [rob])
                cp("act", ro[:, t, :], ob[:, 0:128], [rob], [r_ro[t]])
                if t > 0:
                    stt("dve", ro[:, t, :], ob[:, 128:256], gpow[:, u:u + 1], ro[:, t, :], ALU.mult, ALU.add,
                        [rob, r_ro[t], rconst], [r_ro[t]])
                p.op("dve", lambda e, t=t: e.bn_stats(out=st6[:, t, :], in_=ro[:, t, :]), [r_ro[t]], [r_st6])
                p.op("dve", lambda e, t=t: e.bn_aggr(out=mv2[:, t, :], in_=st6[:, t, :]), [r_st6], [r_mv2])
                lru_stage(t // 4, t % 4)
            if STOP <= 4:
                continue
            p.phase = "ev_gn"
            act(rs2[:], mv2[:, :, 1], AF.Sqrt, [r_mv2, r_eps], [r_rs2], bias=epst[:])
            p.op("dve", lambda e: e.reciprocal(out=rs2[:], in_=rs2[:]), [r_rs2], [r_rs2])
            stt("dve", nm2[:], mv2[:, :, 0], -1.0, rs2[:], ALU.mult, ALU.mult, [r_mv2, r_rs2], [r_nm2])
            for t in range(NT):
                rv = ro[:, t, :]
                ts("dve", rv, rv, rs2[:, t:t + 1], nm2[:, t:t + 1], ALU.mult, ALU.add, [r_ro[t], r_rs2, r_nm2], [r_ro[t]])
                tt("pool", rv, rv, gnt[:, 0, :], ALU.mult, [r_ro[t], r_gnt], [r_ro[t]])
                tt("pool", rv, rv, gnt[:, 1, :], ALU.add, [r_ro[t], r_gnt], [r_ro[t]])
                tt("dve", sgr[:, t, :], rv, sgr[:, t, :], ALU.mult, [r_ro[t], r_sgr[t]], [r_sgr[t]])
            for t4 in range(4):
                bank, rb = nxt("proj")
                bv = bank[:, 0:256].bitcast(BF16)
                for k in range(4):
                    t = t4 * 4 + k
                    tr(bv[:, k * 128:(k + 1) * 128], sgr[:, t, :], identb[:], [r_sgr[t], rconst], [rb])
                cp("act", mixT[:, 0, t4 * 512:(t4 + 1) * 512], bv, [rb], [r_mixR])
            load_next("B")
            if STOP <= 6:
                continue
            p.phase = "ev_out"
            wo, rwo = load_wout(l, u)
            out_proj([mixT[:, 0, :], mixT[:, 1, :]], [r_mixR, r_mixL], wo, rwo)

    load_next("A")
    load_next("B")
    for s in range(nseq):
        xv = x_d[s].rearrange("(t p) d -> p t d", p=128)
        for t in range(NT):
            p.dma("sp", h32[:, t, :], xv[:, t, :], writes=[rh[t]])
        for t in range(NT):
            make_hT(t)
        if not layers:
            for t in range(NT):
                p.dma("sp", out_d[s].rearrange("(t p) d -> p t d", p=128)[:, t, :], h32[:, t, :], reads=[rh[t]])
        for li_, l in enumerate(layers):
            p.barrier()
            prescale()
            if l % 2 == 0:
                even_layer(l)
            else:
                odd_layer(l)
            if STOP <= -1:
                for t in range(NT):
                    p.dma("sp", out_d[s].rearrange("(t p) d -> p t d", p=128)[:, t, :], h32[:, t, :], reads=[rh[t]])
                continue
            layer_norm(l, s, last=(li_ == len(layers) - 1))
    p.wait_all_dma("sp", rh)
    p.emit()
    p.stats["pe_phases"] = [o.get("ph", "") for o in p.ops["pe"] if o["fn"] is not None]
    return nc, p.stats


def host_constants():
    half = 64
    inv_freq = (10000.0 ** (-np.arange(half, dtype=np.float32) / half)).astype(np.float32)
    pos = np.arange(S, dtype=np.float32)
    ang = pos[None, :] * inv_freq[:, None]
    cos = np.cos(ang).astype(np.float32)
    sin = np.sin(ang).astype(np.float32)
    cosT = np.concatenate([cos, cos], axis=0)
    sinS = np.concatenate([-sin, sin], axis=0)
    identf = np.eye(128, dtype=np.float32)
    j = np.arange(128)[:, None]
    i = np.arange(128)[None, :]
    cmask = np.where(i >= j, 0.0, -30000.0).astype(np.float32)
    g = (1.0 - 2.0 ** (-5.0 - np.arange(8, dtype=np.float64)))
    dmask = np.zeros((128, 8, 128), np.float64)
    for h in range(8):
        dmask[:, h, :] = np.where(i >= j, g[h] ** np.maximum(i - j, 0), 0.0) * SCALE
    gpow = g[None, :] ** (np.arange(128)[:, None] + 1.0)
    kend = (g[None, :] ** (127.0 - np.arange(128)[:, None])) * SCALE
    return dict(cosT=np.ascontiguousarray(cosT), sinS=np.ascontiguousarray(sinS), identf=identf, cmask=cmask,
                dmask=dmask.astype(np.float32), gpow=gpow.astype(np.float32), kend=kend.astype(np.float32))


def rep(a):
    return np.ascontiguousarray(np.broadcast_to(a, (128,) + a.shape))


def host_weights(inp, layers):
    m = {}
    for l in layers:
        j = l // 2
        if l % 2 == 0:
            w = inp["ev_w_in"][j]
            cols = []
            for u in range(8):
                cols.append(np.concatenate([w[:, k * 1024 + u * 128:k * 1024 + (u + 1) * 128] for k in range(6)], axis=1))
            m[f"w_in{l}"] = np.ascontiguousarray(np.stack(cols, 0))
            wo = inp["ev_w_out"][j]
            m[f"w_out{l}"] = np.ascontiguousarray(np.stack(
                [np.concatenate([wo[u * 128:(u + 1) * 128], wo[1024 + u * 128:1024 + (u + 1) * 128]], 0) for u in range(8)], 0))
            m[f"gw{l}"] = np.ascontiguousarray(np.concatenate([inp["ev_gate_a_w"][j], inp["ev_gate_x_w"][j]], axis=2))
            lr = np.stack([inp["ev_conv_w"][j][0], inp["ev_conv_w"][j][1], inp["ev_conv_w"][j][2], inp["ev_conv_w"][j][3],
                           inp["ev_conv_b"][j], inp["ev_gate_a_b"][j], inp["ev_gate_x_b"][j], inp["ev_lru_lambda"][j]], axis=1)
            m[f"lrup{l}"] = np.ascontiguousarray(lr.reshape(8, 128, 8).transpose(1, 0, 2))
            m[f"gn{l}"] = rep(np.stack([inp["ev_ret_gn_g"][j], inp["ev_ret_gn_b"][j]], 0))
            m[f"ln{l}"] = rep(np.stack([inp["ev_ln_g"][j], inp["ev_ln_b"][j]], 0))
        else:
            w = inp["od_w_in"][j]
            cols = []
            for h in range(8):
                q1 = w[:, (2 * h) * 128:(2 * h + 1) * 128]
                q2 = w[:, (2 * h + 1) * 128:(2 * h + 2) * 128]
                k1 = w[:, 2048 + (2 * h) * 128:2048 + (2 * h + 1) * 128]
                k2 = w[:, 2048 + (2 * h + 1) * 128:2048 + (2 * h + 2) * 128]
                v = w[:, 4096 + h * 256:4096 + (h + 1) * 256]
                gg = w[:, 6144 + h * 256:6144 + (h + 1) * 256]
                cols.append(np.concatenate([q1, q2, k1, k2, v, gg], axis=1))
            m[f"w_in{l}"] = np.ascontiguousarray(np.stack(cols, 0))
            m[f"w_out{l}"] = np.ascontiguousarray(inp["od_w_out"][j].reshape(8, 256, 1024))
            m[f"lam{l}"] = rep(np.stack([inp["od_lambda_q1"][j], inp["od_lambda_k1"][j], inp["od_lambda_q2"][j], inp["od_lambda_k2"][j]], 0))
            m[f"sub{l}"] = rep(inp["od_subln_g"][j])
            m[f"ln{l}"] = rep(np.stack([inp["od_ln_g"][j], inp["od_ln_b"][j]], 0))
    return m


_PROGS = {}


def run_layers(x, inp, layers, ncores=NCORES, trace=False):
    B = x.shape[0]
    nseq = B // ncores
    key = (tuple(layers), nseq)
    if key not in _PROGS:
        _PROGS[key] = build_program(layers, nseq)
    nc, stats = _PROGS[key]
    consts = host_constants()
    wts = host_weights(inp, layers)
    in_maps = []
    for c in range(ncores):
        mdict = dict(consts)
        mdict.update(wts)
        mdict["x"] = np.ascontiguousarray(x[c * nseq:(c + 1) * nseq])
        in_maps.append(mdict)
    res = run_bass_kernel_spmd(nc, in_maps, core_ids=list(range(ncores)))
    return np.concatenate([r["out"] for r in res.results], axis=0)


FUSED = True
import os
STOP = int(os.environ.get("KSTOP", "99"))


def kernel(**inputs):
    inp = {k: np.asarray(v) for k, v in inputs.items()}
    x = np.ascontiguousarray(inp["x"], dtype=np.float32)
    if FUSED:
        return run_layers(x, inp, [0, 1, 2, 3]).astype(np.float32)
    h = x
    for l in range(DEPTH):
        h = run_layers(h, inp, [l])
    return h.astype(np.float32)
```

```python
import math
import numpy as np
from contextlib import ExitStack
import concourse.bass as bass
import concourse.mybir as mybir
from concourse.bass_utils import run_bass_kernel_spmd

F32 = mybir.dt.float32
BF16 = mybir.dt.bfloat16
ALU = mybir.AluOpType
AF = mybir.ActivationFunctionType
AX = mybir.AxisListType

NCORES = 8
S = 2048
D = 1024
NT = 16
DEPTH = 4
EPS = 1e-5
ALPHA = (2.0 * DEPTH) ** 0.25
SCALE = 128.0 ** -0.5

ENG = ("pe", "act", "dve", "pool", "sp")
EPOCH = 30000


class Res:
    __slots__ = ("name", "w", "r", "sem", "excl")

    def __init__(self, name, excl=False):
        self.excl = excl
        self.name = name
        self.w = None
        self.r = {}
        self.sem = None


class Prog:
    def __init__(self, nc):
        self.nc = nc
        self.ops = {e: [] for e in ENG}
        self.waited = {e: {} for e in ENG}
        self.dma_cnt = []
        self.stack = ExitStack()
        self.n_sb = 0

    def sb(self, shape, dt, name=None):
        self.n_sb += 1
        return self.stack.enter_context(self.nc.sbuf_tensor("S_" + (name or f"sb{self.n_sb}"), list(shape), dt))

    def ps(self, shape, dt, name=None):
        self.n_sb += 1
        return self.stack.enter_context(self.nc.psum_tensor("P_" + (name or f"ps{self.n_sb}"), list(shape), dt))

    def _need(self, eng, dep, waits):
        if dep is None:
            return
        key = (dep[0], dep[1])
        val = dep[2]
        if self.waited[eng].get(key, -1) >= val:
            return
        if waits.get(key, -1) >= val:
            return
        waits[key] = val

    def op(self, eng, fn, reads=(), writes=()):
        ex = [r for r in reads if r.excl]
        if ex:
            reads = [r for r in reads if not r.excl]
            writes = list(writes) + [r for r in ex if r not in writes]
        idx = len(self.ops[eng])
        waits = {}
        for r in reads:
            d = r.w
            if d is None:
                continue
            if d[0] == "op" and d[1] == eng and eng in ("pe", "sp"):
                continue
            self._need(eng, d, waits)
        for w in writes:
            for d in ([w.w] if w.w is not None else []) + list(w.r.values()):
                if d[0] == "op" and d[1] == eng:
                    continue
                self._need(eng, d, waits)
        for k, v in waits.items():
            self.waited[eng][k] = v
        me = ("op", eng, idx)
        for r in reads:
            r.r[("op", eng)] = me
        for w in writes:
            w.w = me
            w.r = {}
        self.ops[eng].append(dict(fn=fn, waits=waits, dma=None, inc=False, ph=getattr(self, "phase", "")))
        return me

    def dma(self, eng, out, in_, reads=(), writes=(), sem_res=None):
        if sem_res is None:
            sem_res = writes[0] if writes else reads[0]
        if sem_res.sem is None:
            sem_res.sem = len(self.dma_cnt)
            self.dma_cnt.append(0)
        sid = sem_res.sem
        waits = {}
        for r in reads:
            self._need(eng, r.w, waits)
        for w in writes:
            for d in ([w.w] if w.w is not None else []) + list(w.r.values()):
                self._need(eng, d, waits)
        for k, v in waits.items():
            self.waited[eng][k] = v
        self.dma_cnt[sid] += 16
        me = ("dma", sid, self.dma_cnt[sid])
        for r in reads:
            r.r[("dma", sid)] = me
        for w in writes:
            w.w = me
            w.r = {}
        self.ops[eng].append(dict(fn=lambda e, o=out, i=in_: e.dma_start(out=o, in_=i), waits=waits, dma=sid, inc=False))
        return me

    def barrier(self, engines=("pe", "act", "dve", "pool")):
        last = {}
        for e in engines:
            i = len(self.ops[e]) - 1
            while i >= 0 and (self.ops[e][i]["fn"] is None or self.ops[e][i]["dma"] is not None):
                i -= 1
            if i >= 0:
                last[e] = i
        for e in tuple(engines) + ("sp",):
            waits = {}
            for e2, i2 in last.items():
                if e2 != e:
                    self._need(e, ("op", e2, i2), waits)
            for k, v in waits.items():
                self.waited[e][k] = v
            self.ops[e].append(dict(fn=None, waits=waits, dma=None, inc=False))

    def wait_all_dma(self, eng, resources):
        waits = {}
        for r in resources:
            for d in ([r.w] if r.w is not None else []) + list(r.r.values()):
                if d[0] == "dma":
                    self._need(eng, d, waits)
        for k, v in waits.items():
            self.waited[eng][k] = v
        self.ops[eng].append(dict(fn=None, waits=waits, dma=None, inc=False))

    def emit(self):
        nc = self.nc
        for e in ENG:
            for o in self.ops[e]:
                for (kind, src), val in o["waits"].items():
                    if kind == "op":
                        self.ops[src][val]["inc"] = True
        semval = {e: {} for e in ENG}
        nsem = {}
        for e in ENG:
            c = 0
            for i, o in enumerate(self.ops[e]):
                if o["inc"]:
                    semval[e][i] = (c // EPOCH, c % EPOCH + 1)
                    c += 1
            nsem[e] = (c + EPOCH - 1) // EPOCH
        st = self.stack
        esems = {e: [st.enter_context(nc.semaphore(f"s_{e}{k}")) for k in range(nsem[e])] for e in ENG}
        dsems = [st.enter_context(nc.semaphore(f"d{k}")) for k in range(len(self.dma_cnt))]
        self.stats = {e: (len(self.ops[e]), sum(len(o["waits"]) for o in self.ops[e])) for e in ENG}
        block = st.enter_context(nc.Block())

        def runner(e):
            def run(eng):
                for i, o in enumerate(self.ops[e]):
                    for (kind, src), val in o["waits"].items():
                        if kind == "op":
                            ep, v = semval[src][val]
                            eng.wait_ge(esems[src][ep], v)
                        else:
                            eng.wait_ge(dsems[src], val)
                    if o["fn"] is None:
                        continue
                    ins = o["fn"](eng)
                    if o["dma"] is not None:
                        ins.then_inc(dsems[o["dma"]], 16)
                    elif o["inc"]:
                        ep, v = semval[e][i]
                        ins.then_inc(esems[e][ep], 1)
            return run

        block.tensor(runner("pe"))
        block.scalar(runner("act"))
        block.vector(runner("dve"))
        block.gpsimd(runner("pool"))
        block.sync(runner("sp"))
        st.close()


def lambda_init(l):
    return 0.8 - 0.6 * math.exp(-0.3 * l)


def build_program(layers, nseq, first_is_input=True):
    nc = bass.Bass("TRN2", target_bir_lowering=False)
    p = Prog(nc)

    def din(name, shape):
        return nc.dram_tensor(name, list(shape), F32, kind="ExternalInput").ap()

    x_d = din("x", [nseq, S, D])
    out_d = nc.dram_tensor("out", [nseq, S, D], F32, kind="ExternalOutput").ap()
    cos_d = din("cosT", [128, S])
    sin_d = din("sinS", [128, S])
    identf_d = din("identf", [128, 128])
    cmask_d = din("cmask", [128, 128])
    dmask_d = din("dmask", [128, 8, 128])
    gpow_d = din("gpow", [128, 8])
    kend_d = din("kend", [128, 8])
    wd = {}
    for l in layers:
        if l % 2 == 0:
            wd[l] = dict(w_in=din(f"w_in{l}", [8, D, 768]), w_out=din(f"w_out{l}", [8, 256, D]),
                         gw=din(f"gw{l}", [8, 128, 256]), lrup=din(f"lrup{l}", [128, 8, 8]),
                         gn=din(f"gn{l}", [128, 2, D]), ln=din(f"ln{l}", [128, 2, D]))
        else:
            wd[l] = dict(w_in=din(f"w_in{l}", [8, D, 1024]), w_out=din(f"w_out{l}", [8, 256, D]),
                         lam=din(f"lam{l}", [128, 4, 128]), sub=din(f"sub{l}", [128, 256]),
                         ln=din(f"ln{l}", [128, 2, D]))

    h32 = p.sb([128, NT, D], F32, "h32")
    rh = [Res(f"h32_{t}") for t in range(NT)]
    hT = p.sb([128, 8, S], BF16, "hT")
    rhT = [Res(f"hT_{t}") for t in range(NT)]
    slabA = p.sb([128, 8, 512], BF16, "slabA"); rslabA = Res("slabA")
    slabB = p.sb([128, 8, 512], BF16, "slabB"); rslabB = Res("slabB")
    wout = [p.sb([128, 2, D], BF16, f"wout{i}") for i in range(1)]
    rwout = [Res(f"wout{i}") for i in range(1)]
    qk = p.sb([128, 4, S], BF16, "qk")
    rqk = [Res(f"qk{i}") for i in range(4)]
    cosT = p.sb([128, S], F32, "cosT"); sinS = p.sb([128, S], F32, "sinS")
    rconst = Res("const")
    identf = p.sb([128, 128], F32, "identf")
    identb = p.sb([128, 128], BF16, "identb")
    cmask = p.sb([128, 128], BF16, "cmask")
    gpow = p.sb([128, 8], F32, "gpow")
    kend = p.sb([128, 8], F32, "kend")
    epst = p.sb([128, 1], F32, "eps")
    onet = p.sb([128, 1], F32, "one")
    rxs = [p.sb([128, 512], F32, f"rxs{i}") for i in range(2)]; r_rxs = [Res(f"rxs{i}") for i in range(2)]
    rtt = [p.sb([128, 512], F32, f"rtt{i}") for i in range(2)]; r_rtt = [Res(f"rtt{i}") for i in range(2)]
    ARENA = 44800
    arena = p.sb([128, ARENA // 4], F32, "arena")
    st12 = p.sb([128, NT, 12], F32, "st12"); r_st12 = Res("st12")
    mvall = p.sb([128, NT, 2], F32, "mvall"); r_mv = Res("mvall")
    rstd = p.sb([128, NT], F32, "rstd"); r_rstd = Res("rstd")
    nmr = p.sb([128, NT], F32, "nmr"); r_nmr = Res("nmr")
    prm = {}
    for l in layers:
        if l % 2 == 0:
            prm[l] = dict(lrup=p.sb([128, 8, 8], F32, f"lrup{l}"), c=p.sb([128, 8], F32, f"c{l}"),
                          c2=p.sb([128, 8], F32, f"c2{l}"), cn=p.sb([128, 8], F32, f"cn{l}"), r=Res(f"prm{l}"))
        else:
            prm[l] = dict(lamt=rxs[0][:].rearrange("p (a b) -> p a b", a=4), nlam=p.sb([128, 1], F32, f"nlam{l}"),
                          sub2=p.sb([128, 256], F32, f"sub2{l}"), tmp=rtt[0][:, 0:256].rearrange("p (a b) -> p a b", a=2),
                          s12=p.sb([128, 2], F32, f"s12{l}"), r=Res(f"prm{l}"))

    banks = [p.ps([128, 512], F32, f"bank{i}") for i in range(8)]
    rbank = [Res(f"bank{i}", excl=True) for i in range(8)]
    grp = {"proj": [0, 1, 2, 3], "st": [0, 1, 2], "o": [4, 5, 6, 7]}
    gcnt = {k: 0 for k in grp}

    def nxt(g):
        i = grp[g][gcnt[g] % len(grp[g])]
        gcnt[g] += 1
        return banks[i], rbank[i]

    def mm(out, lhsT, rhs, start, stop, reads, writes):
        p.op("pe", lambda e: e.matmul(out, lhsT=lhsT, rhs=rhs, start=start, stop=stop), reads, writes)

    def tr(out, in_, ident, reads, writes):
        p.op("pe", lambda e: e.transpose(out=out, in_=in_, identity=ident), reads, writes)

    def act(out, in_, func, reads, writes, bias=None, scale=1.0, accum=None):
        kw = {}
        if bias is not None:
            kw["bias"] = bias
        if accum is not None:
            kw["accum_out"] = accum
        p.op("act", lambda e: e.activation(out=out, in_=in_, func=func, scale=scale, **kw), reads, writes)

    def tt(eng, out, in0, in1, op, reads, writes):
        p.op(eng, lambda e: e.tensor_tensor(out=out, in0=in0, in1=in1, op=op), reads, writes)

    def ts(eng, out, in0, s1, s2, op0, op1, reads, writes):
        if s2 is None:
            p.op(eng, lambda e: e.tensor_single_scalar(out=out, in_=in0, scalar=s1, op=op0), reads, writes)
        else:
            p.op(eng, lambda e: e.tensor_scalar(out=out, in0=in0, scalar1=s1, scalar2=s2, op0=op0, op1=op1), reads, writes)

    def stt(eng, out, in0, scalar, in1, op0, op1, reads, writes):
        p.op(eng, lambda e: e.scalar_tensor_tensor(out=out, in0=in0, scalar=scalar, in1=in1, op0=op0, op1=op1), reads, writes)

    def cp(eng, out, in_, reads, writes):
        if eng == "act":
            p.op("act", lambda e: e.copy(out=out, in_=in_), reads, writes)
        else:
            p.op(eng, lambda e: e.tensor_copy(out=out, in_=in_), reads, writes)

    p.dma("sp", cosT[:], cos_d, writes=[rconst])
    p.dma("sp", sinS[:], sin_d, writes=[rconst])
    p.dma("sp", identf[:], identf_d, writes=[rconst])
    p.dma("pool", identb[:], identf_d, writes=[rconst])
    p.dma("pool", cmask[:], cmask_d, writes=[rconst])
    p.dma("sp", gpow[:], gpow_d, writes=[rconst])
    p.dma("sp", kend[:], kend_d, writes=[rconst])
    r_eps = Res("eps")
    p.op("dve", lambda e: e.memset(epst[:], EPS), writes=[r_eps])
    p.op("dve", lambda e: e.memset(onet[:], 1.0), writes=[r_eps])
    for l in layers:
        pr = prm[l]
        if STOP <= -2:
            continue
        if l % 2 == 0:
            p.dma("sp", pr["lrup"][:], wd[l]["lrup"], writes=[pr["r"]])
            act(pr["c"][:], pr["lrup"][:, :, 7], AF.Exp, [pr["r"]], [pr["r"]], scale=-1.0)
            act(pr["c"][:], pr["c"][:], AF.Ln, [pr["r"], r_eps], [pr["r"]], bias=onet[:])
            ts("dve", pr["cn"][:], pr["c"][:], 8.0, None, ALU.mult, None, [pr["r"]], [pr["r"]])
            ts("dve", pr["c"][:], pr["cn"][:], -1.0, None, ALU.mult, None, [pr["r"]], [pr["r"]])
            ts("dve", pr["c2"][:], pr["c"][:], 2.0, None, ALU.mult, None, [pr["r"]], [pr["r"]])
        else:
            li = lambda_init(l)
            p.dma("sp", pr["lamt"], wd[l]["lam"], writes=[r_rxs[0]])
            p.dma("sp", pr["sub2"][:], wd[l]["sub"], writes=[pr["r"]])
            tt("dve", pr["tmp"][:, 0, :], pr["lamt"][:, 0, :], pr["lamt"][:, 1, :], ALU.mult, [r_rxs[0]], [r_rtt[0]])
            tt("dve", pr["tmp"][:, 1, :], pr["lamt"][:, 2, :], pr["lamt"][:, 3, :], ALU.mult, [r_rxs[0]], [r_rtt[0]])
            p.op("dve", lambda e, pr=pr: e.reduce_sum(out=pr["s12"][:], in_=pr["tmp"], axis=AX.X), [r_rtt[0]], [pr["r"]])
            act(pr["s12"][:], pr["s12"][:], AF.Exp, [pr["r"]], [pr["r"]])
            tt("dve", pr["nlam"][:], pr["s12"][:, 1:2], pr["s12"][:, 0:1], ALU.subtract, [pr["r"]], [pr["r"]])
            ts("dve", pr["nlam"][:], pr["nlam"][:], -li, None, ALU.add, None, [pr["r"]], [pr["r"]])
            ts("dve", pr["sub2"][:], pr["sub2"][:], 1.0 - li, None, ALU.mult, None, [pr["r"]], [pr["r"]])

    sched = []
    for s_ in range(nseq):
        for l_ in layers:
            for u_ in range(8):
                sched.append((l_, u_))
    pos = {"A": 0, "B": 0}

    def load_next(which):
        i = pos[which]
        pos[which] += 1
        if i >= len(sched):
            return
        l, u = sched[i]
        src = wd[l]["w_in"][u].rearrange("(kc p) c -> p kc c", p=128)
        if which == "A":
            dst, rdst, c0, w = slabA, rslabA, 0, 512
        else:
            dst, rdst, c0, w = slabB, rslabB, 512, (512 if l % 2 == 1 else 256)
        for kc0 in range(0, 8, 4):
            p.dma("pool", dst[:, kc0:kc0 + 4, 0:w], src[:, kc0:kc0 + 4, c0:c0 + w], writes=[rdst])

    def load_wout(l, u):
        src = wd[l]["w_out"][u].rearrange("(c p) n -> p c n", p=128)
        p.dma("pool", wout[0][:], src, writes=[rwout[0]])
        return wout[0], rwout[0]

    rope_ctr = [0]
    rope_bufs = {"xs": list(zip(rxs, r_rxs)), "tt": list(zip(rtt, r_rtt))}

    def rope(bank, rb, dst, rdst, G):
        i = rope_ctr[0] % len(rope_bufs["xs"])
        rope_ctr[0] += 1
        xs, r1 = rope_bufs["xs"][i]
        t1, r2 = rope_bufs["tt"][i]
        cs = cosT[:, G * 512:(G + 1) * 512]
        sn = sinS[:, G * 512:(G + 1) * 512]
        p.op("act", lambda e: e.copy(out=xs[0:64, :], in_=bank[64:128, :]), [rb], [r1])
        p.op("act", lambda e: e.copy(out=xs[64:128, :], in_=bank[0:64, :]), [rb], [r1])
        tt("dve", t1[:], bank[:], cs, ALU.mult, [rb, rconst], [r2])
        tt("pool", xs[:], xs[:], sn, ALU.mult, [r1, rconst], [r1])
        tt("dve", dst, t1[:], xs[:], ALU.add, [r1, r2], [rdst])

    def proj_feat(sl, rsl, c0, G):
        bank, rb = nxt("proj")
        for kc in range(8):
            mm(bank[:], sl[:, kc, c0:c0 + 128], hT[:, kc, G * 512:(G + 1) * 512], kc == 0, kc == 7,
               [rsl] + rhT[4 * G:4 * G + 4], [rb])
        return bank, rb

    def proj_tok(sl, rsl, c0, width, t):
        bank, rb = nxt("proj")
        for kc in range(8):
            mm(bank[:, 0:width], hT[:, kc, t * 128:(t + 1) * 128], sl[:, kc, c0:c0 + width], kc == 0, kc == 7,
               [rsl, rhT[t]], [rb])
        return bank, rb

    def out_proj(mixT_views, rmix, wo, rwo, scale=None, rscale=None):
        k = 0
        for t in range(NT):
            for cg in range(2):
                bank, rb = nxt("proj")
                for c in range(2):
                    mm(bank[:], mixT_views[c][:, t * 128:(t + 1) * 128], wo[:, c, cg * 512:(cg + 1) * 512],
                       c == 0, c == 1, [rwo] + rmix, [rb])
                hv = h32[:, t, cg * 512:(cg + 1) * 512]
                extra = [rscale] if rscale is not None else []
                if k % 2 == 0:
                    if scale is None:
                        tt("dve", hv, hv, bank[:], ALU.add, [rb, rh[t]], [rh[t]])
                    else:
                        stt("dve", hv, bank[:], scale[:, t:t + 1], hv, ALU.mult, ALU.add, [rb, rh[t]] + extra, [rh[t]])
                else:
                    i = (k // 2) % 2
                    if scale is None:
                        cp("act", rxs[i][:], bank[:], [rb], [r_rxs[i]])
                    else:
                        act(rxs[i][:], bank[:], AF.Copy, [rb] + extra, [r_rxs[i]], scale=scale[:, t:t + 1])
                    tt("pool", hv, hv, rxs[i][:], ALU.add, [r_rxs[i], rh[t]], [rh[t]])
                k += 1

    def prescale():
        for t in range(NT):
            ts("dve", h32[:, t, :], h32[:, t, :], ALPHA, None, ALU.mult, None, [rh[t]], [rh[t]])

    def make_hT(t):
        for half in range(2):
            bank, rb = nxt("proj")
            for k in range(4):
                kc = half * 4 + k
                tr(bank[:, k * 128:(k + 1) * 128], h32[:, t, kc * 128:(kc + 1) * 128], identf[:], [rh[t], rconst], [rb])
            dst = hT[:, half * 4:half * 4 + 4, t * 128:(t + 1) * 128]
            src = bank[:].rearrange("p (k n) -> p k n", k=4)
            cp("act", dst, src, [rb], [rhT[t]])

    def layer_norm(l, s, last):
        p.phase = "ln"
        lnt = qk[:, 0:2, :].bitcast(F32)
        p.dma("sp", lnt, wd[l]["ln"], writes=[rqk[0], rqk[1]])
        rl = [rqk[0], rqk[1]]
        for t in range(NT):
            for hf in range(2):
                p.op("dve", lambda e, t=t, hf=hf: e.bn_stats(out=st12[:, t, hf * 6:(hf + 1) * 6], in_=h32[:, t, hf * 512:(hf + 1) * 512]),
                     [rh[t]], [r_st12])
            p.op("dve", lambda e, t=t: e.bn_aggr(out=mvall[:, t, :], in_=st12[:, t, :]), [r_st12], [r_mv])
        act(rstd[:], mvall[:, :, 1], AF.Sqrt, [r_mv, r_eps], [r_rstd], bias=epst[:])
        p.op("dve", lambda e: e.reciprocal(out=rstd[:], in_=rstd[:]), [r_rstd], [r_rstd])
        stt("dve", nmr[:], mvall[:, :, 0], -1.0, rstd[:], ALU.mult, ALU.mult, [r_mv, r_rstd], [r_nmr])
        for t in range(NT):
            hv = h32[:, t, :]
            ts("dve", hv, hv, rstd[:, t:t + 1], nmr[:, t:t + 1], ALU.mult, ALU.add, [rh[t], r_rstd, r_nmr], [rh[t]])
            tt("pool", hv, hv, lnt[:, 0, :], ALU.mult, [rh[t]] + rl, [rh[t]])
            tt("dve", hv, hv, lnt[:, 1, :], ALU.add, [rh[t]] + rl, [rh[t]])
            if last:
                p.dma("sp", out_d[s].rearrange("(t p) d -> p t d", p=128)[:, t, :], hv, reads=[rh[t]])
            else:
                make_hT(t)

    def odd_layer(l):
        grp.update({"proj": [0, 1, 2, 3], "st": [0, 1, 2], "o": [4, 5, 6, 7]})
        pr = prm[l]
        off = [0]

        def carve(nbytes, dt, shape):
            a = arena[:, off[0] // 4:(off[0] + nbytes) // 4]
            off[0] += nbytes
            if dt == BF16:
                a = a.bitcast(BF16)
            if len(shape) == 3:
                a = a.rearrange("p (a b) -> p a b", a=shape[1])
            return a

        vaug = carve(NT * 264 * 2, BF16, [128, NT, 264]); r_v = Res("vaug")
        sg = carve(NT * 256 * 2, BF16, [128, NT, 256]); r_sg = Res("sg")
        dd = carve(4 * 256 * 4, F32, [128, 4, 256]); r_dd = [Res(f"dd{i}") for i in range(4)]
        pTs = [carve(512 * 2, BF16, [128, 512]) for _ in range(3)]; r_pT = [Res(f"pT{i}") for i in range(3)]
        mixb = [carve(256 * 2, BF16, [128, 256]) for _ in range(2)]; r_mixb = [Res(f"mixb{i}") for i in range(2)]
        mtmp = [carve(256 * 4, F32, [128, 256]) for _ in range(2)]; r_mtmp = [Res(f"mtmp{i}") for i in range(2)]
        mixT = carve(2 * S * 2, BF16, [128, 2, S]); r_mixT = Res("mixT")
        sq = carve(256 * 4, F32, [128, 256]); r_sq = Res("sqj")
        ss = carve(16 * 4, F32, [128, 16]); r_ss = Res("ss")
        rec = carve(16 * 4, F32, [128, 16]); r_rec = Res("rec")
        rope_bufs["xs"] = list(zip(rxs, r_rxs)) + [(carve(2048, F32, [128, 512]), Res(f"rxs_x{i}")) for i in range(2)]
        rope_bufs["tt"] = list(zip(rtt, r_rtt)) + [(carve(2048, F32, [128, 512]), Res(f"rtt_x{i}")) for i in range(2)]
        assert off[0] <= ARENA, off[0]
        p.op("pool", lambda e: e.memset(vaug[:, :, 256:257], 1.0), [], [r_v])
        pT_ctr = [0]
        mix_ctr = [0]
        for h in range(8):
            p.phase = "odd_qk"
            for c in range(4):
                for G in range(4):
                    bank, rb = proj_feat(slabA, rslabA, c * 128, G)
                    rope(bank, rb, qk[:, c, G * 512:(G + 1) * 512], rqk[c], G)
            load_next("A")
            p.phase = "odd_vg"
            for t in range(NT):
                bank, rb = proj_tok(slabB, rslabB, 0, 512, t)
                cp("act", vaug[:, t, 0:256], bank[:, 0:256], [rb], [r_v])
                act(sg[:, t, :], bank[:, 256:512], AF.Silu, [rb], [r_sg])
            load_next("B")
            wo, rwo = load_wout(l, h)
            for G in range(4):
                p.phase = "odd_attn"
                nk = 4 * G + 4
                for m in range(2):
                    qc, kc_ = m, 2 + m
                    obs = [nxt("o") for _ in range(4)]
                    stb = {}

                    def emit_s(kt):
                        q0 = max(4 * G, kt)
                        o_ = (q0 - 4 * G) * 128
                        bank, rb = nxt("st")
                        diag = kt >= 4 * G
                        mm(bank[:, o_:512], qk[:, kc_, kt * 128:(kt + 1) * 128], qk[:, qc, G * 512 + o_:(G + 1) * 512],
                           True, not diag, [rqk[kc_], rqk[qc]], [rb])
                        if diag:
                            mm(bank[:, o_:o_ + 128], identb[:], cmask[:], False, True, [rconst], [rb])
                        stb[kt] = (bank, rb, q0, o_)

                    emit_s(0)
                    emit_s(1)
                    for kt in range(nk):
                        if kt + 2 < nk:
                            emit_s(kt + 2)
                        bank, rb, q0, o_ = stb.pop(kt)
                        i = pT_ctr[0] % 3
                        pT_ctr[0] += 1
                        pT, rp = pTs[i], r_pT[i]
                        act(pT[:, o_:512], bank[:, o_:512], AF.Exp, [rb], [rp], scale=SCALE)
                        for qt in range(q0, 4 * G + 4):
                            qi = qt - 4 * G
                            ob, rob = obs[qi]
                            mm(ob[:, 0:257], pT[:, qi * 128:(qi + 1) * 128], vaug[:, kt, 0:257], kt == 0, kt == qt,
                               [rp, r_v], [rob])
                    for qi in range(4):
                        ob, rob = obs[qi]
                        t = 4 * G + qi
                        col = m * 8 + qi
                        p.op("dve", lambda e, ob=ob, col=col: e.reciprocal(out=rec[:, col:col + 1], in_=ob[:, 256:257]), [rob], [r_rec])
                        if m == 0:
                            ts("dve", dd[:, qi, :], ob[:, 0:256], rec[:, col:col + 1], None, ALU.mult, None, [rob, r_rec], [r_dd[qi]])
                        else:
                            tt("dve", rec[:, col + 4:col + 5], rec[:, col:col + 1], pr["nlam"][:], ALU.mult, [r_rec, pr["r"]], [r_rec])
                            stt("dve", dd[:, qi, :], ob[:, 0:256], rec[:, col + 4:col + 5], dd[:, qi, :], ALU.mult, ALU.add,
                                [rob, r_rec, r_dd[qi]], [r_dd[qi]])
                p.phase = "odd_fin"
                for qi in range(4):
                    t = 4 * G + qi
                    i = mix_ctr[0] % 2
                    mix_ctr[0] += 1
                    act(sq[:], dd[:, qi, :], AF.Square, [r_dd[qi]], [r_sq, r_ss], accum=ss[:, t:t + 1])
                    tt("pool", mtmp[i][:], dd[:, qi, :], pr["sub2"][:], ALU.mult, [r_dd[qi], pr["r"]], [r_mtmp[i]])
                    tt("dve", mixb[i][:], mtmp[i][:], sg[:, t, :], ALU.mult, [r_mtmp[i], r_sg], [r_mixb[i]])
                    bank, rb = nxt("proj")
                    bv = bank[:, 0:128].bitcast(BF16)
                    for c in range(2):
                        tr(bv[:, c * 128:(c + 1) * 128], mixb[i][:, c * 128:(c + 1) * 128], identb[:], [r_mixb[i], rconst], [rb])
                    cp("act", mixT[:, :, t * 128:(t + 1) * 128], bv.rearrange("p (c n) -> p c n", c=2), [rb], [r_mixT])
            p.phase = "odd_out"
            act(ss[:], ss[:], AF.Sqrt, [r_ss, r_eps], [r_ss], bias=epst[:], scale=1.0 / 256.0)
            p.op("dve", lambda e: e.reciprocal(out=ss[:], in_=ss[:]), [r_ss], [r_ss])
            out_proj([mixT[:, 0, :], mixT[:, 1, :]], [r_mixT], wo, rwo, scale=ss, rscale=r_ss)

    def even_layer(l):
        rope_bufs["xs"] = list(zip(rxs, r_rxs))
        rope_bufs["tt"] = list(zip(rtt, r_rtt))
        grp.update({"proj": [0, 1, 2, 3], "st": [4, 5], "o": [6, 7]})
        pr = prm[l]
        off = [0]

        def carve(nbytes, dt, shape):
            a = arena[:, off[0] // 4:(off[0] + nbytes) // 4]
            off[0] += nbytes
            if dt == BF16:
                a = a.bitcast(BF16)
            if len(shape) == 3:
                a = a.rearrange("p (a b) -> p a b", a=shape[1])
            return a

        kE = carve(NT * 128 * 2, BF16, [128, NT, 128]); r_kE = Res("kE")
        vv = carve(NT * 128 * 2, BF16, [128, NT, 128]); r_vv = Res("vv")
        sgr = carve(NT * 128 * 2, BF16, [128, NT, 128]); r_sgr = [Res(f"sgr{t}") for t in range(NT)]
        Rbs = [carve(128 * 2, BF16, [128, 128]) for _ in range(3)]; r_Rbs = [Res(f"Rb{i}") for i in range(3)]
        R32 = carve(128 * 4, F32, [128, 128]); r_R32 = Res("R32")
        ro = qk[:, 2:4, :].bitcast(F32).rearrange("p a (b c) -> p (a b) c", c=128)
        r_ro = [Res(f"ro{t}") for t in range(NT)]
        dmk = carve(128 * 4, F32, [128, 128]); r_dmk = Res("dmk")
        scs = [carve(128 * 2, BF16, [128, 128]) for _ in range(3)]; r_sc = [Res(f"sc{i}") for i in range(3)]
        mixT = carve(2 * S * 2, BF16, [128, 2, S]); r_mixR = Res("mixTr"); r_mixL = Res("mixTl")
        gnt = carve(2 * 128 * 4, F32, [128, 2, 128]); r_gnt = Res("gnt")
        gwt = carve(256 * 2, BF16, [128, 256]); r_gwt = Res("gwt")
        junk = carve(128 * 2, BF16, [128, 128]); r_junk = Res("junk")
        mv2 = carve(NT * 2 * 4, F32, [128, NT, 2]); r_mv2 = Res("mv2")
        rs2 = carve(NT * 4, F32, [128, NT]); r_rs2 = Res("rs2")
        nm2 = carve(NT * 4, F32, [128, NT]); r_nm2 = Res("nm2")
        xb32 = carve(516 * 4, F32, [128, 516]); r_xb = Res("xb32")
        names = ["xc", "rr", "ig", "aa", "a2", "hh0", "hh1"]
        lt = {n: carve(512 * 4, F32, [128, 512]) for n in names}
        rlt = {n: Res(n) for n in names}
        xcb = carve(512 * 2, BF16, [128, 512]); r_xcb = Res("xcb")
        sgl = carve(512 * 4, F32, [128, 512]); r_sgl = Res("sgl")
        assert off[0] <= ARENA, off[0]
        sc_ctr = [0]
        for u in range(8):
            if STOP <= 0:
                continue
            gC = float((1.0 - 2.0 ** (-5.0 - u)) ** 128)
            p.dma("sp", gnt[:], wd[l]["gn"][:, :, u * 128:(u + 1) * 128], writes=[r_gnt])
            p.dma("pool", gwt[:], wd[l]["gw"][u], writes=[r_gwt])
            p.dma("sp", dmk[:], dmask_d[:, u, :], writes=[r_dmk])
            p.phase = "ev_qk"
            for c in range(2):
                for G in range(4):
                    bank, rb = proj_feat(slabA, rslabA, c * 128, G)
                    rope(bank, rb, qk[:, c, G * 512:(G + 1) * 512], rqk[c], G)
            if STOP <= 1:
                continue
            p.phase = "ev_vg"
            for t in range(NT):
                bank, rb = proj_tok(slabA, rslabA, 256, 256, t)
                cp("act", vv[:, t, :], bank[:, 0:128], [rb], [r_vv])
                act(sgr[:, t, :], bank[:, 128:256], AF.Silu, [rb], [r_sgr[t]])
            load_next("A")
            if STOP <= 2:
                continue
            p.phase = "ev_kE"
            for t4 in range(4):
                bank, rb = nxt("proj")
                bv = bank[:, 0:256].bitcast(BF16)
                for k in range(4):
                    t = t4 * 4 + k
                    tr(bv[:, k * 128:(k + 1) * 128], qk[:, 1, t * 128:(t + 1) * 128], identb[:], [rqk[1], rconst], [rb])
                ts("dve", kE[:, t4 * 4:t4 * 4 + 4, :], bv.rearrange("p (k n) -> p k n", k=4), kend[:, u:u + 1], None, ALU.mult, None,
                   [rb, rconst], [r_kE])
            if STOP <= 3:
                continue
            L = pr["lrup"]
            p.op("dve", lambda e: e.memset(xb32[:, 0:3], 0.0), [], [r_xb])

            def lru_stage(G, k):
                hh, r_hh = lt[f"hh{G % 2}"], rlt[f"hh{G % 2}"]
                hp, r_hp = lt[f"hh{(G + 1) % 2}"], rlt[f"hh{(G + 1) % 2}"]
                xc = lt["xc"]
                if k == 0:
                    if G > 0:
                        cp("dve", xb32[:, 0:3], xb32[:, 512:515], [r_xb], [r_xb])
                    bank, rb = proj_feat(slabB, rslabB, 0, G)
                    cp("act", xb32[:, 3:515], bank[:], [rb], [r_xb])
                    bank, rb = proj_feat(slabB, rslabB, 128, G)
                    act(sgl[:], bank[:], AF.Silu, [rb], [r_sgl])
                    ts("dve", xc[:], xb32[:, 3:515], L[:, u, 3:4], L[:, u, 4:5], ALU.mult, ALU.add, [r_xb, pr["r"]], [rlt["xc"]])
                    for w in (2, 1, 0):
                        stt("dve", xc[:], xb32[:, w:w + 512], L[:, u, w:w + 1], xc[:], ALU.mult, ALU.add, [r_xb, pr["r"], rlt["xc"]], [rlt["xc"]])
                    cp("pool", xcb[:], xc[:], [rlt["xc"]], [r_xcb])
                elif k == 1:
                    bank, rb = nxt("proj")
                    mm(bank[:], gwt[:, 0:128], xcb[:], True, True, [r_gwt, r_xcb], [rb])
                    act(lt["rr"][:], bank[:], AF.Sigmoid, [rb, pr["r"]], [rlt["rr"]], bias=L[:, u, 5:6])
                    bank, rb = nxt("proj")
                    mm(bank[:], gwt[:, 128:256], xcb[:], True, True, [r_gwt, r_xcb], [rb])
                    act(lt["ig"][:], bank[:], AF.Sigmoid, [rb, pr["r"]], [rlt["ig"]], bias=L[:, u, 6:7])
                elif k == 2:
                    act(lt["aa"][:], lt["rr"][:], AF.Exp, [rlt["rr"], pr["r"]], [rlt["aa"]], scale=pr["c"][:, u:u + 1])
                    act(lt["a2"][:], lt["rr"][:], AF.Exp, [rlt["rr"], pr["r"]], [rlt["a2"]], scale=pr["c2"][:, u:u + 1])
                    act(lt["rr"][:], lt["rr"][:], AF.Tanh, [rlt["rr"], pr["r"]], [rlt["rr"]], scale=pr["cn"][:, u:u + 1])
                    stt("dve", lt["a2"][:], lt["a2"][:], 1.0, lt["rr"][:], ALU.add, ALU.mult, [rlt["a2"], rlt["rr"]], [rlt["a2"]])
                    tt("pool", lt["ig"][:], lt["ig"][:], xc[:], ALU.mult, [rlt["ig"], rlt["xc"]], [rlt["ig"]])
                else:
                    act(lt["a2"][:], lt["a2"][:], AF.Sqrt, [rlt["a2"]], [rlt["a2"]])
                    tt("dve", lt["ig"][:], lt["ig"][:], lt["a2"][:], ALU.mult, [rlt["ig"], rlt["a2"]], [rlt["ig"]])
                    init = 0.0 if G == 0 else hp[:, 511:512]
                    p.op("dve", lambda e, hh=hh, init=init: e.tensor_tensor_scan(out=hh[:], data0=lt["aa"][:], data1=lt["ig"][:], initial=init,
                                                                                  op0=ALU.mult, op1=ALU.add),
                         [rlt["aa"], rlt["ig"], r_hp], [r_hh])
                    tt("pool", mixT[:, 1, G * 512:(G + 1) * 512], hh[:], sgl[:], ALU.mult, [r_hh, r_sgl], [r_mixL])

            p.phase = "ev_ret"
            stbs = {}

            def emit_a(t):
                bank, rb = nxt("st")
                mm(bank[:, 0:128], qk[:, 1, t * 128:(t + 1) * 128], qk[:, 0, t * 128:(t + 1) * 128], True, True, [rqk[0], rqk[1]], [rb])
                mm(bank[:, 128:256], kE[:, t, :], vv[:, t, :], True, True, [r_kE, r_vv], [rb])
                stbs[t] = (bank, rb)

            emit_a(0)
            for t in range(NT):
                if t + 1 < NT:
                    emit_a(t + 1)
                bank, rb = stbs.pop(t)
                i = sc_ctr[0] % 3
                sc_ctr[0] += 1
                sc, rsc = scs[i], r_sc[i]
                tt("dve", sc[:], bank[:, 0:128], dmk[:], ALU.mult, [rb, r_dmk], [rsc])
                if t + 1 < NT:
                    if t == 0:
                        cp("act", R32[:], bank[:, 128:256], [rb], [r_R32])
                    else:
                        stt("dve", R32[:], R32[:], gC, bank[:, 128:256], ALU.mult, ALU.add, [rb, r_R32], [r_R32])
                    cp("act", Rbs[(t + 1) % 3][:], R32[:], [r_R32], [r_Rbs[(t + 1) % 3]])
                ob, rob = nxt("o")
                mm(ob[:, 0:128], sc[:], vv[:, t, :], True, True, [rsc, r_vv], [rob])
                if t > 0:
                    mm(ob[:, 128:256], qk[:, 0, t * 128:(t + 1) * 128], Rbs[t % 3][:], True, True, [rqk[0], r_Rbs[t % 3]], [rob])
                cp("act", ro[:, t, :], ob[:, 0:128], [rob], [r_ro[t]])
                if t > 0:
                    stt("dve", ro[:, t, :], ob[:, 128:256], gpow[:, u:u + 1], ro[:, t, :], ALU.mult, ALU.add,
                        [rob, r_ro[t], rconst], [r_ro[t]])
                lru_stage(t // 4, t % 4)
            if STOP <= 4:
                continue
            p.phase = "ev_gn"
            p.op("pool", lambda e: e.memset(mv2[:], 0.0), [], [r_mv2])
            for t in range(NT):
                act(junk[:], ro[:, t, :], AF.Copy, [r_ro[t]], [r_junk, r_mv2], accum=mv2[:, t, 0:1])
                act(junk[:], ro[:, t, :], AF.Square, [r_ro[t]], [r_junk, r_mv2], accum=mv2[:, t, 1:2])
            ts("dve", mv2[:], mv2[:], 1.0 / 128.0, None, ALU.mult, None, [r_mv2], [r_mv2])
            tt("dve", nm2[:], mv2[:, :, 0], mv2[:, :, 0], ALU.mult, [r_mv2], [r_nm2])
            tt("dve", rs2[:], mv2[:, :, 1], nm2[:], ALU.subtract, [r_mv2, r_nm2], [r_rs2])
            act(rs2[:], rs2[:], AF.Sqrt, [r_rs2, r_eps], [r_rs2], bias=epst[:])
            p.op("dve", lambda e: e.reciprocal(out=rs2[:], in_=rs2[:]), [r_rs2], [r_rs2])
            stt("dve", nm2[:], mv2[:, :, 0], -1.0, rs2[:], ALU.mult, ALU.mult, [r_mv2, r_rs2], [r_nm2])
            for t in range(NT):
                rv = ro[:, t, :]
                ts("dve", rv, rv, rs2[:, t:t + 1], nm2[:, t:t + 1], ALU.mult, ALU.add, [r_ro[t], r_rs2, r_nm2], [r_ro[t]])
                tt("pool", rv, rv, gnt[:, 0, :], ALU.mult, [r_ro[t], r_gnt], [r_ro[t]])
                tt("pool", rv, rv, gnt[:, 1, :], ALU.add, [r_ro[t], r_gnt], [r_ro[t]])
                tt("dve", sgr[:, t, :], rv, sgr[:, t, :], ALU.mult, [r_ro[t], r_sgr[t]], [r_sgr[t]])
            for t4 in range(4):
                bank, rb = nxt("proj")
                bv = bank[:, 0:256].bitcast(BF16)
                for k in range(4):
                    t = t4 * 4 + k
                    tr(bv[:, k * 128:(k + 1) * 128], sgr[:, t, :], identb[:], [r_sgr[t], rconst], [rb])
                cp("act", mixT[:, 0, t4 * 512:(t4 + 1) * 512], bv, [rb], [r_mixR])
            load_next("B")
            if STOP <= 6:
                continue
            p.phase = "ev_out"
            wo, rwo = load_wout(l, u)
            out_proj([mixT[:, 0, :], mixT[:, 1, :]], [r_mixR, r_mixL], wo, rwo)

    load_next("A")
    load_next("B")
    for s in range(nseq):
        xv = x_d[s].rearrange("(t p) d -> p t d", p=128)
        for t in range(NT):
            p.dma("sp", h32[:, t, :], xv[:, t, :], writes=[rh[t]])
        for t in range(NT):
            make_hT(t)
        if not layers:
            for t in range(NT):
                p.dma("sp", out_d[s].rearrange("(t p) d -> p t d", p=128)[:, t, :], h32[:, t, :], reads=[rh[t]])
        for li_, l in enumerate(layers):
            p.barrier()
            prescale()
            if l % 2 == 0:
                even_layer(l)
            else:
                odd_layer(l)
            if STOP <= -1:
                for t in range(NT):
                    p.dma("sp", out_d[s].rearrange("(t p) d -> p t d", p=128)[:, t, :], h32[:, t, :], reads=[rh[t]])
                continue
            layer_norm(l, s, last=(li_ == len(layers) - 1))
    p.wait_all_dma("sp", rh)
    p.emit()
    p.stats["pe_phases"] = [o.get("ph", "") for o in p.ops["pe"] if o["fn"] is not None]
    return nc, p.stats


def host_constants():
    half = 64
    inv_freq = (10000.0 ** (-np.arange(half, dtype=np.float32) / half)).astype(np.float32)
    pos = np.arange(S, dtype=np.float32)
    ang = pos[None, :] * inv_freq[:, None]
    cos = np.cos(ang).astype(np.float32)
    sin = np.sin(ang).astype(np.float32)
    cosT = np.concatenate([cos, cos], axis=0)
    sinS = np.concatenate([-sin, sin], axis=0)
    identf = np.eye(128, dtype=np.float32)
    j = np.arange(128)[:, None]
    i = np.arange(128)[None, :]
    cmask = np.where(i >= j, 0.0, -30000.0).astype(np.float32)
    g = (1.0 - 2.0 ** (-5.0 - np.arange(8, dtype=np.float64)))
    dmask = np.zeros((128, 8, 128), np.float64)
    for h in range(8):
        dmask[:, h, :] = np.where(i >= j, g[h] ** np.maximum(i - j, 0), 0.0) * SCALE
    gpow = g[None, :] ** (np.arange(128)[:, None] + 1.0)
    kend = (g[None, :] ** (127.0 - np.arange(128)[:, None])) * SCALE
    return dict(cosT=np.ascontiguousarray(cosT), sinS=np.ascontiguousarray(sinS), identf=identf, cmask=cmask,
                dmask=dmask.astype(np.float32), gpow=gpow.astype(np.float32), kend=kend.astype(np.float32))


def rep(a):
    return np.ascontiguousarray(np.broadcast_to(a, (128,) + a.shape))


def host_weights(inp, layers):
    m = {}
    for l in layers:
        j = l // 2
        if l % 2 == 0:
            w = inp["ev_w_in"][j]
            cols = []
            for u in range(8):
                cols.append(np.concatenate([w[:, k * 1024 + u * 128:k * 1024 + (u + 1) * 128] for k in range(6)], axis=1))
            m[f"w_in{l}"] = np.ascontiguousarray(np.stack(cols, 0))
            wo = inp["ev_w_out"][j]
            m[f"w_out{l}"] = np.ascontiguousarray(np.stack(
                [np.concatenate([wo[u * 128:(u + 1) * 128], wo[1024 + u * 128:1024 + (u + 1) * 128]], 0) for u in range(8)], 0))
            m[f"gw{l}"] = np.ascontiguousarray(np.concatenate([inp["ev_gate_a_w"][j], inp["ev_gate_x_w"][j]], axis=2))
            lr = np.stack([inp["ev_conv_w"][j][0], inp["ev_conv_w"][j][1], inp["ev_conv_w"][j][2], inp["ev_conv_w"][j][3],
                           inp["ev_conv_b"][j], inp["ev_gate_a_b"][j], inp["ev_gate_x_b"][j], inp["ev_lru_lambda"][j]], axis=1)
            m[f"lrup{l}"] = np.ascontiguousarray(lr.reshape(8, 128, 8).transpose(1, 0, 2))
            m[f"gn{l}"] = rep(np.stack([inp["ev_ret_gn_g"][j], inp["ev_ret_gn_b"][j]], 0))
            m[f"ln{l}"] = rep(np.stack([inp["ev_ln_g"][j], inp["ev_ln_b"][j]], 0))
        else:
            w = inp["od_w_in"][j]
            cols = []
            for h in range(8):
                q1 = w[:, (2 * h) * 128:(2 * h + 1) * 128]
                q2 = w[:, (2 * h + 1) * 128:(2 * h + 2) * 128]
                k1 = w[:, 2048 + (2 * h) * 128:2048 + (2 * h + 1) * 128]
                k2 = w[:, 2048 + (2 * h + 1) * 128:2048 + (2 * h + 2) * 128]
                v = w[:, 4096 + h * 256:4096 + (h + 1) * 256]
                gg = w[:, 6144 + h * 256:6144 + (h + 1) * 256]
                cols.append(np.concatenate([q1, q2, k1, k2, v, gg], axis=1))
            m[f"w_in{l}"] = np.ascontiguousarray(np.stack(cols, 0))
            m[f"w_out{l}"] = np.ascontiguousarray(inp["od_w_out"][j].reshape(8, 256, 1024))
            m[f"lam{l}"] = rep(np.stack([inp["od_lambda_q1"][j], inp["od_lambda_k1"][j], inp["od_lambda_q2"][j], inp["od_lambda_k2"][j]], 0))
            m[f"sub{l}"] = rep(inp["od_subln_g"][j])
            m[f"ln{l}"] = rep(np.stack([inp["od_ln_g"][j], inp["od_ln_b"][j]], 0))
    return m


_PROGS = {}


def run_layers(x, inp, layers, ncores=NCORES, trace=False):
    B = x.shape[0]
    nseq = B // ncores
    key = (tuple(layers), nseq)
    if key not in _PROGS:
        _PROGS[key] = build_program(layers, nseq)
    nc, stats = _PROGS[key]
    consts = host_constants()
    wts = host_weights(inp, layers)
    in_maps = []
    for c in range(ncores):
        mdict = dict(consts)
        mdict.update(wts)
        mdict["x"] = np.ascontiguousarray(x[c * nseq:(c + 1) * nseq])
        in_maps.append(mdict)
    res = run_bass_kernel_spmd(nc, in_maps, core_ids=list(range(ncores)))
    return np.concatenate([r["out"] for r in res.results], axis=0)


FUSED = True
import os
STOP = int(os.environ.get("KSTOP", "99"))


def kernel(**inputs):
    inp = {k: np.asarray(v) for k, v in inputs.items()}
    x = np.ascontiguousarray(inp["x"], dtype=np.float32)
    if FUSED:
        return run_layers(x, inp, [0, 1, 2, 3]).astype(np.float32)
    h = x
    for l in range(DEPTH):
        h = run_layers(h, inp, [l])
    return h.astype(np.float32)
```
